# Optimizing a Trainium2 kernel written in Bass

```python
import jax, jax.numpy as jnp
from jax import lax
import numpy as np

D_MODEL = 1024
BATCH = 8
SEQ = 2048
DEPTH = 4
DEC_BATCH = 32
DEC_SEQ = 4
PAST_LEN = 8192
PAGE_SIZE = 128

D_S5 = D_MODEL // 2
S5_GROUP = 16
S5_GROUPS = D_S5 // S5_GROUP
S5_STATE = 64
D_NSA = D_MODEL // 2
HEAD_DIM = 64
N_HEADS = D_NSA // HEAD_DIM
N_KV = 2
GROUP_SIZE = N_HEADS // N_KV
BLOCK = 64
N_SELECT = 16
WINDOW = 512
QUERY_BLOCK = 64
ROT_DIM = HEAD_DIM // 4
ROT_HALF = ROT_DIM // 2
ROPE_THETA = 500000.0
RMS_EPS = 1e-6
NEG_INF = -1e30
FORCED_SCORE = 1e4
KV_W = N_KV * HEAD_DIM
IN_SIZES = (D_S5, D_S5, D_NSA, 6 * KV_W, 3 * N_HEADS, D_NSA, 2 * D_MODEL)
N_IN = 2 * D_S5 + 2 * D_NSA + 6 * KV_W + 3 * N_HEADS + 2 * D_MODEL

kernel_name = 'hybrid_s5_nsa_adaln_step'


def rmsnorm(x, g):
    xf = x.astype(jnp.float32)
    y = xf * lax.rsqrt(jnp.mean(xf * xf, axis=-1, keepdims=True) + RMS_EPS) * g
    return y.astype(x.dtype)


def rope(x, pos):
    inv = ROPE_THETA ** (-jnp.arange(ROT_HALF, dtype=jnp.float32) / ROT_HALF)
    ang = pos.astype(jnp.float32)[:, None] * inv[None, :]
    cos = jnp.cos(ang)[:, None, :]
    sin = jnp.sin(ang)[:, None, :]
    x1 = x[..., :ROT_HALF].astype(jnp.float32)
    x2 = x[..., ROT_HALF:ROT_DIM].astype(jnp.float32)
    rot = jnp.concatenate([x1 * cos - x2 * sin, x1 * sin + x2 * cos], axis=-1).astype(x.dtype)
    return jnp.concatenate([rot, x[..., ROT_DIM:]], axis=-1)


def masked_softmax(s, mask):
    s = jnp.where(mask, s.astype(jnp.float32), NEG_INF)
    return jnp.where(mask, jax.nn.softmax(s, axis=-1), 0.0)


def s5_scan(u, h0, a_re, a_im, log_dt, b_re, b_im, c_re, c_im, d):
    bsz, t, _ = u.shape
    uf = u.astype(jnp.float32).reshape(bsz, t, S5_GROUPS, S5_GROUP)
    dt = jnp.exp(log_dt.astype(jnp.float32))[:, None]
    a_re = a_re.astype(jnp.float32)
    a_im = a_im.astype(jnp.float32)
    mag = jnp.exp(a_re * dt)
    ab_re = mag * jnp.cos(a_im * dt)
    ab_im = mag * jnp.sin(a_im * dt)
    den = a_re * a_re + a_im * a_im
    nr = ab_re - 1.0
    e_re = (nr * a_re + ab_im * a_im) / den
    e_im = (ab_im * a_re - nr * a_im) / den
    b_re = b_re.astype(jnp.float32)
    b_im = b_im.astype(jnp.float32)
    bb_re = e_re[..., None] * b_re - e_im[..., None] * b_im
    bb_im = e_re[..., None] * b_im + e_im[..., None] * b_re
    bu_re = jnp.einsum('gpc,btgc->btgp', bb_re, uf)
    bu_im = jnp.einsum('gpc,btgc->btgp', bb_im, uf)
    at_re = jnp.broadcast_to(ab_re, bu_re.shape)
    at_im = jnp.broadcast_to(ab_im, bu_im.shape)

    def combine(l, r):
        lar, lai, lbr, lbi = l
        rar, rai, rbr, rbi = r
        return (lar * rar - lai * rai, lar * rai + lai * rar,
                rar * lbr - rai * lbi + rbr, rar * lbi + rai * lbr + rbi)

    acr, aci, hr, hi = lax.associative_scan(combine, (at_re, at_im, bu_re, bu_im), axis=1)
    if h0 is not None:
        h0f = h0.astype(jnp.float32)
        h0r = h0f[:, None, 0]
        h0i = h0f[:, None, 1]
        hr, hi = hr + acr * h0r - aci * h0i, hi + acr * h0i + aci * h0r
    y = (jnp.einsum('gcp,btgp->btgc', c_re.astype(jnp.float32), hr)
         - jnp.einsum('gcp,btgp->btgc', c_im.astype(jnp.float32), hi)
         + d.astype(jnp.float32) * uf)
    h_last = jnp.stack([hr[:, -1], hi[:, -1]], axis=1)
    return y.reshape(bsz, t, D_S5), h_last


def compress_rows(rows, pe, w1, w2):
    bsz, tk = rows.shape[:2]
    nb = tk // BLOCK
    blk = rows.reshape(bsz, nb, BLOCK, N_KV, HEAD_DIM) + pe[None, None, :, None, :]
    blk = blk.transpose(0, 1, 3, 2, 4).reshape(bsz, nb, N_KV, BLOCK * HEAD_DIM)
    return jax.nn.silu(blk @ w1) @ w2


def nsa_attention(q, qpos, kv_all, kw, vw, kwpos, gate_logits, cmp_pe, cmp_w1, cmp_w2, window_chunked):
    bsz, tq = q.shape[:2]
    nb = kv_all.shape[1] // BLOCK
    qg = q.reshape(bsz, tq, N_KV, GROUP_SIZE, HEAD_DIM) * (HEAD_DIM ** -0.5)

    kc = compress_rows(kv_all[:, :, 0], cmp_pe[0], cmp_w1[0], cmp_w2[0])
    vc = compress_rows(kv_all[:, :, 1], cmp_pe[1], cmp_w1[1], cmp_w2[1])
    cpos = jnp.arange(nb) * BLOCK + (BLOCK - 1)
    kc = rope(kc, cpos)
    s_c = jnp.einsum('btgrd,bngd->btgrn', qg, kc)
    mask_c = (cpos[None, :] <= qpos[:, None])[None, :, None, None, :]
    p_c = masked_softmax(s_c, mask_c)
    o_c = jnp.einsum('btgrn,bngd->btgrd', p_c.astype(vc.dtype), vc)

    imp = p_c.sum(axis=3)
    blk = jnp.arange(nb)[None, :]
    qblk = (qpos // BLOCK)[:, None]
    forced = (blk == 0) | (blk == qblk) | (blk == qblk - 1)
    imp = jnp.where(forced[None, :, None, :], FORCED_SCORE, imp)
    imp = jnp.where((blk > qblk)[None, :, None, :], -1.0, imp)
    n_sel = min(N_SELECT, nb)
    top_val, top_idx = lax.top_k(imp, n_sel)
    top_ok = top_val > -0.5
    kb = kv_all[:, :, 2].reshape(bsz, nb, BLOCK, N_KV, HEAD_DIM).transpose(0, 3, 1, 2, 4)
    vb = kv_all[:, :, 3].reshape(bsz, nb, BLOCK, N_KV, HEAD_DIM).transpose(0, 3, 1, 2, 4)

    qb = QUERY_BLOCK if tq % QUERY_BLOCK == 0 else tq
    n_chunks = tq // qb
    wlen = WINDOW + qb
    bi = jnp.arange(bsz)[:, None, None, None]
    gi = jnp.arange(N_KV)[None, None, :, None]

    def chunk(i):
        s0 = i * qb
        qc = lax.dynamic_slice_in_dim(qg, s0, qb, 1)
        pc = lax.dynamic_slice_in_dim(qpos, s0, qb, 0)
        ic = lax.dynamic_slice_in_dim(top_idx, s0, qb, 1)
        okc = lax.dynamic_slice_in_dim(top_ok, s0, qb, 1)
        ks = kb[bi, gi, ic]
        vs = vb[bi, gi, ic]
        s_s = jnp.einsum('bqgrd,bqgkpd->bqgrkp', qc, ks)
        kpos = ic[..., None] * BLOCK + jnp.arange(BLOCK)
        mask_s = okc[..., None] & (kpos <= pc[None, :, None, None, None])
        p_s = masked_softmax(s_s.reshape(bsz, qb, N_KV, GROUP_SIZE, n_sel * BLOCK),
                             mask_s.reshape(bsz, qb, N_KV, 1, n_sel * BLOCK))
        p_s = p_s.reshape(s_s.shape).astype(vs.dtype)
        o_s = jnp.einsum('bqgrkp,bqgkpd->bqgrd', p_s, vs)
        if window_chunked:
            kwc = lax.dynamic_slice_in_dim(kw, s0, wlen, 1)
            vwc = lax.dynamic_slice_in_dim(vw, s0, wlen, 1)
            kwpc = lax.dynamic_slice_in_dim(kwpos, s0, wlen, 0)
        else:
            kwc, vwc, kwpc = kw, vw, kwpos
        s_w = jnp.einsum('bqgrd,bkgd->bqgrk', qc, kwc)
        mask_w = ((kwpc[None, :] <= pc[:, None]) & (kwpc[None, :] > pc[:, None] - WINDOW)
                  & (kwpc[None, :] >= 0))[None, :, None, None, :]
        p_w = masked_softmax(s_w, mask_w).astype(vwc.dtype)
        o_w = jnp.einsum('bqgrk,bkgd->bqgrd', p_w, vwc)
        return o_s, o_w

    o_s, o_w = lax.map(chunk, jnp.arange(n_chunks))
    o_s = o_s.transpose(1, 0, 2, 3, 4, 5).reshape(bsz, tq, N_KV, GROUP_SIZE, HEAD_DIM)
    o_w = o_w.transpose(1, 0, 2, 3, 4, 5).reshape(bsz, tq, N_KV, GROUP_SIZE, HEAD_DIM)
    g = jax.nn.sigmoid(gate_logits.astype(jnp.float32)).reshape(bsz, tq, 3, N_KV, GROUP_SIZE, 1)
    o = g[:, :, 0] * o_c + g[:, :, 1] * o_s + g[:, :, 2] * o_w
    return o.reshape(bsz, tq, D_NSA).astype(q.dtype)


def layer(x, c, qpos, kv_past, win_past, ssm_h0, lp):
    bsz, t = x.shape[:2]
    mod = jax.nn.silu(c) @ lp['ada_w'] + lp['ada_b']
    shift, scale, gate = jnp.split(mod, 3, axis=-1)
    h = rmsnorm(x, lp['norm_g']) * (1.0 + scale[:, None]) + shift[:, None]
    proj = h @ lp['w_in']
    splits = [int(s) for s in np.cumsum(IN_SIZES)[:-1]]
    u, z_s5, q, kvx, gate_logits, z_nsa, merge_logits = jnp.split(proj, splits, axis=-1)

    y5, h5 = s5_scan(u, ssm_h0, lp['s5_a_re'], lp['s5_a_im'], lp['s5_log_dt'], lp['s5_b_re'],
                     lp['s5_b_im'], lp['s5_c_re'], lp['s5_c_im'], lp['s5_d'])
    y5 = jax.nn.gelu(y5)
    y5 = y5 * jax.nn.sigmoid(y5 @ lp['s5_glu_w'] + lp['s5_glu_b'])
    y5 = (y5 * jax.nn.silu(z_s5.astype(jnp.float32))).astype(x.dtype)
    b_s5 = y5 @ lp['w_s5_out']

    q = rope(q.reshape(bsz, t, N_HEADS, HEAD_DIM), qpos)
    kvx = kvx.reshape(bsz, t, 6, N_KV, HEAD_DIM)
    k_sel = rope(kvx[:, :, 2], qpos)
    k_win = rope(kvx[:, :, 4], qpos)
    kv_new = jnp.stack([kvx[:, :, 0], kvx[:, :, 1], k_sel, kvx[:, :, 3]], axis=2)
    win_rows = jnp.stack([k_win, kvx[:, :, 5]], axis=2)
    if kv_past is None:
        kv_all = kv_new
        pad = ((0, 0), (WINDOW, 0), (0, 0), (0, 0))
        kw = jnp.pad(win_rows[:, :, 0], pad)
        vw = jnp.pad(win_rows[:, :, 1], pad)
        kwpos = jnp.arange(t + WINDOW) - WINDOW
        keep = min(WINDOW, t)
        win_state = win_rows[:, t - keep:]
        chunked = True
    else:
        kv_all = jnp.concatenate([kv_past.astype(kv_new.dtype), kv_new], axis=1)
        extra = (-kv_all.shape[1]) % BLOCK
        kv_all = jnp.pad(kv_all, ((0, 0), (0, extra), (0, 0), (0, 0), (0, 0)))
        wbuf = win_past.shape[1]
        win_all = jnp.concatenate([win_past.astype(win_rows.dtype), win_rows], axis=1)
        kw = win_all[:, :, 0]
        vw = win_all[:, :, 1]
        kwpos = qpos[0] - wbuf + jnp.arange(wbuf + t)
        win_state = win_all[:, t:]
        chunked = False
    o = nsa_attention(q, qpos, kv_all, kw, vw, kwpos, gate_logits,
                      lp['cmp_pe'], lp['cmp_w1'], lp['cmp_w2'], chunked)
    b_nsa = (o * jax.nn.silu(z_nsa)) @ lp['w_nsa_out']

    m_s5, m_nsa = jnp.split(jax.nn.sigmoid(merge_logits), 2, axis=-1)
    out = (m_s5 * b_s5 + m_nsa * b_nsa) @ lp['w_o']
    x = x + gate[:, None] * out
    return x, kv_new, win_state, h5


def setup_inputs(seed: int = 0) -> dict:
    key = jax.random.key(seed)
    ks = jax.random.split(key, 32)
    f32 = jnp.float32

    def nrm(k, shape, s):
        return jax.random.normal(k, shape, f32) * s

    n_pages = PAST_LEN // PAGE_SIZE
    n_used = DEC_BATCH * n_pages
    n_pool = (n_used * 5) // 4
    win_buf = min(WINDOW, PAST_LEN)
    x_prompt = nrm(ks[0], (BATCH, SEQ, D_MODEL), 1.0)
    x_sample = nrm(ks[1], (DEC_BATCH, DEC_SEQ, D_MODEL), 1.0)
    c_prompt = nrm(ks[2], (BATCH, D_MODEL), 1.0)
    c_sample = nrm(ks[3], (DEC_BATCH, D_MODEL), 1.0)
    cache_kv = nrm(ks[4], (n_pool, DEPTH, PAGE_SIZE, 4, N_KV, HEAD_DIM), 1.0)
    page_table = jax.random.permutation(ks[5], n_pool)[:n_used].reshape(DEC_BATCH, n_pages).astype(jnp.int32)
    state_win = nrm(ks[6], (DEC_BATCH, DEPTH, win_buf, 2, N_KV, HEAD_DIM), 1.0)
    state_ssm = nrm(ks[7], (DEC_BATCH, DEPTH, 2, S5_GROUPS, S5_STATE), 0.5)
    ada_w = nrm(ks[8], (DEPTH, D_MODEL, 3 * D_MODEL), 0.5 * D_MODEL ** -0.5)
    ada_b = nrm(ks[9], (DEPTH, 3 * D_MODEL), 0.01)
    norm_g = 1.0 + nrm(ks[10], (DEPTH, D_MODEL), 0.02)
    w_in = nrm(ks[11], (DEPTH, D_MODEL, N_IN), D_MODEL ** -0.5)
    s5_a_re = -0.5 + nrm(ks[12], (DEPTH, S5_GROUPS, S5_STATE), 0.01)
    s5_a_im = jnp.pi * jnp.arange(S5_STATE, dtype=f32) + nrm(ks[13], (DEPTH, S5_GROUPS, S5_STATE), 0.01)
    s5_log_dt = jax.random.uniform(ks[14], (DEPTH, S5_GROUPS), f32, float(np.log(1e-3)), float(np.log(1e-1)))
    s5_b_re = nrm(ks[15], (DEPTH, S5_GROUPS, S5_STATE, S5_GROUP), (2 * S5_GROUP) ** -0.5)
    s5_b_im = nrm(ks[16], (DEPTH, S5_GROUPS, S5_STATE, S5_GROUP), (2 * S5_GROUP) ** -0.5)
    s5_c_re = nrm(ks[17], (DEPTH, S5_GROUPS, S5_GROUP, S5_STATE), (2 * S5_STATE) ** -0.5)
    s5_c_im = nrm(ks[18], (DEPTH, S5_GROUPS, S5_GROUP, S5_STATE), (2 * S5_STATE) ** -0.5)
    s5_d = nrm(ks[19], (DEPTH, S5_GROUPS, S5_GROUP), 1.0)
    s5_glu_w = nrm(ks[20], (DEPTH, D_S5, D_S5), D_S5 ** -0.5)
    s5_glu_b = nrm(ks[21], (DEPTH, D_S5), 0.01)
    cmp_pe = nrm(ks[22], (DEPTH, 2, BLOCK, HEAD_DIM), 0.02)
    cmp_w1 = nrm(ks[23], (DEPTH, 2, BLOCK * HEAD_DIM, HEAD_DIM), (BLOCK * HEAD_DIM) ** -0.5)
    cmp_w2 = nrm(ks[24], (DEPTH, 2, HEAD_DIM, HEAD_DIM), HEAD_DIM ** -0.5)
    w_s5_out = nrm(ks[25], (DEPTH, D_S5, D_MODEL), D_S5 ** -0.5)
    w_nsa_out = nrm(ks[26], (DEPTH, D_NSA, D_MODEL), D_NSA ** -0.5)
    w_o = nrm(ks[27], (DEPTH, D_MODEL, D_MODEL), D_MODEL ** -0.5)
    final_g = 1.0 + nrm(ks[28], (D_MODEL,), 0.02)
    return {'x_prompt': x_prompt, 'x_sample': x_sample, 'c_prompt': c_prompt, 'c_sample': c_sample,
            'cache_kv': cache_kv, 'page_table': page_table, 'state_win': state_win, 'state_ssm': state_ssm,
            'ada_w': ada_w, 'ada_b': ada_b, 'norm_g': norm_g, 'w_in': w_in,
            's5_a_re': s5_a_re, 's5_a_im': s5_a_im, 's5_log_dt': s5_log_dt,
            's5_b_re': s5_b_re, 's5_b_im': s5_b_im, 's5_c_re': s5_c_re, 's5_c_im': s5_c_im, 's5_d': s5_d,
            's5_glu_w': s5_glu_w, 's5_glu_b': s5_glu_b, 'cmp_pe': cmp_pe, 'cmp_w1': cmp_w1, 'cmp_w2': cmp_w2,
            'w_s5_out': w_s5_out, 'w_nsa_out': w_nsa_out, 'w_o': w_o, 'final_g': final_g}


def reference(x_prompt, x_sample, c_prompt, c_sample, cache_kv, page_table, state_win, state_ssm,
              ada_w, ada_b, norm_g, w_in, s5_a_re, s5_a_im, s5_log_dt, s5_b_re, s5_b_im,
              s5_c_re, s5_c_im, s5_d, s5_glu_w, s5_glu_b, cmp_pe, cmp_w1, cmp_w2,
              w_s5_out, w_nsa_out, w_o, final_g):
    n_pages = page_table.shape[1]
    page = cache_kv.shape[2]
    past_len = n_pages * page
    qpos_p = jnp.arange(x_prompt.shape[1])
    qpos_s = past_len + jnp.arange(x_sample.shape[1])
    xp, xs = x_prompt, x_sample
    kv_p, kv_s, win_p, win_s, ssm_p, ssm_s = [], [], [], [], [], []
    for l in range(DEPTH):
        lp = {'ada_w': ada_w[l], 'ada_b': ada_b[l], 'norm_g': norm_g[l], 'w_in': w_in[l],
              's5_a_re': s5_a_re[l], 's5_a_im': s5_a_im[l], 's5_log_dt': s5_log_dt[l],
              's5_b_re': s5_b_re[l], 's5_b_im': s5_b_im[l], 's5_c_re': s5_c_re[l], 's5_c_im': s5_c_im[l],
              's5_d': s5_d[l], 's5_glu_w': s5_glu_w[l], 's5_glu_b': s5_glu_b[l],
              'cmp_pe': cmp_pe[l], 'cmp_w1': cmp_w1[l], 'cmp_w2': cmp_w2[l],
              'w_s5_out': w_s5_out[l], 'w_nsa_out': w_nsa_out[l], 'w_o': w_o[l]}
        xp, kvn, wn, hn = layer(xp, c_prompt, qpos_p, None, None, None, lp)
        kv_p.append(kvn)
        win_p.append(wn)
        ssm_p.append(hn)
        kv_past = cache_kv[page_table, l]
        kv_past = kv_past.reshape(kv_past.shape[0], past_len, 4, N_KV, HEAD_DIM)
        xs, kvn, wn, hn = layer(xs, c_sample, qpos_s, kv_past, state_win[:, l], state_ssm[:, l], lp)
        kv_s.append(kvn)
        win_s.append(wn)
        ssm_s.append(hn)
    y_prompt = rmsnorm(xp, final_g)
    y_sample = rmsnorm(xs, final_g)
    kv_prompt = jnp.stack(kv_p, axis=1)
    kv_sample = jnp.stack(kv_s, axis=1)
    win_prompt = jnp.stack(win_p, axis=1)
    win_sample = jnp.stack(win_s, axis=1)
    ssm_prompt = jnp.stack(ssm_p, axis=1)
    ssm_sample = jnp.stack(ssm_s, axis=1)
    return (y_prompt, y_sample, kv_prompt, kv_sample, win_prompt, win_sample, ssm_prompt, ssm_sample)
```

```python
import numpy as np
import concourse.bass as bass
import concourse.mybir as mybir
from concourse.bass_utils import run_bass_kernel_spmd

F32 = mybir.dt.float32
BF16 = mybir.dt.bfloat16
I32 = mybir.dt.int32
AF = mybir.ActivationFunctionType
ALU = mybir.AluOpType
AX = mybir.AxisListType

NT = 2064
TP = 2048
DEPTH = 4
LCH = 16
NCH = TP // LCH
NEG = -30000.0
TWO_PI = float(2 * np.pi)


class Tok:
    __slots__ = ("name", "w", "r", "excl")

    def __init__(self, name="", excl=False):
        self.name = name
        self.w = None
        self.r = {}
        self.excl = excl


class Prog:
    ENG = ("pe", "act", "dve", "pool", "sp")

    def __init__(self, nc):
        self.nc = nc
        self.ops = {e: [] for e in self.ENG}
        self.cnt = {e: 0 for e in self.ENG}
        self.seen = {e: {} for e in self.ENG}
        self.sems = {}
        self.dcnt = {}
        self._stack = []
        self.nops = 0

    def sem(self, name):
        if name not in self.sems:
            cm = self.nc.semaphore(name)
            h = cm.__enter__()
            self._stack.append(cm)
            self.sems[name] = h
        return self.sems[name]

    def _wait(self, eng, ev):
        sname, val = ev
        if self.seen[eng].get(sname, 0) >= val:
            return
        self.seen[eng][sname] = val
        h = self.sem(sname)
        self.ops[eng].append(lambda e, h=h, val=val: e.wait_ge(h, val))

    def _deps(self, eng, reads, writes, pe_accum=False):
        deps = {}

        def add(ev):
            if ev is None:
                return
            s, v = ev
            if deps.get(s, 0) < v:
                deps[s] = v
        for t in reads:
            add(t.w)
            if t.excl:
                for s_, v_ in t.r.items():
                    if s_ != "e_" + eng:
                        add((s_, v_))
        for t in writes:
            if not (pe_accum and t.w is not None and t.w[0] == "e_pe"):
                add(t.w)
            for s, v in t.r.items():
                if pe_accum and s == "e_pe":
                    continue
                add((s, v))
        for s, v in deps.items():
            self._wait(eng, (s, v))

    def _mark(self, ev, reads, writes):
        s, v = ev
        for t in reads:
            if t.r.get(s, 0) < v:
                t.r[s] = v
        for t in writes:
            t.w = ev
            t.r = {}

    def op(self, eng, fn, reads=(), writes=(), pe_accum=False):
        self._deps(eng, reads, writes, pe_accum)
        sname = "e_" + eng
        h = self.sem(sname)
        self.cnt[eng] += 1
        ev = (sname, self.cnt[eng])
        self.ops[eng].append(lambda e, fn=fn, h=h: fn(e).then_inc(h, 1))
        self._mark(ev, reads, writes)
        self.nops += 1
        return ev

    NDSEM = 40

    def dma(self, q, key, fn, reads=(), writes=()):
        self._deps(q, reads, writes)
        i = getattr(self, "_dnext", 0)
        self._dnext = (i + 1) % self.NDSEM
        sname = f"d{i}"
        prev = self.dcnt.get(sname, 0)
        if prev:
            self._wait(q, (sname, prev))
        h = self.sem(sname)
        self.dcnt[sname] = prev + 16
        ev = (sname, self.dcnt[sname])
        self.ops[q].append(lambda e, fn=fn, h=h: fn(e).then_inc(h, 16))
        self._mark(ev, reads, writes)
        self.nops += 1
        return ev

    def wait_all(self, eng):
        for e in self.ENG:
            if self.cnt[e]:
                self._wait(eng, ("e_" + e, self.cnt[e]))
        for s, v in self.dcnt.items():
            self._wait(eng, (s, v))

    def barrier(self):
        for e in self.ENG:
            self.wait_all(e)

    def emit(self):
        nc = self.nc
        with nc.Block() as block:
            @block.tensor
            def _(e):
                for f in self.ops["pe"]:
                    f(e)

            @block.scalar
            def _(e):
                for f in self.ops["act"]:
                    f(e)

            @block.vector
            def _(e):
                for f in self.ops["dve"]:
                    f(e)

            @block.gpsimd
            def _(e):
                for f in self.ops["pool"]:
                    f(e)

            @block.sync
            def _(e):
                for f in self.ops["sp"]:
                    f(e)

    def close(self):
        while self._stack:
            self._stack.pop().__exit__(None, None, None)


class Mem:
    def __init__(self, nc):
        self.nc = nc
        self.stack = []
        self.n = 0

    def sb(self, name, shape, dt):
        self.n += 1
        cm = self.nc.sbuf_tensor(f"s{self.n}_{name}", list(shape), dt)
        t = cm.__enter__()
        self.stack.append(cm)
        return t

    def ps(self, name, shape, dt=F32):
        self.n += 1
        cm = self.nc.psum_tensor(f"p{self.n}_{name}", list(shape), dt)
        t = cm.__enter__()
        self.stack.append(cm)
        return t

    def mark(self):
        return len(self.stack)

    def release(self, mark):
        while len(self.stack) > mark:
            self.stack.pop().__exit__(None, None, None)


def _s5_layouts(inp):
    L = DEPTH
    a_re, a_im, ldt = inp["s5_a_re"], inp["s5_a_im"], inp["s5_log_dt"]
    b = np.stack([inp["s5_b_re"], inp["s5_b_im"]], 1)
    c = np.stack([inp["s5_c_re"], inp["s5_c_im"]], 1)
    aNL = np.zeros((L, 128, 3, 16), np.float32)
    aPR = np.zeros((L, 128, 4, 3, 128), np.float32)
    BPR = np.zeros((L, 128, 4, 2, 128), np.float32)
    BNL = np.zeros((L, 128, 4, 2, 128), np.float32)
    CNL = np.zeros((L, 128, 4, 2, 4, 32), np.float32)
    dv = np.zeros((L, 128, 4), np.float32)
    for q in range(4):
        for pb in range(4):
            for gl in range(2):
                g = 8 * q + 2 * pb + gl
                sl = slice(gl * 64, gl * 64 + 64)
                aNL[:, sl, 0, q * 4 + pb] = a_re[:, g]
                aNL[:, sl, 1, q * 4 + pb] = a_im[:, g]
                aNL[:, sl, 2, q * 4 + pb] = ldt[:, g][:, None]
                rows = slice(pb * 32, pb * 32 + 32)
                aPR[:, rows, q, 0, sl] = a_re[:, g][:, None, :]
                aPR[:, rows, q, 1, sl] = a_im[:, g][:, None, :]
                aPR[:, rows, q, 2, sl] = ldt[:, g][:, None, None]
                r16 = slice(pb * 32 + gl * 16, pb * 32 + gl * 16 + 16)
                BPR[:, r16, q, :, sl] = np.transpose(b[:, :, g], (0, 3, 1, 2))
                BNL[:, sl, q, :, r16] = np.transpose(b[:, :, g], (0, 2, 1, 3))
                CNL[:, sl, q, :, pb, gl * 16:gl * 16 + 16] = np.transpose(c[:, :, g], (0, 3, 1, 2))
        dv[:, :, q] = inp["s5_d"][:, 8 * q:8 * q + 8].reshape(L, 128)
    return aNL, aPR, BPR, BNL, CNL, dv


def _h0_layout(state_ssm4):
    L = DEPTH
    out = np.zeros((L, 128, 2, 16, 4), np.float32)
    for q in range(4):
        for pb in range(4):
            for gl in range(2):
                g = 8 * q + 2 * pb + gl
                out[:, gl * 64:gl * 64 + 64, :, q * 4 + pb, :] = np.transpose(state_ssm4[:, :, :, g, :], (1, 3, 2, 0))
    return out


def _rope_tables():
    inv = (500000.0 ** (-np.arange(8, dtype=np.float32) / 8)).astype(np.float32)
    pos = np.concatenate([np.arange(2048), 8192 + (np.arange(128) % 4)]).astype(np.float32)
    ang = pos[:, None] * inv[None, :]
    cos = np.cos(ang).astype(np.float32).reshape(17, 128, 8).transpose(1, 0, 2)
    sin = np.sin(ang).astype(np.float32).reshape(17, 128, 8).transpose(1, 0, 2)
    cpos = (np.arange(128) * 64 + 63).astype(np.float32)
    angc = cpos[None, :] * inv[:, None]
    cosC = np.ones((128, 128), np.float32)
    sinC = np.zeros((128, 128), np.float32)
    for g in range(2):
        cosC[g * 64:g * 64 + 8] = np.cos(angc)
        cosC[g * 64 + 8:g * 64 + 16] = np.cos(angc)
        sinC[g * 64:g * 64 + 8] = -np.sin(angc)
        sinC[g * 64 + 8:g * 64 + 16] = np.sin(angc)
    return np.ascontiguousarray(cos), np.ascontiguousarray(sin), cosC, sinC


def _attn_consts():
    t = np.arange(2048)
    n = np.arange(32)
    valid = (64 * n[None, :] + 63) <= t[:, None]
    maskC = np.where(valid, 0.0, NEG).astype(np.float32)
    mask01 = valid.astype(np.float32)
    qblk = t // 64
    f0 = (n[None, :] == 0)
    f1 = (n[None, :] == qblk[:, None])
    f2 = (n[None, :] == qblk[:, None] - 1)
    fut = n[None, :] > qblk[:, None]
    forced = f0 | f1 | f2
    selA = (~(forced | fut)).astype(np.float32)
    selB = np.maximum(np.maximum(f0 * 3e4, f1 * 2e4), f2 * 1e4).astype(np.float32) - fut.astype(np.float32)
    tm = lambda a: np.ascontiguousarray(a.reshape(16, 128, 32).transpose(1, 0, 2))
    cm = np.stack([tm(maskC), tm(mask01), tm(selA), tm(selB)], 1)
    E = np.zeros((128, 16, 128), np.float32)
    for kt in range(16):
        for key in range(128):
            nn = 2 * kt + key // 64
            E[nn, kt, key] = 1.0
            E[64 + nn, kt, key] = 1.0
    kl = np.arange(128)[:, None]
    tl = np.arange(128)[None, :]
    triC = np.where(kl > tl, NEG, 0.0).astype(np.float32)
    triW = np.where(kl <= tl, NEG, 0.0).astype(np.float32)
    return cm, E, np.stack([triC, triW], 1)


def _sample_consts():
    sel = np.zeros((16, 2, 129), np.float32)
    sel[:, 0, :] = 1.0
    for c, v in ((0, 3e4), (127, 1e4), (128, 2e4)):
        sel[:, 0, c] = 0.0
        sel[:, 1, c] = v
    r = np.arange(16)
    Smat = (r[:, None] // 4 == r[None, :] // 4).astype(np.float32)
    masknew = np.full((16, 4, 16), NEG, np.float32)
    delta = np.zeros((16, 4, 4), np.float32)
    for b in range(4):
        for rr in range(16):
            j = rr // 4
            for jp in range(j + 1):
                masknew[rr, b, 4 * b + jp] = 0.0
        for j in range(4):
            delta[4 * b + j, b, j] = 1.0
    i = np.arange(512)
    maskwin = np.where(i[None, :] <= (r[:, None] // 4), NEG, 0.0).astype(np.float32)
    return sel, Smat, masknew, delta, maskwin


def _cmp_layouts(inp):
    L = DEPTH
    w1 = inp["cmp_w1"].reshape(L, 2, 64, 64, 64)
    W1bd = np.zeros((L, 2, 128, 64, 128), np.float32)
    W2bd = np.zeros((L, 3, 128, 128), np.float32)
    peT = np.zeros((L, 2, 128, 64), np.float32)
    perm = np.arange(64)
    perm[0:8] = np.arange(8, 16)
    perm[8:16] = np.arange(0, 8)
    for g in range(2):
        sl = slice(g * 64, g * 64 + 64)
        W1bd[:, :, sl, :, sl] = np.transpose(w1, (0, 1, 3, 2, 4))
        W2bd[:, 0, sl, sl] = inp["cmp_w2"][:, 0]
        W2bd[:, 1, sl, sl] = inp["cmp_w2"][:, 1]
        w2p = inp["cmp_w2"][:, 0][:, :, perm].copy()
        w2p[:, :, 16:] = 0.0
        W2bd[:, 2, sl, sl] = w2p
        peT[:, :, sl, :] = np.transpose(inp["cmp_pe"], (0, 1, 3, 2))
    return W1bd, W2bd, peT


class _Stop(Exception):
    pass


class Buf:
    def __init__(self, t, name, ntok=1):
        self.t = t
        self.tok = Tok(name)
        self.toks = [Tok(f"{name}{i}") for i in range(ntok)] if ntok > 1 else [self.tok]


def build(cfg=None):
    cfg = cfg or {}
    nlayers = cfg.get("nlayers", DEPTH)
    taps = cfg.get("taps", {})
    stop_after = cfg.get("stop_after", None)
    nc = bass.Bass("TRN2", target_bir_lowering=False)
    P = Prog(nc)
    mem = Mem(nc)

    def din(name, shape, dt=F32):
        return nc.dram_tensor(name, list(shape), dt, kind="ExternalInput").ap()

    def dout(name, shape, dt=F32):
        return nc.dram_tensor(name, list(shape), dt, kind="ExternalOutput").ap()

    D = {}
    D["xT0"] = din("xT0", [128, 8, NT])
    D["cT"] = din("cT", [128, 8, 5])
    D["ident"] = din("ident", [128, 128])
    D["ada_w"] = din("ada_w", [DEPTH, 1024, 3072])
    D["ada_bT"] = din("ada_bT", [DEPTH, 128, 24])
    D["norm_gT"] = din("norm_gT", [DEPTH, 128, 8])
    D["final_gT"] = din("final_gT", [128, 8])
    D["w_in"] = din("w_in", [DEPTH, 1024, 4888])
    D["aNL"] = din("aNL", [DEPTH, 128, 3, 16])
    D["aPR"] = din("aPR", [DEPTH, 128, 4, 3, 128])
    D["BPR"] = din("BPR", [DEPTH, 128, 4, 2, 128])
    D["BNL"] = din("BNL", [DEPTH, 128, 4, 2, 128])
    D["CNL"] = din("CNL", [DEPTH, 128, 4, 2, 4, 32])
    D["dv"] = din("dv", [DEPTH, 128, 4])
    D["h0NL"] = din("h0NL", [DEPTH, 128, 2, 16, 4])
    D["glu_w"] = din("glu_w", [DEPTH, 512, 512])
    D["glu_bT"] = din("glu_bT", [DEPTH, 128, 4])
    D["w_s5_out"] = din("w_s5_out", [DEPTH, 512, 1024])
    D["w_nsa_out"] = din("w_nsa_out", [DEPTH, 512, 1024])
    D["w_o"] = din("w_o", [DEPTH, 1024, 1024])
    D["cosT"] = din("cosT", [128, 17, 8])
    D["sinT"] = din("sinT", [128, 17, 8])
    D["swin"] = din("swin", [4, DEPTH, 512, 256])
    D["cm"] = din("cm", [128, 4, 16, 32])
    D["Ecst"] = din("Ecst", [128, 16, 128])
    D["tri"] = din("tri", [128, 2, 128])
    D["ropeC"] = din("ropeC", [128, 2, 128])
    D["W1bd"] = din("W1bd", [DEPTH, 2, 128, 64, 128])
    D["W2bd"] = din("W2bd", [DEPTH, 3, 128, 128])
    D["peT"] = din("peT", [DEPTH, 2, 128, 64])
    D["cache2"] = din("cache2", [cfg.get("npool", 2560) * DEPTH * 128 * 2, 256])
    D["ptab"] = din("ptab", [4, 64], I32)
    D["iot2"] = din("iot2", [128, 1])
    D["s_sel"] = din("s_sel", [16, 2, 129])
    D["s_smat"] = din("s_smat", [16, 16])
    D["s_mnew"] = din("s_mnew", [16, 4, 16])
    D["s_delta"] = din("s_delta", [16, 4, 4])
    D["s_mwin"] = din("s_mwin", [16, 512])
    O = {}
    O["kvP"] = dout("kvP", [DEPTH, TP, 512])
    O["kvS"] = dout("kvS", [DEPTH, 16, 512])
    O["winP"] = dout("winP", [DEPTH, 512, 256])
    O["winS"] = dout("winS", [4, DEPTH, 512, 256])
    O["yT"] = dout("yT", [128, 8, NT])
    O["hlP"] = dout("hlP", [DEPTH, 128, 2, 16])
    O["hlS"] = dout("hlS", [DEPTH, 128, 2, 16, 4])
    TAP = {n: dout("tap_" + n, shp[0], BF16 if shp[1] == "bf16" else F32) for n, shp in taps.items()}
    out_tok = Tok("out")

    def chk(name):
        if stop_after == name:
            raise _Stop()

    def tt(eng, out, in0, in1, op, R, W):
        return P.op(eng, lambda e: e.tensor_tensor(out=out, in0=in0, in1=in1, op=op), R, W)

    def ts(eng, out, in0, s1, s2, op0, op1, R, W):
        if op1 is None:
            return P.op(eng, lambda e: e.tensor_scalar(out=out, in0=in0, scalar1=s1, scalar2=None, op0=op0), R, W)
        return P.op(eng, lambda e: e.tensor_scalar(out=out, in0=in0, scalar1=s1, scalar2=s2, op0=op0, op1=op1), R, W)

    def stt(out, in0, sc, in1, op0, op1, R, W):
        return P.op("dve", lambda e: e.scalar_tensor_tensor(out=out, in0=in0, scalar=sc, in1=in1, op0=op0, op1=op1), R, W)

    def cp(eng, out, in_, R, W):
        return P.op(eng, lambda e: e.tensor_copy(out=out, in_=in_), R, W)

    def act(out, in_, func, R, W, bias=None, scale=None):
        kw = {}
        if bias is not None:
            kw["bias"] = bias
        if scale is not None:
            kw["scale"] = scale
        return P.op("act", lambda e: e.activation(out=out, in_=in_, func=func, **kw), R, W)

    pe_mode = [None]

    def _rnd(v):
        return 32 if v <= 32 else (64 if v <= 64 else 128)

    def pe_drain_if(mode):
        if pe_mode[0] is not None and pe_mode[0] != mode and P.cnt["pe"]:
            P._wait("pe", ("e_pe", P.cnt["pe"]))
        pe_mode[0] = mode

    def mm(out, lhsT, rhs, start, stop, R, W):
        shp = list(lhsT.shape)
        kt_ = _rnd(shp[0])
        pe_drain_if((kt_, _rnd(int(np.prod(shp[1:]))), lhsT.start_partition() if kt_ < 128 else 0))
        return P.op("pe", lambda e: e.matmul(out, lhsT=lhsT, rhs=rhs, start=start, stop=stop), R, W, pe_accum=True)

    def tr(out, in_, ident, R, W):
        shp = list(in_.shape)
        pe_drain_if((_rnd(shp[0]), _rnd(int(np.prod(shp[1:])))))
        return P.op("pe", lambda e: e.transpose(out=out, in_=in_, identity=ident), R, W, pe_accum=True)

    def ms(eng, ap, val, W):
        return P.op(eng, lambda e: e.memset(ap, val), (), W)

    def ld(q, key, out, in_, W, R=()):
        return P.dma(q, key, lambda e: e.dma_start(out=out, in_=in_), R, W)

    def tap(name, ap, R):
        if name in TAP:
            P.dma("sp", "tap", lambda e: e.dma_start(out=TAP[name], in_=ap), R, [out_tok])

    def cmul(eng, ore, oim, xre, xim, yre, yim, t1, t2, R, W, Tt):
        tt(eng, t1, xre, yre, ALU.mult, R, [Tt])
        tt(eng, t2, xim, yim, ALU.mult, R, [Tt])
        tt(eng, ore, t1, t2, ALU.subtract, [Tt], W)
        tt(eng, t1, xre, yim, ALU.mult, R, [Tt])
        tt(eng, t2, xim, yre, ALU.mult, R, [Tt])
        tt(eng, oim, t1, t2, ALU.add, [Tt], W)

    xT = Buf(mem.sb("xT", [128, 8, NT], F32), "xT", 5)
    identf = Buf(mem.sb("identf", [128, 128], F32), "identf")
    identb = Buf(mem.sb("identb", [128, 128], BF16), "identb")
    onesb = Buf(mem.sb("onesb", [128, 128], BF16), "onesb")
    siluC = Buf(mem.sb("siluC", [128, 8, 5], F32), "siluC")
    PS = [Buf(mem.ps(f"ps{i}", [128, 512], F32), f"ps{i}") for i in range(8)]
    for b__ in PS:
        b__.tok.excl = True
    BANKS = [(i * 512, min(NT, (i + 1) * 512)) for i in range(5)]

    ropeT = Buf(mem.sb("ropeT", [128, 2, 17, 8], F32), "ropeT")
    ropeC = Buf(mem.sb("ropeC", [128, 2, 128], F32), "ropeC")
    zerob = Buf(mem.sb("zerob", [128, 512], BF16), "zerob")
    rmk = Buf(mem.sb("rmk", [128, 2], F32), "rmk")
    ld("sp", "cst", ropeC.t[:], D["ropeC"], [ropeC.tok])
    ms("pool", zerob.t[:], 0.0, [zerob.tok])
    ms("pool", rmk.t[64:128, 0:1], 1.0, [rmk.tok])
    ms("pool", rmk.t[64:128, 1:2], 1.0, [rmk.tok])
    ms("pool", rmk.t[64:96, 1:2], 0.0, [rmk.tok])
    ts("pool", rmk.t[64:128, 0:1], rmk.t[64:128, 1:2], -1.0, 1.0, ALU.mult, ALU.add, [rmk.tok], [rmk.tok])
    ld("sp", "cst", ropeT.t[:, 0], D["cosT"], [ropeT.tok])
    ld("sp", "cst", ropeT.t[:, 1], D["sinT"], [ropeT.tok])
    for b_, (c0, c1) in enumerate(BANKS):
        ld("sp", "xin", xT.t[:, :, c0:c1], D["xT0"][:, :, c0:c1], [xT.toks[b_]])
    ld("sp", "cst", identf.t[:], D["ident"], [identf.tok])
    ld("sp", "cst", siluC.t[:], D["cT"], [siluC.tok])
    cp("dve", identb.t[:], identf.t[:], [identf.tok], [identb.tok])
    ms("dve", onesb.t[:], 1.0, [onesb.tok])
    act(siluC.t[:], siluC.t[:], AF.Silu, [siluC.tok], [siluC.tok])

    modT = Buf(mem.sb("modT", [128, 24, 5], F32), "modT")
    s1g = Buf(mem.sb("s1g", [128, 8, 5], F32), "s1g")
    s1gS = Buf(mem.sb("s1gS", [128, 8, 16], F32), "s1gS")
    shS = Buf(mem.sb("shS", [128, 8, 16], F32), "shS")
    gtS = Buf(mem.sb("gtS", [128, 8, 16], F32), "gtS")
    lvec = Buf(mem.sb("lvec", [128, 24 + 8 + 4], F32), "lvec")

    def adaln_phase(l):
        m0 = mem.mark()
        adw = [Buf(mem.sb(f"adw{i}", [128, 8, 512], F32), f"adw{i}") for i in range(2)]
        ld("sp", "lvec", lvec.t[:, 0:24], D["ada_bT"][l], [lvec.tok])
        ld("sp", "lvec", lvec.t[:, 24:32], D["norm_gT"][l], [lvec.tok])
        ld("sp", "lvec", lvec.t[:, 32:36], D["glu_bT"][l], [lvec.tok])
        psA = PS[0]
        wv = D["ada_w"][l].rearrange("(kc p) n -> p kc n", p=128)
        for blk in range(6):
            buf = adw[blk % 2]
            ld("sp", f"adw{blk % 2}", buf.t[:], wv[:, :, blk * 512:(blk + 1) * 512], [buf.tok])
            for oc4 in range(4):
                oc = blk * 4 + oc4
                for kc in range(8):
                    mm(psA.t[:, oc * 5:(oc + 1) * 5], buf.t[:, kc, oc4 * 128:(oc4 + 1) * 128], siluC.t[:, kc, :],
                       kc == 0, kc == 7, [buf.tok, siluC.tok], [psA.tok])
        tt("dve", modT.t[:], psA.t[:, 0:120].rearrange("p (o b) -> p o b", b=5),
           lvec.t[:, 0:24].unsqueeze(2).to_broadcast([128, 24, 5]), ALU.add, [psA.tok, lvec.tok], [modT.tok])
        ts("dve", s1g.t[:], modT.t[:, 8:16, :], 1.0, None, ALU.add, None, [modT.tok], [s1g.tok])
        tt("dve", s1g.t[:], s1g.t[:], lvec.t[:, 24:32].unsqueeze(2).to_broadcast([128, 8, 5]), ALU.mult,
           [s1g.tok, lvec.tok], [s1g.tok])
        for dst, src in ((s1gS, s1g.t[:, :, 1:5]), (shS, modT.t[:, 0:8, 1:5]), (gtS, modT.t[:, 16:24, 1:5])):
            cp("dve", dst.t[:].rearrange("p k (b j) -> p k b j", j=4),
               src.unsqueeze(3).to_broadcast([128, 8, 4, 4]), [modT.tok, s1g.tok], [dst.tok])
        end_phase(m0)

    def end_phase(m0):
        P.barrier()
        mem.release(m0)

    def norm_phase(hT):
        m0 = mem.mark()
        sq = Buf(mem.sb("sq", [128, 8, 256], BF16), "sq")
        xn = Buf(mem.sb("xn", [128, 8, 256], F32), "xn")
        rt = Buf(mem.sb("rt", [128, 256], F32), "rt")
        chunks = [(i * 256, i * 256 + 256) for i in range(8)] + [(TP, NT)]
        for ci, (c0, c1) in enumerate(chunks):
            w = c1 - c0
            b_ = min(c0 // 512, 4)
            ps = PS[1 + (ci % 2)]
            act(sq.t[:, :, 0:w], xT.t[:, :, c0:c1], AF.Square, [xT.toks[b_]], [sq.tok])
            for kc in range(8):
                mm(ps.t[:, 0:w], onesb.t[:], sq.t[:, kc, 0:w], kc == 0, kc == 7, [onesb.tok, sq.tok], [ps.tok])
            act(rt.t[:, 0:w], ps.t[:, 0:w], AF.Sqrt, [ps.tok], [rt.tok], bias=1e-6, scale=1.0 / 1024.0)
            P.op("dve", lambda e, w=w: e.reciprocal(out=rt.t[:, 0:w], in_=rt.t[:, 0:w]), [rt.tok], [rt.tok])
            tt("dve", xn.t[:, :, 0:w], xT.t[:, :, c0:c1], rt.t[:, 0:w].unsqueeze(1).to_broadcast([128, 8, w]),
               ALU.mult, [xT.toks[b_], rt.tok], [xn.tok])
            if b_ < 4:
                for kc in range(8):
                    act(hT.t[:, kc, c0:c1], xn.t[:, kc, 0:w], AF.Identity, [xn.tok, s1g.tok, modT.tok], [hT.toks[b_]],
                        bias=modT.t[:, kc, 0:1], scale=s1g.t[:, kc, 0:1])
            else:
                tt("dve", xn.t[:, :, 0:w], xn.t[:, :, 0:w], s1gS.t[:], ALU.mult, [xn.tok, s1gS.tok], [xn.tok])
                tt("dve", hT.t[:, :, c0:c1], xn.t[:, :, 0:w], shS.t[:], ALU.add, [xn.tok, shS.tok], [hT.toks[b_]])
        end_phase(m0)

    hlP = Buf(mem.sb("hlP", [128, 2, 16], F32), "hlP")
    hlS = Buf(mem.sb("hlS", [128, 2, 16, 4], F32), "hlS")

    def s5_params(eng, are, aim, ldt, shape, nm):
        T = Tok(nm)
        mk = lambda n, dt=F32: mem.sb(f"{nm}_{n}", shape, dt)
        abr, abi, er, ei = mk("abr"), mk("abi"), mk("er"), mk("ei")
        m1 = mem.mark()
        dt_, ang, mag, r, kf, mk_, sn, cs, t1, t2 = [mk(n) for n in ("dt", "ang", "mag", "r", "kf", "m", "sn", "cs", "t1", "t2")]
        ki = mk("ki", I32)
        R = W = [T]
        A = lambda t: t[:]
        act(A(dt_), ldt, AF.Exp, R, W)
        tt(eng, A(ang), aim, A(dt_), ALU.mult, R, W)
        tt(eng, A(t1), are, A(dt_), ALU.mult, R, W)
        act(A(mag), A(t1), AF.Exp, R, W)

        def red_sin(out, add):
            ts(eng, A(r), A(ang), add, None, ALU.add, None, R, W)
            ts(eng, A(ki), A(r), 1.0 / TWO_PI, None, ALU.mult, None, R, W)
            cp(eng, A(kf), A(ki), R, W)
            ts(eng, A(kf), A(kf), -TWO_PI, None, ALU.mult, None, R, W)
            tt(eng, A(r), A(r), A(kf), ALU.add, R, W)
            ts(eng, A(mk_), A(r), float(np.pi), -TWO_PI, ALU.is_gt, ALU.mult, R, W)
            tt(eng, A(r), A(r), A(mk_), ALU.add, R, W)
            ts(eng, A(mk_), A(r), -float(np.pi), TWO_PI, ALU.is_lt, ALU.mult, R, W)
            tt(eng, A(r), A(r), A(mk_), ALU.add, R, W)
            act(A(out), A(r), AF.Sin, R, W)
        red_sin(sn, 0.0)
        red_sin(cs, float(np.pi / 2))
        tt(eng, A(abr), A(mag), A(cs), ALU.mult, R, W)
        tt(eng, A(abi), A(mag), A(sn), ALU.mult, R, W)
        tt(eng, A(t1), are, are, ALU.mult, R, W)
        tt(eng, A(t2), aim, aim, ALU.mult, R, W)
        tt(eng, A(t1), A(t1), A(t2), ALU.add, R, W)
        P.op("dve", lambda e: e.reciprocal(out=A(t1), in_=A(t1)), R, W)
        ts(eng, A(t2), A(abr), -1.0, None, ALU.add, None, R, W)
        tt(eng, A(er), A(t2), are, ALU.mult, R, W)
        tt(eng, A(mk_), A(abi), aim, ALU.mult, R, W)
        tt(eng, A(er), A(er), A(mk_), ALU.add, R, W)
        tt(eng, A(er), A(er), A(t1), ALU.mult, R, W)
        tt(eng, A(ei), A(abi), are, ALU.mult, R, W)
        tt(eng, A(mk_), A(t2), aim, ALU.mult, R, W)
        tt(eng, A(ei), A(ei), A(mk_), ALU.subtract, R, W)
        tt(eng, A(ei), A(ei), A(t1), ALU.mult, R, W)
        P.barrier()
        mem.release(m1)
        return abr, abi, er, ei, T

    def s5_phase(l, uT, zs5T):
        m0 = mem.mark()
        aNL = mem.sb("aNL", [128, 3, 16], F32)
        aPR = mem.sb("aPR", [128, 4, 3, 128], F32)
        Tin = Tok("s5in")
        ld("sp", "s5in", aNL[:], D["aNL"][l], [Tin])
        ld("sp", "s5in", aPR[:], D["aPR"][l], [Tin])
        P.barrier()
        nabr, nabi, ner, nei, TN = s5_params("dve", aNL[:, 0, :], aNL[:, 1, :], aNL[:, 2, :], [128, 16], "pn")
        pabr, pabi, per_, pei, TPp = s5_params("dve", aPR[:, :, 0, :], aPR[:, :, 1, :], aPR[:, :, 2, :], [128, 4, 128], "pp")
        chk("s5a")
        pw = [(nabr, nabi)]
        tA, tB = mem.sb("pw_t1", [128, 16], F32), mem.sb("pw_t2", [128, 16], F32)
        for k in range(4):
            r_, i_ = mem.sb(f"pw{k}r", [128, 16], F32), mem.sb(f"pw{k}i", [128, 16], F32)
            cmul("dve", r_[:], i_[:], pw[-1][0][:], pw[-1][1][:], pw[-1][0][:], pw[-1][1][:], tA[:], tB[:], [TN], [TN], TN)
            pw.append((r_, i_))
        a4r, a4i = pw[2]
        a16r, a16i = pw[4]
        Bp = mem.sb("Bp", [128, 2, 128], F32)
        Bn = mem.sb("Bn", [128, 2, 128], F32)
        Cn = mem.sb("Cn", [128, 2, 4, 32], F32)
        h0 = mem.sb("h0", [128, 2, 4, 4], F32)
        dvq = mem.sb("dvq", [128, 4], F32)
        TL = Tok("s5ld")
        ld("sp", "s5ld", dvq[:], D["dv"][l], [TL])
        curP = [mem.sb(f"curP{i}", [128, 2, 128], F32) for i in range(2)]
        curC = [mem.sb(f"curC{i}", [128, 2, 4, 32], F32) for i in range(2)]
        tP1, tP2 = mem.sb("tP1", [128, 128], F32), mem.sb("tP2", [128, 128], F32)
        tC1, tC2 = mem.sb("tC1", [128, 4, 32], F32), mem.sb("tC2", [128, 4, 32], F32)
        BAtab = mem.sb("BAtab", [128, 16, 2, 128], BF16)
        CAtab = mem.sb("CAtab", [128, 17, 2, 5, 32], BF16)
        BAtab3 = mem.sb("BAtab3", [128, 16, 2, 128], BF16)
        rmask = mem.sb("rmask", [128, 2], F32)
        BbN = mem.sb("BbN", [128, 2, 128], F32)
        BbNb = mem.sb("BbNb", [128, 2, 5, 32], BF16)
        Ktab = mem.sb("Ktab", [128, 16, 128], BF16)
        yacc = mem.sb("yacc", [128, NT], F32)
        SS = [mem.sb(f"SS{i}", [128, 2, 4, 128], F32) for i in range(2)]
        st = [mem.sb(f"st{i}", [128, 4, 128], F32) for i in range(4)]
        Ak = [mem.sb(f"Ak{i}", [128, 2, 4], F32) for i in range(2)]
        Akt = [mem.sb(f"Akt{i}", [128, 4], F32) for i in range(2)]
        Hb = mem.sb("Hb", [128, 2, 4, 128], BF16)
        Hsb = mem.sb("Hsb", [128, 2, 4, 4], BF16)
        Zs = mem.sb("Zs", [128, 2, 4, 4], F32)
        hs1, hs2 = mem.sb("hs1", [128, 4, 4], F32), mem.sb("hs2", [128, 4, 4], F32)
        g1, g2 = mem.sb("g1", [128, 512], F32), mem.sb("g2", [128, 512], F32)
        TBA, TCA, TK, TBb, TY, TS, TH, TG, TZ = [Tok(n) for n in ("BA", "CA", "K", "Bb", "Y", "S", "H", "G", "Z")]
        Tst = [Tok(f"st{i}") for i in range(2)]
        TSS = [[Tok(f"SS{i}{j}") for j in range(2)] for i in range(2)]
        uv_all = uT.t
        TM_ = Tok("rmask")
        ms("pool", CAtab[:], 0.0, [TCA])
        ms("pool", BbNb[:], 0.0, [TBb])
        ms("pool", BAtab3[:], 0.0, [TBA])
        ms("pool", rmask[64:128, 0:1], 1.0, [TM_])
        ms("pool", rmask[64:128, 1:2], 0.0, [TM_])
        ms("pool", rmask[64:96, 1:2], 1.0, [TM_])
        ts("pool", rmask[64:128, 0:1], rmask[64:128, 1:2], 1.0, None, ALU.mult, None, [TM_], [TM_])
        ts("pool", rmask[64:128, 1:2], rmask[64:128, 0:1], -1.0, 1.0, ALU.mult, ALU.add, [TM_], [TM_])
        for q in range(4):
            qs = slice(q * 4, q * 4 + 4)
            ld("sp", "s5ld", Bp[:], D["BPR"][l][:, q], [TL])
            ld("sp", "s5ld", Bn[:], D["BNL"][l][:, q], [TL])
            ld("sp", "s5ld", Cn[:], D["CNL"][l][:, q], [TL])
            ld("sp", "s5ld", h0[:], D["h0NL"][l][:, :, qs, :], [TL])
            cmul("pool", curP[0][:, 0, :], curP[0][:, 1, :], Bp[:, 0, :], Bp[:, 1, :], per_[:, q, :], pei[:, q, :],
                 tP1[:], tP2[:], [TL, TPp], [TBA], TBA)
            ci = 0
            for j in range(15, -1, -1):
                cp("pool", BAtab[:, j, :, :], curP[ci][:], [TBA], [TBA])
                ts("pool", BAtab3[64:128, j, :, :], curP[ci][64:128], rmask[64:128, 1:2], None, ALU.mult, None, [TBA, TM_], [TBA])
                if j > 0:
                    cmul("pool", curP[1 - ci][:, 0, :], curP[1 - ci][:, 1, :], curP[ci][:, 0, :], curP[ci][:, 1, :],
                         pabr[:, q, :], pabi[:, q, :], tP1[:], tP2[:], [TPp, TBA], [TBA], TBA)
                    ci = 1 - ci
            chk("s5b")
            abr_b = nabr[:, qs].unsqueeze(2).to_broadcast([128, 4, 32])
            abi_b = nabi[:, qs].unsqueeze(2).to_broadcast([128, 4, 32])
            cp("dve", curC[0][:], Cn[:], [TL], [TCA])
            ci = 0
            for d in range(17):
                cp("dve", CAtab[:, d, :, 0:3, :], curC[ci][:, :, 0:3, :], [TCA], [TCA])
                cp("dve", CAtab[:, d, :, 4, :], curC[ci][:, :, 3, :], [TCA], [TCA])
                if d < 16:
                    cmul("dve", curC[1 - ci][:, 0], curC[1 - ci][:, 1], curC[ci][:, 0], curC[ci][:, 1], abr_b, abi_b,
                         tC1[:], tC2[:], [TN, TCA], [TCA], TCA)
                    ci = 1 - ci
            er_b = ner[:, qs].unsqueeze(2).to_broadcast([128, 4, 32])
            ei_b = nei[:, qs].unsqueeze(2).to_broadcast([128, 4, 32])
            v4 = lambda ap: ap.rearrange("p (a b) -> p a b", b=32)
            cmul("dve", v4(BbN[:, 0, :]), v4(BbN[:, 1, :]), v4(Bn[:, 0, :]), v4(Bn[:, 1, :]), er_b, ei_b, tC1[:], tC2[:],
                 [TL, TN], [TBb], TBb)
            BbN4 = BbN[:].rearrange("p r (a b) -> p r a b", b=32)
            cp("dve", BbNb[:, 0, 0:3, :], BbN4[:, 0, 0:3, :], [TBb], [TBb])
            cp("dve", BbNb[:, 0, 4, :], BbN4[:, 0, 3, :], [TBb], [TBb])
            ts("dve", BbNb[:, 1, 0:3, :], BbN4[:, 1, 0:3, :], -1.0, None, ALU.mult, None, [TBb], [TBb])
            ts("dve", BbNb[:, 1, 4, :], BbN4[:, 1, 3, :], -1.0, None, ALU.mult, None, [TBb], [TBb])
            chk("s5c")
            psK = PS[2]
            BI = [0, 1, 2, 4]
            for pb in range(2):
                rows = slice(pb * 32, pb * 32 + 32)
                for ri in range(2):
                    mm(psK.t[rows, :].rearrange("p (d c) -> p d c", c=32), BbNb[:, ri, pb, :],
                       CAtab[:, 0:16, ri, pb, :], ri == 0, ri == 1, [TBb, TCA], [psK.tok])
            k_ = 0
            for pb in (2, 3):
                for ri in range(2):
                    mm(psK.t[64:128, :].rearrange("p (d c) -> p d c", c=32),
                       BbNb[:, ri, pb:pb + 2, :].rearrange("p a b -> p (a b)"),
                       CAtab[:, 0:16, ri, BI[pb], :], k_ == 0, k_ == 3, [TBb, TCA], [psK.tok])
                    k_ += 1
            ms("pool", Ktab[:], 0.0, [TK])
            for pb in range(2):
                rows = slice(pb * 32, pb * 32 + 32)
                cp("dve", Ktab[rows, :, pb * 32:(pb + 1) * 32], psK.t[rows, :].rearrange("p (d c) -> p d c", c=32),
                   [psK.tok], [TK])
            for pb in (2, 3):
                ts("dve", Ktab[64:128, :, pb * 32:(pb + 1) * 32], psK.t[64:128, :].rearrange("p (d c) -> p d c", c=32),
                   rmask[64:128, pb - 2:pb - 1], None, ALU.mult, None, [psK.tok, TM_], [TK])
            stt(Ktab[:, 0, :], identf.t[:], dvq[:, q:q + 1], Ktab[:, 0, :], ALU.mult, ALU.add, [identf.tok, TL, TK], [TK])
            chk("s5d")
            for b_, (c0, c1) in enumerate(BANKS):
                ps = PS[b_ % 2]
                if b_ < 4:
                    Yv = ps.t[:].rearrange("p (n i) -> p n i", i=16)
                    uv = uv_all[:, q, c0:c1].rearrange("p (n i) -> p n i", i=16)
                    nl = 16
                else:
                    Yv = ps.t[:, 0:16].rearrange("p (n i) -> p n i", i=4)
                    uv = uv_all[:, q, c0:c1].rearrange("p (n i) -> p n i", i=4)
                    nl = 4
                for d in range(nl):
                    mm(Yv[:, :, d:nl], Ktab[:, d, :], uv[:, :, 0:nl - d], d == 0, d == nl - 1, [TK, uT.tok], [ps.tok])
                act(yacc[:, c0:c1], ps.t[:, 0:c1 - c0], AF.Copy, [ps.tok], [TY])
            chk("s5e")
            up = uv_all[:, q, 0:TP].rearrange("p (n j) -> p n j", j=16)
            us = uv_all[:, q, TP:NT].rearrange("p (b j) -> p b j", j=4)
            for pb in range(4):
                rows = slice(pb * 32, pb * 32 + 32)
                pz = PS[cfg.get('zbank', 4) + pb]
                tab = BAtab
                if pb == 3:
                    rows = slice(64, 128)
                    tab = BAtab3
                zmode = cfg.get("zmode", 0)
                if zmode == 2 and pb == 3:
                    continue
                if zmode == 3 and pb > 0:
                    continue
                for ri in range(2):
                    if zmode == 5:
                        continue
                    for j in range(16):
                        mm(pz.t[:, ri * 128:(ri + 1) * 128], tab[rows, j, ri, :], up[rows, :, j], j == 0, j == 15,
                           [TBA, uT.tok], [pz.tok])
                    if zmode == 1:
                        continue
                    for j in range(4):
                        mm(pz.t[:, 256 + ri * 4:260 + ri * 4], tab[rows, 12 + j, ri, :], us[rows, :, j], j == 0, j == 3,
                           [TBA, uT.tok], [pz.tok])
                if zmode == 4:
                    continue
                cp("dve", SS[0][:, 0, pb, :], pz.t[:, 0:128], [pz.tok], [TSS[0][0]])
                act(SS[0][:, 1, pb, :], pz.t[:, 128:256], AF.Copy, [pz.tok], [TSS[0][1]])
                cp("dve", Zs[:, :, pb, :], pz.t[:, 256:264].rearrange("p (r b) -> p r b", b=4), [pz.tok], [TZ])
            chk("s5f")
            cp("dve", Ak[0][:, 0, :], a16r[:, qs], [TN], [TG])
            cp("dve", Ak[0][:, 1, :], a16i[:, qs], [TN], [TG])
            cur = 0
            for k in range(7):
                s = 1 << k
                src, dst = SS[cur], SS[1 - cur]
                Tsrc, Tdst = TSS[cur], TSS[1 - cur]
                w_ = 128 - s
                Ar = Ak[k % 2][:, 0, :].unsqueeze(2).to_broadcast([128, 4, w_])
                Ai = Ak[k % 2][:, 1, :].unsqueeze(2).to_broadcast([128, 4, w_])
                tt("dve", st[0][:, :, 0:w_], src[:, 0, :, 0:w_], Ar, ALU.mult, [Tsrc[0], TG], [Tst[0]])
                tt("dve", st[1][:, :, 0:w_], src[:, 1, :, 0:w_], Ai, ALU.mult, [Tsrc[1], TG], [Tst[0]])
                tt("dve", st[0][:, :, 0:w_], st[0][:, :, 0:w_], st[1][:, :, 0:w_], ALU.subtract, [Tst[0]], [Tst[0]])
                tt("dve", dst[:, 0, :, s:128], src[:, 0, :, s:128], st[0][:, :, 0:w_], ALU.add, [Tsrc[0], Tst[0]], [Tdst[0]])
                cp("dve", dst[:, 0, :, 0:s], src[:, 0, :, 0:s], [Tsrc[0]], [Tdst[0]])
                tt("pool", st[2][:, :, 0:w_], src[:, 1, :, 0:w_], Ar, ALU.mult, [Tsrc[1], TG], [Tst[1]])
                tt("pool", st[3][:, :, 0:w_], src[:, 0, :, 0:w_], Ai, ALU.mult, [Tsrc[0], TG], [Tst[1]])
                tt("pool", st[2][:, :, 0:w_], st[2][:, :, 0:w_], st[3][:, :, 0:w_], ALU.add, [Tst[1]], [Tst[1]])
                tt("pool", dst[:, 1, :, s:128], src[:, 1, :, s:128], st[2][:, :, 0:w_], ALU.add, [Tsrc[1], Tst[1]], [Tdst[1]])
                cp("pool", dst[:, 1, :, 0:s], src[:, 1, :, 0:s], [Tsrc[1]], [Tdst[1]])
                if k < 6:
                    a_, b_2 = Ak[k % 2], Ak[1 - k % 2]
                    cmul("dve", b_2[:, 0, :], b_2[:, 1, :], a_[:, 0, :], a_[:, 1, :], a_[:, 0, :], a_[:, 1, :],
                         Akt[0][:], Akt[1][:], [TG], [TG], TG)
                cur = 1 - cur
            Sf, TSf = SS[cur], TSS[cur]
            chk("s5g")
            cp("dve", hlP.t[:, :, qs], Sf[:, :, :, 127], TSf, [hlP.tok])
            ms("pool", Hb[:, :, :, 0:1], 0.0, [TH])
            cp("dve", Hb[:, 0, :, 1:128], Sf[:, 0, :, 0:127], [TSf[0]], [TH])
            ts("dve", Hb[:, 1, :, 1:128], Sf[:, 1, :, 0:127], -1.0, None, ALU.mult, None, [TSf[1]], [TH])
            cp("dve", Hsb[:, 0], h0[:, 0], [TL], [TH])
            ts("dve", Hsb[:, 1], h0[:, 1], -1.0, None, ALU.mult, None, [TL], [TH])
            a4r_b = a4r[:, qs].unsqueeze(2).to_broadcast([128, 4, 4])
            a4i_b = a4i[:, qs].unsqueeze(2).to_broadcast([128, 4, 4])
            cmul("dve", hlS.t[:, 0, qs, :], hlS.t[:, 1, qs, :], h0[:, 0], h0[:, 1], a4r_b, a4i_b, hs1[:], hs2[:],
                 [TL, TN], [hlS.tok], hlS.tok)
            tt("dve", hlS.t[:, :, qs, :], hlS.t[:, :, qs, :], Zs[:], ALU.add, [hlS.tok, TZ], [hlS.tok])
            chk("s5h")
            def inter(outp, c0_, c1_, i, Hop, part):
                if part == 0:
                    for pb in range(2):
                        rows = slice(pb * 32, pb * 32 + 32)
                        for ri in range(2):
                            mm(outp.t[rows, c0_:c1_], CAtab[:, i + 1, ri, pb, :], Hop[:, ri, pb, :],
                               ri == 0, ri == 1, [TCA, TH], [outp.tok])
                else:
                    k_ = 0
                    for pb in (2, 3):
                        for ri in range(2):
                            mm(outp.t[64:128, c0_:c1_], CAtab[:, i + 1, ri, pb:pb + 2, :].rearrange("p a b -> p (a b)"),
                               Hop[:, ri, pb, :], k_ == 0, k_ == 3, [TCA, TH], [outp.tok])
                            k_ += 1
            pss = PS[3]
            for part in range(2):
                for i in range(16):
                    inter(PS[4 + i // 4], (i % 4) * 128, (i % 4 + 1) * 128, i, Hb, part)
                for i in range(4):
                    inter(pss, i * 4, i * 4 + 4, i, Hsb, part)
            chk("s5i")
            yv = yacc[:, 0:TP].rearrange("p (n i) -> p n i", i=16)
            for ib in range(4):
                tt("dve", yv[:, :, ib * 4:ib * 4 + 4], yv[:, :, ib * 4:ib * 4 + 4],
                   PS[4 + ib].t[:].rearrange("p (i n) -> p n i", n=128), ALU.add, [TY, PS[4 + ib].tok], [TY])
            ysv = yacc[:, TP:NT].rearrange("p (b i) -> p b i", i=4)
            tt("dve", ysv, ysv, pss.t[:, 0:16].rearrange("p (i b) -> p b i", b=4), ALU.add, [TY, pss.tok], [TY])
            tap(f"y5scan{q}", yacc[:], [TY])
            for b_, (c0, c1) in enumerate(BANKS):
                w = c1 - c0
                yy = yacc[:, c0:c1]
                tt("dve", g1[:, 0:w], yy, yy, ALU.mult, [TY], [TG])
                ts("dve", g1[:, 0:w], g1[:, 0:w], 0.044715 * 1.5957691216, 1.5957691216, ALU.mult, ALU.add, [TG], [TG])
                tt("dve", g1[:, 0:w], g1[:, 0:w], yy, ALU.mult, [TG, TY], [TG])
                act(g2[:, 0:w], g1[:, 0:w], AF.Sigmoid, [TG], [TG])
                tt("dve", uv_all[:, q, c0:c1], yy, g2[:, 0:w], ALU.mult, [TG, TY], [uT.tok])
        P.dma("sp", "out", lambda e: e.dma_start(out=O["hlP"][l], in_=hlP.t[:]), [hlP.tok], [out_tok])
        P.dma("sp", "out", lambda e: e.dma_start(out=O["hlS"][l], in_=hlS.t[:]), [hlS.tok], [out_tok])
        end_phase(m0)

    def uz_proj_phase(l, hT, uT, zs5T):
        m0 = mem.mark()
        wuz = Buf(mem.sb("wuz", [128, 8, 1024], BF16), "wuz", 2)
        wv = D["w_in"][l].rearrange("(kc p) n -> p kc n", p=128)
        for h_ in range(2):
            for kc in range(8):
                ld("pool", f"wuz{h_}", wuz.t[:, kc, h_ * 512:(h_ + 1) * 512], wv[:, kc, h_ * 512:(h_ + 1) * 512], [wuz.toks[h_]])
        k = 0
        for oc in range(8):
            for b_, (c0, c1) in enumerate(BANKS):
                w = c1 - c0
                ps = PS[k % 4]
                k += 1
                for kc in range(8):
                    mm(ps.t[:, 0:w], wuz.t[:, kc, oc * 128:(oc + 1) * 128], hT.t[:, kc, c0:c1], kc == 0, kc == 7,
                       [wuz.toks[oc // 4], hT.toks[b_]], [ps.tok])
                if oc < 4:
                    act(uT.t[:, oc, c0:c1], ps.t[:, 0:w], AF.Copy, [ps.tok], [uT.tok])
                else:
                    act(zs5T.t[:, oc - 4, c0:c1], ps.t[:, 0:w], AF.Silu, [ps.tok], [zs5T.tok])
        end_phase(m0)

    def glu_phase(l, uT, zs5T):
        m0 = mem.mark()
        wg = Buf(mem.sb("wg", [128, 4, 512], BF16), "wg")
        sg = Buf(mem.sb("sg", [128, 512], F32), "sg")
        for kc in range(4):
            ld("pool", "wg", wg.t[:, kc, :], D["glu_w"][l].rearrange("(kc p) n -> p kc n", p=128)[:, kc, :], [wg.tok])
        k = 0
        for oc in range(4):
            for b_, (c0, c1) in enumerate(BANKS):
                w = c1 - c0
                ps = PS[k % 4]
                k += 1
                for kc in range(4):
                    mm(ps.t[:, 0:w], wg.t[:, kc, oc * 128:(oc + 1) * 128], uT.t[:, kc, c0:c1], kc == 0, kc == 3,
                       [wg.tok, uT.tok], [ps.tok])
                act(sg.t[:, 0:w], ps.t[:, 0:w], AF.Sigmoid, [ps.tok, lvec.tok], [sg.tok], bias=lvec.t[:, 32 + oc:33 + oc])
                tt("dve", zs5T.t[:, oc, c0:c1], zs5T.t[:, oc, c0:c1], sg.t[:, 0:w], ALU.mult, [sg.tok, zs5T.tok], [zs5T.tok])
        for b_, (c0, c1) in enumerate(BANKS):
            tt("dve", uT.t[:, :, c0:c1], uT.t[:, :, c0:c1], zs5T.t[:, :, c0:c1], ALU.mult, [uT.tok, zs5T.tok], [uT.tok])
        end_phase(m0)

    TILES = [(i, i * 128, 128) for i in range(16)] + [(16, TP, 16)]
    Q_OFF, KV_OFF, G_OFF, ZN_OFF, MG_OFF = 1024, 1536, 2304, 2328, 2840

    def rope_inplace(v, ti, rows, H, tmp, Tv, Tt):
        cos = ropeT.t[0:rows, 0, ti, :].unsqueeze(1).to_broadcast([rows, H, 8])
        sin = ropeT.t[0:rows, 1, ti, :].unsqueeze(1).to_broadcast([rows, H, 8])
        x1, x2 = v[:, :, 0:8], v[:, :, 8:16]
        t = [tmp[0:rows, i, 0:H, :] for i in range(4)]
        tt("dve", t[0], x1, cos, ALU.mult, [Tv, ropeT.tok], [Tt])
        tt("dve", t[1], x2, sin, ALU.mult, [Tv, ropeT.tok], [Tt])
        tt("dve", t[2], x1, sin, ALU.mult, [Tv, ropeT.tok], [Tt])
        tt("dve", t[3], x2, cos, ALU.mult, [Tv, ropeT.tok], [Tt])
        tt("dve", x1, t[0], t[1], ALU.subtract, [Tt], [Tv])
        tt("dve", x2, t[2], t[3], ALU.add, [Tt], [Tv])

    def tm_proj_phase(l, hT, A):
        m0 = mem.mark()
        wv = D["w_in"][l].rearrange("(kc p) n -> p kc n", p=128)
        wb = [Buf(mem.sb(f"wtm{i}", [128, 8, 512], BF16), f"wtm{i}") for i in range(2)]
        stg = [Buf(mem.sb(f"stg{i}", [128, 512], F32), f"stg{i}") for i in range(2)]
        rtmp = mem.sb("rtmp", [128, 4, 8, 8], F32)
        Trt = Tok("rtmp")
        nld = [0]

        def load_w(c0, ncol):
            buf = wb[nld[0] % 2]
            nld[0] += 1
            for kc in range(8):
                ld("pool", "wtm" + str(nld[0] % 2), buf.t[:, kc, 0:ncol], wv[:, kc, c0:c0 + ncol], [buf.tok])
            return buf
        k = [0]

        def proj_tile(buf, ncol, t0, rows):
            ps = PS[k[0] % 2]
            k[0] += 1
            for kc in range(8):
                mm(ps.t[0:rows, 0:ncol], hT.t[:, kc, t0:t0 + rows], buf.t[:, kc, 0:ncol], kc == 0, kc == 7,
                   [buf.tok, hT.toks[min(t0 // 512, 4)]], [ps.tok])
            return ps
        wq = wb[nld[0] % 2]
        nld[0] += 1
        for kc in range(8):
            for hq in range(4):
                src = wv[:, kc, Q_OFF + hq * 64:Q_OFF + hq * 64 + 512].rearrange("p (g r) -> p g r", g=2)[:, :, 0:64]
                ld("pool", "wtm" + str(nld[0] % 2), wq.t[:, kc, hq * 128:(hq + 1) * 128].rearrange("p (g d) -> p g d", g=2),
                   src, [wq.tok])
        for (ti, t0, rows) in TILES:
            ps = proj_tile(wq, 512, t0, rows)
            sg_ = stg[ti % 2]
            act(sg_.t[0:rows, :], ps.t[0:rows, :], AF.Copy, [ps.tok], [sg_.tok], scale=0.125)
            rope_inplace(sg_.t[0:rows, :].rearrange("p (h d) -> p h d", d=64), ti, rows, 8, rtmp, sg_.tok, Trt)
            pt = PS[2 + ti % 2]
            for hq in range(4):
                tr(pt.t[:, hq * 128:hq * 128 + rows], sg_.t[0:rows, hq * 128:(hq + 1) * 128], identf.t[0:rows, 0:rows],
                   [sg_.tok, identf.tok], [pt.tok])
            if ti < 16:
                cp("dve", A["qT"].t[:, :, t0:t0 + 128], pt.t[:].rearrange("p (h t) -> p h t", t=128), [pt.tok], [A["qT"].tok])
            else:
                cp("dve", A["qTs"].t[:], pt.t[:].rearrange("p (h t) -> p h t", t=128)[:, :, 0:16], [pt.tok], [A["qTs"].tok])
        wk = load_w(KV_OFF, 512)
        for (ti, t0, rows) in TILES:
            ps = proj_tile(wk, 512, t0, rows)
            sg_ = stg[ti % 2]
            act(sg_.t[0:rows, :], ps.t[0:rows, :], AF.Copy, [ps.tok], [sg_.tok])
            rope_inplace(sg_.t[0:rows, 256:384].rearrange("p (h d) -> p h d", d=64), ti, rows, 2, rtmp, sg_.tok, Trt)
            if ti < 16:
                P.dma("sp", f"stg{ti % 2}", lambda e, sg_=sg_, t0=t0: e.dma_start(out=O["kvP"][l][t0:t0 + 128, :], in_=sg_.t[:]),
                      [sg_.tok], [out_tok])
            else:
                P.dma("sp", f"stg{ti % 2}", lambda e, sg_=sg_: e.dma_start(out=O["kvS"][l], in_=sg_.t[0:16, :]),
                      [sg_.tok], [out_tok])
            pt = PS[2 + ti % 2]
            for s_ in range(3):
                tr(pt.t[:, s_ * 128:s_ * 128 + rows], sg_.t[0:rows, s_ * 128:(s_ + 1) * 128], identf.t[0:rows, 0:rows],
                   [sg_.tok, identf.tok], [pt.tok])
            cp("dve", A["KT3"].t[:, :, t0:t0 + rows], pt.t[:, 0:384].rearrange("p (s t) -> p s t", t=128)[:, :, 0:rows],
               [pt.tok], [A["KT3"].tok])
            cp("pool", A["vsel"].t[0:rows, ti, :, 0:64], sg_.t[0:rows, 384:512].rearrange("p (g d) -> p g d", d=64),
               [sg_.tok], [A["vsel"].tok])
            if ti == 16:
                cp("dve", A["S_kT"].t[:, 0, :], pt.t[:, 256:272], [pt.tok], [A["S_kT"].tok])
                cp("pool", A["S_v"].t[:, 0, :, :], sg_.t[0:16, 384:512].rearrange("p (g d) -> p g d", d=64), [sg_.tok], [A["S_v"].tok])
        ww = load_w(KV_OFF + 512, 280)
        for (ti, t0, rows) in TILES:
            ps = proj_tile(ww, 280, t0, rows)
            sg_ = stg[ti % 2]
            act(sg_.t[0:rows, 0:256], ps.t[0:rows, 0:256], AF.Copy, [ps.tok], [sg_.tok])
            act(A["gsig"].t[0:rows, ti, :], ps.t[0:rows, 256:280], AF.Sigmoid, [ps.tok], [A["gsig"].tok])
            rope_inplace(sg_.t[0:rows, 0:128].rearrange("p (h d) -> p h d", d=64), ti, rows, 2, rtmp, sg_.tok, Trt)
            if 12 <= ti < 16:
                P.dma("sp", f"stg{ti % 2}", lambda e, sg_=sg_, ti=ti: e.dma_start(
                    out=O["winP"][l][(ti - 12) * 128:(ti - 11) * 128, :], in_=sg_.t[:, 0:256]), [sg_.tok], [out_tok])
            elif ti == 16:
                for b in range(4):
                    P.dma("sp", f"stg{ti % 2}", lambda e, sg_=sg_, b=b: e.dma_start(
                        out=O["winS"][b, l, 508:512, :], in_=sg_.t[4 * b:4 * b + 4, 0:256]), [sg_.tok], [out_tok])
                    P.dma("sp", "wcopy", lambda e, b=b: e.dma_start(
                        out=O["winS"][b, l, 0:508, :], in_=D["swin"][b, l, 4:512, :]), [], [out_tok])
            pt = PS[2 + ti % 2]
            tr(pt.t[:, 0:rows], sg_.t[0:rows, 0:128], identf.t[0:rows, 0:rows], [sg_.tok, identf.tok], [pt.tok])
            cp("dve", A["kwinT"].t[:, t0:t0 + rows], pt.t[:, 0:rows], [pt.tok], [A["kwinT"].tok])
            cp("pool", A["vwin"].t[0:rows, ti, :, 0:64], sg_.t[0:rows, 128:256].rearrange("p (g d) -> p g d", d=64),
               [sg_.tok], [A["vwin"].tok])
            if ti == 16:
                cp("dve", A["S_kT"].t[:, 1, :], pt.t[:, 0:16], [pt.tok], [A["S_kT"].tok])
                cp("pool", A["S_v"].t[:, 1, :, :], sg_.t[0:16, 128:256].rearrange("p (g d) -> p g d", d=64), [sg_.tok], [A["S_v"].tok])
        wz = load_w(ZN_OFF, 512)
        kk = 0
        for oc in range(4):
            for b_, (c0, c1) in enumerate(BANKS):
                w = c1 - c0
                ps = PS[kk % 4]
                kk += 1
                for kc in range(8):
                    mm(ps.t[:, 0:w], wz.t[:, kc, oc * 128:(oc + 1) * 128], hT.t[:, kc, c0:c1], kc == 0, kc == 7,
                       [wz.tok, hT.toks[b_]], [ps.tok])
                act(A["zsT"].t[:, oc, c0:c1], ps.t[:, 0:w], AF.Silu, [ps.tok], [A["zsT"].tok])
        end_phase(m0)

    def compress_setup(l, A, W1, W2):
        for v in range(3):
            ld("pool", "w2", W2.t[:, v, :], D["W2bd"][l, v], [W2.tok])

    def load_w1(l, strm, W1):
        for c4 in range(16):
            ld("pool", "w1", W1.t[:, c4 * 4:(c4 + 1) * 4, :], D["W1bd"][l, strm][:, c4 * 4:(c4 + 1) * 4, :], [W1.tok])

    def compress(l, strm, rowsT, nblk, W1, psC, Trows):
        rv = rowsT.rearrange("p (n j) -> p n j", j=64)
        for j in range(64):
            mm(psC.t[:, 0:nblk + 1], W1.t[:, j, :], rv[:, :, j], j == 0, j == 63, [W1.tok, Trows], [psC.tok])

    def attn_prompt_phase(l, A):
        m0 = mem.mark()
        cmC = Buf(mem.sb("cmC", [128, 4, 16, 32], F32), "cmC")
        Ecst = Buf(mem.sb("Ecst", [128, 16, 128], BF16), "Ecst")
        triB = Buf(mem.sb("triB", [128, 2, 128], BF16), "triB")
        ld("sp", "cst", cmC.t[:], D["cm"], [cmC.tok])
        for kt4 in range(4):
            ld("pool", "cstp", Ecst.t[:, kt4 * 4:(kt4 + 1) * 4, :], D["Ecst"][:, kt4 * 4:(kt4 + 1) * 4, :], [Ecst.tok])
        ld("pool", "cstp", triB.t[:], D["tri"], [triB.tok])
        W1l = [Buf(mem.sb(f"W1_{i}", [128, 64, 128], BF16), f"W1_{i}") for i in range(2)]
        W2 = Buf(mem.sb("W2", [128, 3, 128], BF16), "W2")
        for i_ in range(2):
            load_w1(l, i_, W1l[i_])
        cb = mem.sb("cb", [128, 2], F32)
        silk = mem.sb("silk", [128, 32], BF16)
        silv4 = mem.sb("silv4", [128, 4, 32], BF16)
        kcT = mem.sb("kcT", [128, 32], BF16)
        Vbd2 = mem.sb("Vbd2", [128, 2, 128], BF16)
        kt1, kt2 = mem.sb("kt1", [128, 32], F32), mem.sb("kt2", [128, 32], F32)
        MT = mem.sb("MT", [128, TP], BF16)
        Tc = Tok("cmp")
        TMT = Tok("MT")
        compress_setup(l, A, None, W2)
        for strm in range(2):
            ld("pool", "pe", A["KT3"].t[:, strm, 2048:2112], D["peT"][l, strm], [A["KT3"].tok])
        for strm in range(2):
            psC = PS[2]
            compress(l, strm, A["KT3"].t[:, strm, 0:2112], 32, W1l[strm], psC, A["KT3"].tok)
            cp("dve", cb[:, strm:strm + 1], psC.t[:, 32:33], [psC.tok], [Tc])
            if strm == 0:
                act(silk[:], psC.t[:, 0:32], AF.Silu, [psC.tok, Tc], [Tc], bias=cb[:, 0:1])
            else:
                act(silv4[:], psC.t[:, 0:32].unsqueeze(1).to_broadcast([128, 4, 32]), AF.Silu, [psC.tok, Tc], [Tc], bias=cb[:, 1:2])
        ps = PS[3]
        mm(ps.t[:, 0:32], W2.t[:, 0, :], silk[:], True, True, [W2.tok, Tc], [ps.tok])
        mm(ps.t[:, 32:64], W2.t[:, 2, :], silk[:], True, True, [W2.tok, Tc], [ps.tok])
        tt("dve", kt1[:], ps.t[:, 0:32], ropeC.t[:, 0, 0:32], ALU.mult, [ps.tok, ropeC.tok], [Tc])
        tt("dve", kt2[:], ps.t[:, 32:64], ropeC.t[:, 1, 0:32], ALU.mult, [ps.tok, ropeC.tok], [Tc])
        tt("dve", kcT[:], kt1[:], kt2[:], ALU.add, [Tc], [Tc])
        ps2 = PS[1]
        mm(ps2.t[:, 0:128], silv4[:].rearrange("p a b -> p (a b)"), W2.t[:, 1, :], True, True, [W2.tok, Tc], [ps2.tok])
        ms("pool", Vbd2[:], 0.0, [Tc])
        for g in range(2):
            gs = slice(g * 64, g * 64 + 64)
            cp("dve", Vbd2[0:32, g, 0:64], ps2.t[0:32, gs], [ps2.tok], [Tc])
            cp("dve", Vbd2[32:64, g, 64:128], ps2.t[32:64, gs], [ps2.tok], [Tc])
            ts("dve", Vbd2[64:128, g, 0:64], ps2.t[64:128, gs], rmk.t[64:128, 0:1], None, ALU.mult, None, [ps2.tok, rmk.tok], [Tc])
            ts("dve", Vbd2[64:128, g, 64:128], ps2.t[64:128, gs], rmk.t[64:128, 1:2], None, ALU.mult, None, [ps2.tok, rmk.tok], [Tc])
        tap("kcT", kcT[:], [Tc])
        tap("Vbd2", Vbd2[:], [Tc])
        sc = mem.sb("sc", [128, 8, 32], F32)
        mx = mem.sb("mx", [128, 8], F32)
        imp = mem.sb("imp", [128, 2, 32], F32)
        wk = mem.sb("wk", [128, 2, 32], F32)
        m8 = mem.sb("m8", [128, 2, 16], F32)
        t1 = mem.sb("t1", [128, 2, 64], F32)
        t2 = mem.sb("t2", [128, 2, 32], F32)
        pT = mem.sb("pT", [128, 2, 128], BF16)
        oacc = mem.sb("oacc", [128, 8, 64], F32)
        tmpo = mem.sb("tmpo", [128, 4, 64], F32)
        rc = mem.sb("rc", [128, 4], F32)
        pbuf = [Buf(mem.sb(f"pbuf{i}", [128, 512], BF16), f"pbuf{i}") for i in range(4)]
        Ts, To, Tr = Tok("sc"), Tok("oacc"), Tok("rc")
        ms("pool", t1[:], 0.0, [Ts])
        npb = [0]

        def dense_branch(i, g, t0, KTap, Vbuf, pso, kts, blockmask, gate_off):
            first = True
            for c0_ in range(0, len(kts), 4):
                grp = kts[c0_:c0_ + 4]
                for k_, kt in enumerate(grp):
                    pss = PS[k_]
                    tri = 0 if kt == i else (1 if kt == i - 4 and not blockmask else None)
                    mm(pss.t[:], KTap[g * 64:(g + 1) * 64, kt * 128:(kt + 1) * 128], A["qT"].t[g * 64:(g + 1) * 64, :, t0:t0 + 128],
                       True, (not blockmask) and tri is None, [A["KT3"].tok, A["kwinT"].tok, A["qT"].tok], [pss.tok])
                    if blockmask:
                        mm(pss.t[:], Ecst.t[64 * g:64 * g + 64, kt, :],
                           MT[64 * g:64 * g + 64, t0:t0 + 128].unsqueeze(1).to_broadcast([64, 4, 128]),
                           False, tri is None, [Ecst.tok, TMT], [pss.tok])
                for k_, kt in enumerate(grp):
                    pss = PS[k_]
                    tri = 0 if kt == i else (1 if kt == i - 4 and not blockmask else None)
                    if tri is not None:
                        mm(pss.t[:], identb.t[:], triB.t[:, tri, :].unsqueeze(1).to_broadcast([128, 4, 128]),
                           False, True, [identb.tok, triB.tok], [pss.tok])
                if first:
                    mm(pso.t[:, 0:260], zerob.t[:, 0:128], zerob.t[:, 0:260], True, False, [zerob.tok], [pso.tok])
                    first = False
                for k_, kt in enumerate(grp):
                    pss = PS[k_]
                    pb_ = pbuf[k_]
                    act(pb_.t[:], pss.t[:], AF.Exp, [pss.tok], [pb_.tok])
                    for hq in range(4):
                        mm(pso.t[:, hq * 65:(hq + 1) * 65], pb_.t[:, hq * 128:(hq + 1) * 128], Vbuf.t[:, kt, g, :],
                           False, kt == kts[-1], [pb_.tok, Vbuf.tok], [pso.tok])
            pv = pso.t[:, 0:260].rearrange("p (h c) -> p h c", c=65)
            P.op("dve", lambda e: e.reciprocal(out=rc[:], in_=pv[:, :, 64]), [pso.tok], [Tr])
            tt("dve", rc[:], rc[:], A["gsig"].t[:, i, gate_off + g * 4:gate_off + g * 4 + 4], ALU.mult, [Tr, A["gsig"].tok], [Tr])
            tt("dve", tmpo[:], pv[:, :, 0:64], rc[:].unsqueeze(2).to_broadcast([128, 4, 64]), ALU.mult, [pso.tok, Tr], [Tr])
            tt("dve", oacc[:, g * 4:(g + 1) * 4, :], oacc[:, g * 4:(g + 1) * 4, :], tmpo[:], ALU.add, [To, Tr], [To])

        for i in range(16):
            t0 = i * 128
            psS = PS[0]
            for g in range(2):
                for hq in range(4):
                    h = g * 4 + hq
                    mm(psS.t[:, h * 32:(h + 1) * 32], A["qT"].t[g * 64:(g + 1) * 64, hq, t0:t0 + 128], kcT[g * 64:(g + 1) * 64, :],
                       True, True, [A["qT"].tok, Tc], [psS.tok])
            bc8 = lambda ap: ap.unsqueeze(1).to_broadcast([128, 8, 32])
            tt("dve", sc[:], psS.t[:, 0:256].rearrange("p (h n) -> p h n", n=32), bc8(cmC.t[:, 0, i, :]), ALU.add,
               [psS.tok, cmC.tok], [Ts])
            P.op("dve", lambda e: e.tensor_reduce(out=mx[:], in_=sc[:], axis=AX.X, op=ALU.max), [Ts], [Ts])
            tt("dve", sc[:], sc[:], mx[:].unsqueeze(2).to_broadcast([128, 8, 32]), ALU.subtract, [Ts], [Ts])
            act(sc[:], sc[:], AF.Exp, [Ts], [Ts])
            P.op("dve", lambda e: e.tensor_reduce(out=mx[:], in_=sc[:], axis=AX.X, op=ALU.add), [Ts], [Ts])
            P.op("dve", lambda e: e.reciprocal(out=mx[:], in_=mx[:]), [Ts], [Ts])
            tt("dve", sc[:], sc[:], mx[:].unsqueeze(2).to_broadcast([128, 8, 32]), ALU.mult, [Ts], [Ts])
            tt("dve", sc[:], sc[:], bc8(cmC.t[:, 1, i, :]), ALU.mult, [Ts, cmC.tok], [Ts])
            P.op("dve", lambda e: e.tensor_reduce(out=imp[:], in_=sc[:].rearrange("p (g q) n -> p g n q", g=2), axis=AX.X, op=ALU.add),
                 [Ts], [Ts])
            bc2 = lambda ap: ap.unsqueeze(1).to_broadcast([128, 2, 32])
            tt("dve", imp[:], imp[:], bc2(cmC.t[:, 2, i, :]), ALU.mult, [Ts, cmC.tok], [Ts])
            tt("dve", imp[:], imp[:], bc2(cmC.t[:, 3, i, :]), ALU.add, [Ts, cmC.tok], [Ts])
            for g in range(2):
                P.op("dve", lambda e, g=g: e.max(out=m8[:, g, 0:8], in_=imp[:, g, :]), [Ts], [Ts])
                P.op("dve", lambda e, g=g: e.match_replace(out=wk[:, g, :], in_to_replace=m8[:, g, 0:8], in_values=imp[:, g, :],
                                                           imm_value=-1e9), [Ts], [Ts])
                P.op("dve", lambda e, g=g: e.max(out=m8[:, g, 8:16], in_=wk[:, g, :]), [Ts], [Ts])
                ts("dve", t1[:, g, 0:32], imp[:, g, :], m8[:, g, 15:16], None, ALU.is_ge, None, [Ts], [Ts])
            ts("dve", t2[:], imp[:], -0.5, None, ALU.is_gt, None, [Ts], [Ts])
            tt("dve", t1[:, :, 0:32], t1[:, :, 0:32], t2[:], ALU.mult, [Ts], [Ts])
            ts("dve", t1[:, :, 0:32], t1[:, :, 0:32], -NEG, NEG, ALU.mult, ALU.add, [Ts], [Ts])
            ptm = PS[0]
            tr(ptm.t[:, 0:128], t1[:].rearrange("p g n -> p (g n)"), identf.t[:], [Ts, identf.tok], [ptm.tok])
            cp("dve", MT[:, t0:t0 + 128], ptm.t[:, 0:128], [ptm.tok], [TMT])
            for g in range(2):
                tr(ptm.t[:, 128 + g * 128:256 + g * 128], sc[:, g * 4:(g + 1) * 4, :].rearrange("p h n -> p (h n)"), identf.t[:],
                   [Ts, identf.tok], [ptm.tok])
            cp("dve", pT[:], ptm.t[:, 128:384].rearrange("p (g t) -> p g t", g=2), [ptm.tok], [Ts])
            psO = PS[1]
            for g in range(2):
                for pr in range(2):
                    mm(psO.t[:, (g * 2 + pr) * 128:(g * 2 + pr + 1) * 128], pT[pr * 64:(pr + 1) * 64, g, :],
                       Vbd2[pr * 64:(pr + 1) * 64, g, :], True, True, [Ts, Tc], [psO.tok])
            tt("dve", oacc[:], psO.t[:].rearrange("p (h d) -> p h d", d=64),
               A["gsig"].t[:, i, 0:8].unsqueeze(2).to_broadcast([128, 8, 64]), ALU.mult, [psO.tok, A["gsig"].tok], [To])
            if "oc" in TAP:
                P.dma("sp", "tap", lambda e, i=i: e.dma_start(out=TAP["oc"][:, i], in_=oacc[:]), [To], [out_tok])
            for g in range(2):
                if not cfg.get("nosel"):
                    dense_branch(i, g, t0, A["KT3"].t[:, 2, :], A["vsel"], PS[4 + g], list(range(i + 1)), True, 8)
            for g in range(2):
                if not cfg.get("nowin"):
                    dense_branch(i, g, t0, A["kwinT"].t, A["vwin"], PS[6 + g], list(range(max(0, i - 4), i + 1)), False, 16)
            if "o" in TAP:
                P.dma("sp", "tap", lambda e, i=i: e.dma_start(out=TAP["o"][:, i], in_=oacc[:]), [To], [out_tok])
            pto = PS[0]
            for c in range(4):
                tr(pto.t[:, c * 128:(c + 1) * 128], oacc[:].rearrange("p h d -> p (h d)")[:, c * 128:(c + 1) * 128], identf.t[:],
                   [To, identf.tok], [pto.tok])
            zv = A["zsT"].t[:, :, t0:t0 + 128]
            tt("dve", zv, pto.t[:].rearrange("p (c t) -> p c t", t=128), zv, ALU.mult, [pto.tok, A["zsT"].tok], [A["zsT"].tok])
        end_phase(m0)

    def attn_sample_phase(l, A):
        m0 = mem.mark()
        W1l = [Buf(mem.sb(f"W1s_{i}", [128, 64, 128], BF16), f"W1s_{i}") for i in range(2)]
        W2 = Buf(mem.sb("W2s", [128, 3, 128], BF16), "W2s")
        for i_ in range(2):
            load_w1(l, i_, W1l[i_])
        compress_setup(l, A, None, W2)
        selc = mem.sb("selc", [16, 2, 129], F32)
        smat = mem.sb("smat", [16, 16], F32)
        mnew = mem.sb("mnew", [16, 4, 16], F32)
        delt = mem.sb("delt", [16, 4, 4], F32)
        mwin = mem.sb("mwin", [16, 512], F32)
        iot2 = mem.sb("iot2", [128, 1], F32)
        Tcs = Tok("scst")
        for dst, src in ((selc, "s_sel"), (smat, "s_smat"), (mnew, "s_mnew"), (delt, "s_delta"), (mwin, "s_mwin"), (iot2, "iot2")):
            ld("sp", "c", dst[:], D[src], [Tcs])
        rowsK = Buf(mem.sb("rowsK", [128, 8256], BF16), "rowsK")
        rowsV = Buf(mem.sb("rowsV", [128, 8256], BF16), "rowsV")
        vselp = rowsV.t[:, 0:8192].rearrange("p (g t) -> p g t", t=128)
        pgb = [Buf(mem.sb(f"pgb{i}", [128, 256], F32), f"pgb{i}") for i in range(4)]
        pti = mem.sb("pti", [128, 64], I32)
        ptf = mem.sb("ptf", [128, 64], F32)
        idx = [mem.sb(f"idx{h}", [128, 64], I32) for h in range(2)]
        Ti = Tok("idx")
        cb = mem.sb("cbs", [128, 2], F32)
        silk = mem.sb("silks", [128, 128], BF16)
        silv = mem.sb("silvs", [128, 128], BF16)
        kcT = mem.sb("kcTs", [128, 128], BF16)
        vcS = mem.sb("vcS", [128, 128], BF16)
        kt1, kt2 = mem.sb("kt1s", [128, 128], F32), mem.sb("kt2s", [128, 128], F32)
        Tc = Tok("cmps")
        sc = mem.sb("scs", [16, 2, 128], F32)
        mx = mem.sb("mxs", [16, 2], F32)
        impx = mem.sb("impx", [16, 2, 129], F32)
        wk = mem.sb("wks", [16, 2, 129], F32)
        m8 = mem.sb("m8s", [16, 2, 16], F32)
        madd = mem.sb("madd", [16, 2, 129], F32)
        Ts = Tok("scs")
        sch = mem.sb("sch", [16, 2048], F32)
        snew = mem.sb("snew", [16, 16], F32)
        mxc = mem.sb("mxc", [16, 8], F32)
        gmx = mem.sb("gmx", [16, 1], F32)
        sums = mem.sb("sums", [16, 8], F32)
        PTb = mem.sb("PTb", [128, 16, 16], BF16)
        PTn = mem.sb("PTn", [16, 16], BF16)
        Tp = Tok("sch")
        swt = mem.sb("swt", [128, 4, 256], F32)
        kwT = mem.sb("kwTs", [128, 512], BF16)
        vwS = mem.sb("vwS", [128, 4, 128], BF16)
        swn = mem.sb("swn", [16, 512], F32)
        Tw = Tok("win")
        grep = mem.sb("grep", [16, 16], F32)
        gsb = mem.sb("gsb", [128, 16], F32)
        rsum = mem.sb("rsum", [128, 16], F32)
        oTs = mem.sb("oTs", [128, 2, 2, 4], F32)
        otmp = mem.sb("otmp", [128, 8], F32)
        To = Tok("oTs")
        ones_f = mem.sb("ones_f", [16, 128], F32)
        ms("dve", ones_f[:], 1.0, [Tcs])

        qsb = mem.sb("qsb", [128, 4, 16], BF16)
        cp("dve", qsb[:].rearrange("p b (j h) -> p b j h", h=4), A["qTs"].t[:].rearrange("p h (b j) -> p b j h", j=4),
           [A["qTs"].tok], [A["qTs"].tok])

        def qop(b, g):
            return qsb[g * 64:(g + 1) * 64, b, :]

        def finish_branch(b, g, branch, psO, psSum, first):
            gcol = branch * 8 + g * 4
            tt("dve", grep[:].rearrange("p (j h) -> p j h", h=4), delt[:, b, :].unsqueeze(2).to_broadcast([16, 4, 4]),
               A["gsig"].t[0:16, 16, gcol:gcol + 4].unsqueeze(1).to_broadcast([16, 4, 4]), ALU.mult, [Tcs, A["gsig"].tok], [To])
            pg_ = PS[7]
            mm(pg_.t[:, 0:16], ones_f[:], grep[:], True, True, [To, Tcs], [pg_.tok])
            P.op("dve", lambda e: e.reciprocal(out=rsum[:], in_=psSum.t[:, 0:16]), [psSum.tok], [To])
            tt("dve", gsb[:], rsum[:], pg_.t[:, 0:16], ALU.mult, [To, pg_.tok], [To])
            gv = gsb[:].rearrange("p (j r q) -> p j r q", r=2, q=2)
            for par in range(2):
                rows = slice(par * 64, par * 64 + 64)
                ov = psO.t[rows, 0:8].rearrange("p (j r) -> p r j", r=2)
                gvv = gv[rows, :, :, par].rearrange("p j r -> p r j")
                if first:
                    tt("dve", oTs[rows, g, :, :], ov, gvv, ALU.mult, [psO.tok, To], [To])
                else:
                    tt("dve", otmp[rows, :].rearrange("p (r j) -> p r j", j=4), ov, gvv, ALU.mult, [psO.tok, To], [To])
                    tt("dve", oTs[rows, g, :, :], oTs[rows, g, :, :], otmp[rows, :].rearrange("p (r j) -> p r j", j=4),
                       ALU.add, [To], [To])

        def pv_T(tiles, psO, psSum):
            n = len(tiles)
            for par in range(2):
                for k, (Vap, PTap) in enumerate(tiles):
                    rhs = PTap.rearrange("p (j r q) -> p j r q", r=2, q=2)[:, :, :, par]
                    mm(psO.t[par * 64:(par + 1) * 64, 0:8], Vap, rhs, k == 0, k == n - 1, [Tp, Tc, Tw, rowsV.tok, A["S_v"].tok], [psO.tok])
            for k, (Vap, PTap) in enumerate(tiles):
                K_ = list(PTap.shape)[0]
                mm(psSum.t[:, 0:16], onesb.t[0:K_, :], PTap, k == 0, k == n - 1, [Tp, Tc, Tw, onesb.tok], [psSum.tok])

        def transposes(src_ap_fn, ntile, kw, dstPT, Tsrc):
            ptp = PS[4]
            for t in range(ntile):
                tr(ptp.t[0:kw, t * 16:(t + 1) * 16], src_ap_fn(t), identf.t[0:16, 0:16], [Tsrc, identf.tok], [ptp.tok])
            cp("dve", dstPT[0:kw, 0:ntile, :], ptp.t[0:kw, 0:ntile * 16].rearrange("p (t c) -> p t c", c=16), [ptp.tok], [Tp])

        cache = D["cache2"]
        for b in range(4):
            P.dma("sp", "c", lambda e, b=b: e.dma_start(out=pti[:], in_=D["ptab"][b:b + 1, :].to_broadcast([128, 64])), [Ti], [Ti])
            cp("dve", ptf[:], pti[:], [Ti], [Ti])
            ts("dve", ptf[:], ptf[:], 1024.0, float(l * 256), ALU.mult, ALU.add, [Ti], [Ti])
            ts("dve", ptf[:], ptf[:], iot2[:, 0:1], None, ALU.add, None, [Ti, Tcs], [Ti])
            cp("dve", idx[0][:], ptf[:], [Ti], [Ti])
            ts("dve", ptf[:], ptf[:], 1.0, None, ALU.add, None, [Ti], [Ti])
            cp("dve", idx[1][:], ptf[:], [Ti], [Ti])
            for strm in range(2):
                ld("pool", "pe", (rowsK if strm == 0 else rowsV).t[:, 8192:8256], D["peT"][l, strm], [(rowsK if strm == 0 else rowsV).tok])
            for pp in range(32):
                ptp = PS[pp % 2]
                for k in range(2):
                    pg = pp * 2 + k
                    buf = pgb[pg % 4]
                    P.dma("pool", "g", lambda e, buf=buf, pg=pg: e.indirect_dma_start(
                        out=buf.t[:], out_offset=None, in_=cache[:, :],
                        in_offset=bass.IndirectOffsetOnAxis(ap=idx[0][:, pg:pg + 1], axis=0)), [Ti], [buf.tok])
                    for s_ in range(2):
                        tr(ptp.t[:, (k * 2 + s_) * 128:(k * 2 + s_ + 1) * 128], buf.t[:, s_ * 128:(s_ + 1) * 128], identf.t[:],
                           [buf.tok, identf.tok], [ptp.tok])
                pv4 = ptp.t[:].rearrange("p (k s t) -> p s k t", k=2, s=2)
                cp("dve", rowsK.t[:, pp * 256:(pp + 1) * 256].rearrange("p (k t) -> p k t", k=2), pv4[:, 0], [ptp.tok], [rowsK.tok])
                act(rowsV.t[:, pp * 256:(pp + 1) * 256].rearrange("p (k t) -> p k t", k=2), pv4[:, 1], AF.Copy, [ptp.tok], [rowsV.tok])
            chk("as1")
            for strm in range(2):
                psC = PS[2]
                rb = rowsK if strm == 0 else rowsV
                compress(l, strm, rb.t[:, 0:8256], 128, W1l[strm], psC, rb.tok)
                cp("dve", cb[:, strm:strm + 1], psC.t[:, 128:129], [psC.tok], [Tc])
                act((silk if strm == 0 else silv)[:], psC.t[:, 0:128], AF.Silu, [psC.tok, Tc], [Tc], bias=cb[:, strm:strm + 1])
            ps = PS[3]
            mm(ps.t[:, 0:128], W2.t[:, 0, :], silk[:], True, True, [W2.tok, Tc], [ps.tok])
            mm(ps.t[:, 128:256], W2.t[:, 2, :], silk[:], True, True, [W2.tok, Tc], [ps.tok])
            tt("dve", kt1[:], ps.t[:, 0:128], ropeC.t[:, 0, :], ALU.mult, [ps.tok, ropeC.tok], [Tc])
            tt("dve", kt2[:], ps.t[:, 128:256], ropeC.t[:, 1, :], ALU.mult, [ps.tok, ropeC.tok], [Tc])
            tt("dve", kcT[:], kt1[:], kt2[:], ALU.add, [Tc], [Tc])
            mm(ps.t[:, 256:384], silv[:], W2.t[:, 1, :], True, True, [W2.tok, Tc], [ps.tok])
            cp("dve", vcS[:], ps.t[:, 256:384], [ps.tok], [Tc])
            psS = PS[3]
            for g in range(2):
                mm(psS.t[0:16, g * 128:(g + 1) * 128], qop(b, g), kcT[g * 64:(g + 1) * 64, :], True, True, [A["qTs"].tok, Tc], [psS.tok])
            cp("dve", sc[:], psS.t[0:16, 0:256].rearrange("p (g n) -> p g n", g=2), [psS.tok], [Ts])
            P.op("dve", lambda e: e.tensor_reduce(out=mx[:], in_=sc[:], axis=AX.X, op=ALU.max), [Ts], [Ts])
            tt("dve", sc[:], sc[:], mx[:].unsqueeze(2).to_broadcast([16, 2, 128]), ALU.subtract, [Ts], [Ts])
            act(sc[:], sc[:], AF.Exp, [Ts], [Ts])
            P.op("dve", lambda e: e.tensor_reduce(out=mx[:], in_=sc[:], axis=AX.X, op=ALU.add), [Ts], [Ts])
            P.op("dve", lambda e: e.reciprocal(out=mx[:], in_=mx[:]), [Ts], [Ts])
            tt("dve", sc[:], sc[:], mx[:].unsqueeze(2).to_broadcast([16, 2, 128]), ALU.mult, [Ts], [Ts])
            psI = PS[2]
            mm(psI.t[0:16, 0:256], smat[:], sc[:].rearrange("p g n -> p (g n)"), True, True, [Ts, Tcs], [psI.tok])
            tt("dve", impx[:, :, 0:128], psI.t[0:16, 0:256].rearrange("p (g n) -> p g n", g=2),
               selc[:, 0, 0:128].unsqueeze(1).to_broadcast([16, 2, 128]), ALU.mult, [psI.tok, Tcs], [Ts])
            tt("dve", impx[:, :, 0:128], impx[:, :, 0:128], selc[:, 1, 0:128].unsqueeze(1).to_broadcast([16, 2, 128]), ALU.add, [Ts, Tcs], [Ts])
            cp("dve", impx[:, :, 128:129], selc[:, 1, 128:129].unsqueeze(1).to_broadcast([16, 2, 1]), [Tcs], [Ts])
            for g in range(2):
                P.op("dve", lambda e, g=g: e.max(out=m8[:, g, 0:8], in_=impx[:, g, :]), [Ts], [Ts])
                P.op("dve", lambda e, g=g: e.match_replace(out=wk[:, g, :], in_to_replace=m8[:, g, 0:8], in_values=impx[:, g, :],
                                                           imm_value=-1e9), [Ts], [Ts])
                P.op("dve", lambda e, g=g: e.max(out=m8[:, g, 8:16], in_=wk[:, g, :]), [Ts], [Ts])
                ts("dve", madd[:, g, :], impx[:, g, :], m8[:, g, 15:16], None, ALU.is_ge, None, [Ts], [Ts])
            ts("dve", madd[:], madd[:], -NEG, NEG, ALU.mult, ALU.add, [Ts], [Ts])
            for g in range(2):
                transposes(lambda t, g=g: sc[:, g, :], 1, 128, PTb, Ts)
                pv_T([(vcS[:, g * 64:(g + 1) * 64], PTb[:, 0, :])], PS[5], PS[6])
                finish_branch(b, g, 0, PS[5], PS[6], True)
            chk("as2")
            for pq in range(16):
                ptp = PS[pq % 2]
                for k in range(4):
                    pg = pq * 4 + k
                    buf = pgb[pg % 4]
                    P.dma("pool", "g", lambda e, buf=buf, pg=pg: e.indirect_dma_start(
                        out=buf.t[:], out_offset=None, in_=cache[:, :],
                        in_offset=bass.IndirectOffsetOnAxis(ap=idx[1][:, pg:pg + 1], axis=0)), [Ti], [buf.tok])
                    tr(ptp.t[:, k * 128:(k + 1) * 128], buf.t[:, 0:128], identf.t[:], [buf.tok, identf.tok], [ptp.tok])
                    act(vselp[:, pg, :], buf.t[:, 128:256], AF.Copy, [buf.tok], [rowsV.tok])
                cp("dve", rowsK.t[:, pq * 512:(pq + 1) * 512], ptp.t[:], [ptp.tok], [rowsK.tok])
            chk("as3")
            P.dma("sp", "w", lambda e, b=b: e.dma_start(out=swt[:], in_=D["swin"][b, l].rearrange("(t p) c -> p t c", p=128)), [Tw], [Tw])
            ptw = PS[2]
            for t in range(4):
                tr(ptw.t[:, t * 128:(t + 1) * 128], swt[:, t, 0:128], identf.t[:], [Tw, identf.tok], [ptw.tok])
            cp("dve", kwT[:], ptw.t[:], [ptw.tok], [Tw])
            cp("dve", vwS[:], swt[:, :, 128:256], [Tw], [Tw])
            for g in range(2):
                def sel_chunk(c):
                    for k in range(4):
                        pk = PS[k]
                        mm(pk.t[0:16, :], qop(b, g), rowsK.t[g * 64:(g + 1) * 64, c * 2048 + k * 512:c * 2048 + (k + 1) * 512],
                           True, True, [A["qTs"].tok, rowsK.tok], [pk.tok])
                        tt("dve", sch[:, k * 512:(k + 1) * 512].rearrange("p (n e) -> p n e", e=64),
                           pk.t[0:16, :].rearrange("p (n e) -> p n e", e=64),
                           madd[:, g, c * 32 + k * 8:c * 32 + k * 8 + 8].unsqueeze(2).to_broadcast([16, 8, 64]), ALU.add,
                           [pk.tok, Ts], [Tp])
                pn = PS[7]
                mm(pn.t[0:16, 0:16], qop(b, g), A["S_kT"].t[g * 64:(g + 1) * 64, 0, :], True, True, [A["qTs"].tok, A["S_kT"].tok], [pn.tok])
                tt("dve", snew[:], pn.t[0:16, 0:16], mnew[:, b, :], ALU.add, [pn.tok, Tcs], [Tp])
                ts("dve", snew[:], snew[:], madd[:, g, 128:129], None, ALU.add, None, [Tp, Ts], [Tp])
                P.op("dve", lambda e: e.tensor_reduce(out=mxc[:, 4:5], in_=snew[:], axis=AX.X, op=ALU.max), [Tp], [Tp])
                for c in range(4):
                    sel_chunk(c)
                    P.op("dve", lambda e, c=c: e.tensor_reduce(out=mxc[:, c:c + 1], in_=sch[:], axis=AX.X, op=ALU.max), [Tp], [Tp])
                P.op("dve", lambda e: e.tensor_reduce(out=gmx[:], in_=mxc[:, 0:5], axis=AX.X, op=ALU.max), [Tp], [Tp])
                ts("dve", gmx[:], gmx[:], -1.0, None, ALU.mult, None, [Tp], [Tp])
                tiles = []
                psO, psSum = PS[5], PS[6]
                first = True
                for c in range(4):
                    sel_chunk(c)
                    act(sch[:], sch[:], AF.Exp, [Tp], [Tp], bias=gmx[:, 0:1])
                    transposes(lambda t: sch[:, t * 128:(t + 1) * 128], 16, 128, PTb, Tp)
                    for par in range(2):
                        for t in range(16):
                            rhs = PTb[:, t, :].rearrange("p (j r q) -> p j r q", r=2, q=2)[:, :, :, par]
                            mm(psO.t[par * 64:(par + 1) * 64, 8 * c:8 * c + 8], vselp[:, c * 16 + t, g * 64:(g + 1) * 64], rhs,
                               t == 0, t == 15, [Tp, rowsV.tok], [psO.tok])
                    for t in range(16):
                        mm(psSum.t[:, 16 * c:16 * c + 16], onesb.t[:], PTb[:, t, :], t == 0, t == 15, [Tp, onesb.tok], [psSum.tok])
                act(snew[:], snew[:], AF.Exp, [Tp], [Tp], bias=gmx[:, 0:1])
                ptn = PS[4]
                tr(ptn.t[0:16, 0:16], snew[:], identf.t[0:16, 0:16], [Tp, identf.tok], [ptn.tok])
                cp("dve", PTn[:], ptn.t[0:16, 0:16], [ptn.tok], [Tp])
                for par in range(2):
                    rhs = PTn[:].rearrange("p (j r q) -> p j r q", r=2, q=2)[:, :, :, par]
                    mm(psO.t[par * 64:(par + 1) * 64, 32:40], A["S_v"].t[:, 0, g, :], rhs, True, True, [Tp, A["S_v"].tok], [psO.tok])
                mm(psSum.t[:, 64:80], onesb.t[0:16, :], PTn[:], True, True, [Tp, onesb.tok], [psSum.tok])
                cp("dve", otmp[:, 0:8], psO.t[:, 0:8], [psO.tok], [To])
                cp("dve", rsum[:], psSum.t[:, 0:16], [psSum.tok], [To])
                for k in range(1, 5):
                    tt("dve", otmp[:, 0:8], psO.t[:, 8 * k:8 * k + 8], otmp[:, 0:8], ALU.add, [psO.tok, To], [To])
                    tt("dve", rsum[:], psSum.t[:, 16 * k:16 * k + 16], rsum[:], ALU.add, [psSum.tok, To], [To])
                finish_from_sbuf(b, g, 1, otmp, rsum, finish_branch, grep, delt, gsb, oTs, To, Tcs, A, ones_f)
                pw_ = PS[0]
                mm(pw_.t[0:16, :], qop(b, g), kwT[g * 64:(g + 1) * 64, :], True, True, [A["qTs"].tok, Tw], [pw_.tok])
                tt("dve", swn[:], pw_.t[0:16, :], mwin[:], ALU.add, [pw_.tok, Tcs], [Tp])
                pn = PS[7]
                mm(pn.t[0:16, 0:16], qop(b, g), A["S_kT"].t[g * 64:(g + 1) * 64, 1, :], True, True, [A["qTs"].tok, A["S_kT"].tok], [pn.tok])
                tt("dve", snew[:], pn.t[0:16, 0:16], mnew[:, b, :], ALU.add, [pn.tok, Tcs], [Tp])
                P.op("dve", lambda e: e.tensor_reduce(out=mxc[:, 0:1], in_=swn[:], axis=AX.X, op=ALU.max), [Tp], [Tp])
                P.op("dve", lambda e: e.tensor_reduce(out=mxc[:, 1:2], in_=snew[:], axis=AX.X, op=ALU.max), [Tp], [Tp])
                P.op("dve", lambda e: e.tensor_reduce(out=gmx[:], in_=mxc[:, 0:2], axis=AX.X, op=ALU.max), [Tp], [Tp])
                ts("dve", gmx[:], gmx[:], -1.0, None, ALU.mult, None, [Tp], [Tp])
                act(swn[:], swn[:], AF.Exp, [Tp], [Tp], bias=gmx[:, 0:1])
                act(snew[:], snew[:], AF.Exp, [Tp], [Tp], bias=gmx[:, 0:1])
                transposes(lambda t: swn[:, t * 128:(t + 1) * 128], 4, 128, PTb, Tp)
                ptn = PS[4]
                tr(ptn.t[0:16, 256:272], snew[:], identf.t[0:16, 0:16], [Tp, identf.tok], [ptn.tok])
                cp("dve", PTn[:], ptn.t[0:16, 256:272], [ptn.tok], [Tp])
                tiles = [(vwS[:, t, g * 64:(g + 1) * 64], PTb[:, t, :]) for t in range(4)] + [(A["S_v"].t[:, 1, g, :], PTn[:])]
                pv_T(tiles, PS[5], PS[6])
                finish_branch(b, g, 2, PS[5], PS[6], False)
            zv = A["zsT"].t[:, :, TP + 4 * b:TP + 4 * b + 4]
            tt("dve", zv, zv, oTs[:].rearrange("p g r j -> p (g r) j"), ALU.mult, [To, A["zsT"].tok], [A["zsT"].tok])
        end_phase(m0)

    def finish_from_sbuf(b, g, branch, osb, ssb, finish_branch, grep, delt, gsb, oTs, To, Tcs, A, ones_f):
        gcol = branch * 8 + g * 4
        tt("dve", grep[:].rearrange("p (j h) -> p j h", h=4), delt[:, b, :].unsqueeze(2).to_broadcast([16, 4, 4]),
           A["gsig"].t[0:16, 16, gcol:gcol + 4].unsqueeze(1).to_broadcast([16, 4, 4]), ALU.mult, [Tcs, A["gsig"].tok], [To])
        pg_ = PS[7]
        mm(pg_.t[:, 0:16], ones_f[:], grep[:], True, True, [To, Tcs], [pg_.tok])
        P.op("dve", lambda e: e.reciprocal(out=ssb[:], in_=ssb[:]), [To], [To])
        tt("dve", gsb[:], ssb[:], pg_.t[:, 0:16], ALU.mult, [To, pg_.tok], [To])
        gv = gsb[:].rearrange("p (j r q) -> p j r q", r=2, q=2)
        for par in range(2):
            rows = slice(par * 64, par * 64 + 64)
            ov = osb[rows, 0:8].rearrange("p (j r) -> p r j", r=2)
            gvv = gv[rows, :, :, par].rearrange("p j r -> p r j")
            tt("dve", osb[rows, 0:8].rearrange("p (j r) -> p r j", r=2), ov, gvv, ALU.mult, [To], [To])
            tt("dve", oTs[rows, g, :, :], oTs[rows, g, :, :], osb[rows, 0:8].rearrange("p (j r) -> p r j", r=2), ALU.add, [To], [To])

    def merge_phase(l, hT, y5gT, onsaT):
        m0 = mem.mark()
        mg = Buf(mem.sb("mg", [128, 8, NT], BF16), "mg", 5)
        wv = D["w_in"][l].rearrange("(kc p) n -> p kc n", p=128)
        w5v = D["w_s5_out"][l].rearrange("(kc p) n -> p kc n", p=128)
        wnv = D["w_nsa_out"][l].rearrange("(kc p) n -> p kc n", p=128)
        wov = D["w_o"][l].rearrange("(kc p) n -> p kc n", p=128)
        wms = Buf(mem.sb("wms", [128, 8, 256], BF16), "wms")
        wmn = Buf(mem.sb("wmn", [128, 8, 256], BF16), "wmn")
        w5 = Buf(mem.sb("w5", [128, 4, 256], BF16), "w5")
        wn = Buf(mem.sb("wn", [128, 4, 256], BF16), "wn")
        wo = Buf(mem.sb("wo", [128, 8, 256], BF16), "wo")
        sg1 = Buf(mem.sb("sg1", [128, 512], F32), "sg1")
        sg2 = Buf(mem.sb("sg2", [128, 512], F32), "sg2")
        for grp in range(4):
            c0g = grp * 256
            for kc in range(8):
                ld("pool", "w", wms.t[:, kc, :], wv[:, kc, MG_OFF + c0g:MG_OFF + c0g + 256], [wms.tok])
                ld("pool", "w", wmn.t[:, kc, :], wv[:, kc, MG_OFF + 1024 + c0g:MG_OFF + 1024 + c0g + 256], [wmn.tok])
            for kc in range(4):
                ld("pool", "w", w5.t[:, kc, :], w5v[:, kc, c0g:c0g + 256], [w5.tok])
                ld("pool", "w", wn.t[:, kc, :], wnv[:, kc, c0g:c0g + 256], [wn.tok])
            for o2 in range(2):
                oc = grp * 2 + o2
                cs = slice(o2 * 128, (o2 + 1) * 128)
                for b_, (c0, c1) in enumerate(BANKS):
                    w = c1 - c0
                    p1, p2, p3, p4 = PS[0], PS[1], PS[2], PS[3]
                    for kc in range(8):
                        mm(p1.t[:, 0:w], wms.t[:, kc, cs], hT.t[:, kc, c0:c1], kc == 0, kc == 7, [wms.tok, hT.toks[b_]], [p1.tok])
                    for kc in range(8):
                        mm(p2.t[:, 0:w], wmn.t[:, kc, cs], hT.t[:, kc, c0:c1], kc == 0, kc == 7, [wmn.tok, hT.toks[b_]], [p2.tok])
                    for kc in range(4):
                        mm(p3.t[:, 0:w], w5.t[:, kc, cs], y5gT.t[:, kc, c0:c1], kc == 0, kc == 3, [w5.tok, y5gT.tok], [p3.tok])
                    for kc in range(4):
                        mm(p4.t[:, 0:w], wn.t[:, kc, cs], onsaT.t[:, kc, c0:c1], kc == 0, kc == 3, [wn.tok, onsaT.tok], [p4.tok])
                    act(sg1.t[:, 0:w], p1.t[:, 0:w], AF.Sigmoid, [p1.tok], [sg1.tok])
                    act(sg2.t[:, 0:w], p2.t[:, 0:w], AF.Sigmoid, [p2.tok], [sg2.tok])
                    tt("dve", sg1.t[:, 0:w], sg1.t[:, 0:w], p3.t[:, 0:w], ALU.mult, [sg1.tok, p3.tok], [sg1.tok])
                    tt("dve", sg2.t[:, 0:w], sg2.t[:, 0:w], p4.t[:, 0:w], ALU.mult, [sg2.tok, p4.tok], [sg2.tok])
                    tt("dve", mg.t[:, oc, c0:c1], sg1.t[:, 0:w], sg2.t[:, 0:w], ALU.add, [sg1.tok, sg2.tok], [mg.toks[b_]])
        for grp in range(4):
            c0g = grp * 256
            for kc in range(8):
                ld("pool", "w", wo.t[:, kc, :], wov[:, kc, c0g:c0g + 256], [wo.tok])
            for o2 in range(2):
                oc = grp * 2 + o2
                cs = slice(o2 * 128, (o2 + 1) * 128)
                for b_, (c0, c1) in enumerate(BANKS):
                    w = c1 - c0
                    ps = PS[4 + (b_ % 2)]
                    for kc in range(8):
                        mm(ps.t[:, 0:w], wo.t[:, kc, cs], mg.t[:, kc, c0:c1], kc == 0, kc == 7, [wo.tok, mg.toks[b_]], [ps.tok])
                    if b_ < 4:
                        stt(xT.t[:, oc, c0:c1], ps.t[:, 0:w], modT.t[:, 16 + oc, 0:1], xT.t[:, oc, c0:c1], ALU.mult, ALU.add,
                            [ps.tok, modT.tok, xT.toks[b_]], [xT.toks[b_]])
                    else:
                        tt("dve", sg1.t[:, 0:w], ps.t[:, 0:w], gtS.t[:, oc, :], ALU.mult, [ps.tok, gtS.tok], [sg1.tok])
                        tt("dve", xT.t[:, oc, c0:c1], xT.t[:, oc, c0:c1], sg1.t[:, 0:w], ALU.add, [sg1.tok, xT.toks[b_]], [xT.toks[b_]])
        end_phase(m0)

    def layer(l):
        if stop_after == "pro":
            return
        adaln_phase(l)
        tap("modT", modT.t[:], [modT.tok])
        if stop_after == "ada":
            return
        mL = mem.mark()
        uT = Buf(mem.sb("uT", [128, 4, NT], BF16), "uT")
        zs5T = Buf(mem.sb("zs5T", [128, 4, NT], BF16), "zs5T")
        m1 = mem.mark()
        hT = Buf(mem.sb("hT", [128, 8, NT], BF16), "hT", 5)
        norm_phase(hT)
        tap("hT", hT.t[:], hT.toks)
        if stop_after == "norm":
            return
        uz_proj_phase(l, hT, uT, zs5T)
        end_phase(m1)
        tap("uT", uT.t[:], [uT.tok])
        if stop_after == "uz":
            return
        s5_phase(l, uT, zs5T)
        if stop_after == "s5":
            return
        glu_phase(l, uT, zs5T)
        tap("y5g", uT.t[:], [uT.tok])
        if stop_after == "glu":
            raise _Stop()
        A = {}
        mA = mem.mark()
        A["qTs"] = Buf(mem.sb("qTs", [128, 4, 16], BF16), "qTs")
        A["S_kT"] = Buf(mem.sb("S_kT", [128, 2, 16], BF16), "S_kT")
        A["S_v"] = Buf(mem.sb("S_v", [16, 2, 2, 64], BF16), "S_v")
        A["gsig"] = Buf(mem.sb("gsig", [128, 17, 24], F32), "gsig")
        mB = mem.mark()
        A["qT"] = Buf(mem.sb("qT", [128, 4, TP], BF16), "qT")
        A["KT3"] = Buf(mem.sb("KT3", [128, 3, 2112], BF16), "KT3")
        A["kwinT"] = Buf(mem.sb("kwinT", [128, NT], BF16), "kwinT")
        A["vsel"] = Buf(mem.sb("vsel", [128, 17, 2, 65], BF16), "vsel")
        A["vwin"] = Buf(mem.sb("vwin", [128, 17, 2, 65], BF16), "vwin")
        A["zsT"] = zs5T
        ms("pool", A["vsel"].t[:], 1.0, [A["vsel"].tok])
        ms("pool", A["vwin"].t[:], 1.0, [A["vwin"].tok])
        m2 = mem.mark()
        hT = Buf(mem.sb("hT", [128, 8, NT], BF16), "hT", 5)
        norm_phase(hT)
        tm_proj_phase(l, hT, A)
        end_phase(m2)
        for n_ in ("qT", "KT3", "kwinT", "zsT", "gsig", "vsel", "vwin", "qTs"):
            tap(n_, A[n_].t[:], [A[n_].tok])
        if stop_after == "tm":
            raise _Stop()
        attn_prompt_phase(l, A)
        tap("onsaT", A["zsT"].t[:], [A["zsT"].tok])
        if stop_after == "ap":
            raise _Stop()
        end_phase(mB)
        attn_sample_phase(l, A)
        tap("onsaTs", A["zsT"].t[:, :, TP:NT], [A["zsT"].tok])
        if stop_after == "as":
            raise _Stop()
        end_phase(mA)
        m3 = mem.mark()
        hT = Buf(mem.sb("hT", [128, 8, NT], BF16), "hT", 5)
        norm_phase(hT)
        merge_phase(l, hT, uT, A["zsT"])
        end_phase(m3)
        tap(f"x{l}", xT.t[:], xT.toks)
        end_phase(mL)

    def final_phase():
        m0 = mem.mark()
        sq = Buf(mem.sb("sq", [128, 8, 256], BF16), "sq")
        xn = Buf(mem.sb("xn", [128, 8, 256], F32), "xn")
        rt = Buf(mem.sb("rt", [128, 256], F32), "rt")
        yo = [Buf(mem.sb(f"yo{i}", [128, 8, 256], F32), f"yo{i}") for i in range(2)]
        fg = Buf(mem.sb("fg", [128, 8], F32), "fg")
        ld("sp", "c", fg.t[:], D["final_gT"], [fg.tok])
        chunks = [(i * 256, i * 256 + 256) for i in range(8)] + [(TP, NT)]
        for ci, (c0, c1) in enumerate(chunks):
            w = c1 - c0
            b_ = min(c0 // 512, 4)
            ps = PS[1 + (ci % 2)]
            y_ = yo[ci % 2]
            act(sq.t[:, :, 0:w], xT.t[:, :, c0:c1], AF.Square, [xT.toks[b_]], [sq.tok])
            for kc in range(8):
                mm(ps.t[:, 0:w], onesb.t[:], sq.t[:, kc, 0:w], kc == 0, kc == 7, [onesb.tok, sq.tok], [ps.tok])
            act(rt.t[:, 0:w], ps.t[:, 0:w], AF.Sqrt, [ps.tok], [rt.tok], bias=1e-6, scale=1.0 / 1024.0)
            P.op("dve", lambda e, w=w: e.reciprocal(out=rt.t[:, 0:w], in_=rt.t[:, 0:w]), [rt.tok], [rt.tok])
            tt("dve", xn.t[:, :, 0:w], xT.t[:, :, c0:c1], rt.t[:, 0:w].unsqueeze(1).to_broadcast([128, 8, w]),
               ALU.mult, [xT.toks[b_], rt.tok], [xn.tok])
            tt("dve", y_.t[:, :, 0:w], xn.t[:, :, 0:w], fg.t[:].unsqueeze(2).to_broadcast([128, 8, w]), ALU.mult,
               [xn.tok, fg.tok], [y_.tok])
            P.dma("sp", "y", lambda e, y_=y_, c0=c0, c1=c1, w=w: e.dma_start(out=O["yT"][:, :, c0:c1], in_=y_.t[:, :, 0:w]),
                  [y_.tok], [out_tok])
        end_phase(m0)

    try:
        for l in range(nlayers):
            layer(l)
        final_phase()
    except _Stop:
        pass

    print('OPCOUNTS', P.cnt, P.dcnt)
    P.barrier()
    P.emit()
    P.close()
    mem.release(0)
    return nc


_NC_CACHE = {}


def _shared_inputs(inp):
    m = {}
    m["ident"] = np.eye(128, dtype=np.float32)
    m["ada_w"] = np.ascontiguousarray(inp["ada_w"], np.float32)
    m["ada_bT"] = np.ascontiguousarray(inp["ada_b"].reshape(DEPTH, 24, 128).transpose(0, 2, 1))
    m["norm_gT"] = np.ascontiguousarray(inp["norm_g"].reshape(DEPTH, 8, 128).transpose(0, 2, 1))
    m["final_gT"] = np.ascontiguousarray(inp["final_g"].reshape(8, 128).T)
    m["w_in"] = np.ascontiguousarray(inp["w_in"], np.float32)
    aNL, aPR, BPR, BNL, CNL, dv = _s5_layouts(inp)
    m.update(aNL=aNL, aPR=aPR, BPR=BPR, BNL=BNL, CNL=CNL, dv=dv)
    m["glu_w"] = np.ascontiguousarray(inp["s5_glu_w"], np.float32)
    m["glu_bT"] = np.ascontiguousarray(inp["s5_glu_b"].reshape(DEPTH, 4, 128).transpose(0, 2, 1))
    m["w_s5_out"] = np.ascontiguousarray(inp["w_s5_out"], np.float32)
    m["w_nsa_out"] = np.ascontiguousarray(inp["w_nsa_out"], np.float32)
    m["w_o"] = np.ascontiguousarray(inp["w_o"], np.float32)
    cos, sin, cosC, sinC = _rope_tables()
    m["cosT"] = cos
    m["sinT"] = sin
    m["ropeC"] = np.ascontiguousarray(np.stack([cosC, sinC], 1))
    cm, E, tri = _attn_consts()
    m["cm"] = cm
    m["Ecst"] = E
    m["tri"] = tri
    W1bd, W2bd, peT = _cmp_layouts(inp)
    m["W1bd"] = W1bd
    m["W2bd"] = W2bd
    m["peT"] = peT
    npool = inp["cache_kv"].shape[0]
    m["cache2"] = np.ascontiguousarray(inp["cache_kv"], np.float32).reshape(npool * DEPTH * 128 * 2, 256)
    m["iot2"] = (2 * np.arange(128, dtype=np.float32)).reshape(128, 1)
    sel, Smat, masknew, delta, maskwin = _sample_consts()
    m.update(s_sel=sel, s_smat=Smat, s_mnew=masknew, s_delta=delta, s_mwin=maskwin)
    return m


def _core_inputs(inp, c, shared):
    m = dict(shared)
    xp = inp["x_prompt"][c]
    xs = inp["x_sample"][4 * c:4 * c + 4].reshape(16, 1024)
    x = np.concatenate([xp, xs], 0)
    m["xT0"] = np.ascontiguousarray(x.T.reshape(8, 128, NT).transpose(1, 0, 2))
    cc = np.concatenate([inp["c_prompt"][c:c + 1], inp["c_sample"][4 * c:4 * c + 4]], 0)
    m["cT"] = np.ascontiguousarray(cc.T.reshape(8, 128, 5).transpose(1, 0, 2))
    m["h0NL"] = _h0_layout(inp["state_ssm"][4 * c:4 * c + 4])
    m["swin"] = np.ascontiguousarray(inp["state_win"][4 * c:4 * c + 4].reshape(4, DEPTH, 512, 256))
    m["ptab"] = np.ascontiguousarray(inp["page_table"][4 * c:4 * c + 4].astype(np.int32))
    return m


def _assemble(results, ncores):
    B, DB = ncores, 4 * ncores
    y_p = np.zeros((B, TP, 1024), np.float32)
    y_s = np.zeros((DB, 4, 1024), np.float32)
    kv_p = np.zeros((B, DEPTH, TP, 4, 2, 64), np.float32)
    kv_s = np.zeros((DB, DEPTH, 4, 4, 2, 64), np.float32)
    win_p = np.zeros((B, DEPTH, 512, 2, 2, 64), np.float32)
    win_s = np.zeros((DB, DEPTH, 512, 2, 2, 64), np.float32)
    ssm_p = np.zeros((B, DEPTH, 2, 32, 64), np.float32)
    ssm_s = np.zeros((DB, DEPTH, 2, 32, 64), np.float32)
    for c, r in enumerate(results):
        yT = np.asarray(r["yT"])
        y = yT.transpose(2, 1, 0).reshape(NT, 1024)
        y_p[c] = y[:TP]
        y_s[4 * c:4 * c + 4] = y[TP:].reshape(4, 4, 1024)
        kv_p[c] = np.asarray(r["kvP"]).reshape(DEPTH, TP, 4, 2, 64)
        kv_s[4 * c:4 * c + 4] = np.asarray(r["kvS"]).reshape(DEPTH, 4, 4, 4, 2, 64).transpose(1, 0, 2, 3, 4, 5)
        win_p[c] = np.asarray(r["winP"]).reshape(DEPTH, 512, 2, 2, 64)
        win_s[4 * c:4 * c + 4] = np.asarray(r["winS"]).reshape(4, DEPTH, 512, 2, 2, 64)
        hlP = np.asarray(r["hlP"])
        hlS = np.asarray(r["hlS"])
        for q in range(4):
            for pb in range(4):
                for gl in range(2):
                    g = 8 * q + 2 * pb + gl
                    ssm_p[c, :, :, g, :] = hlP[:, gl * 64:gl * 64 + 64, :, q * 4 + pb].transpose(0, 2, 1)
                    ssm_s[4 * c:4 * c + 4, :, :, g, :] = hlS[:, gl * 64:gl * 64 + 64, :, q * 4 + pb, :].transpose(3, 0, 2, 1)
    return (y_p, y_s, kv_p, kv_s, win_p, win_s, ssm_p, ssm_s)


def kernel(**inputs):
    inp = {k: np.asarray(v) for k, v in inputs.items()}
    ncores = inp["x_prompt"].shape[0]
    npool = int(inp["cache_kv"].shape[0])
    key = (npool,)
    if key not in _NC_CACHE:
        _NC_CACHE[key] = build({"npool": npool})
    nc = _NC_CACHE[key]
    shared = _shared_inputs(inp)
    in_maps = [_core_inputs(inp, c, shared) for c in range(ncores)]
    res = run_bass_kernel_spmd(nc, in_maps, core_ids=list(range(ncores)))
    return _assemble(res.results, ncores)
```

```python
import numpy as np
import concourse.bass as bass
import concourse.mybir as mybir
from concourse.bass_utils import run_bass_kernel_spmd

F32 = mybir.dt.float32
BF16 = mybir.dt.bfloat16
I32 = mybir.dt.int32
AF = mybir.ActivationFunctionType
ALU = mybir.AluOpType
AX = mybir.AxisListType

NT = 2064
TP = 2048
DEPTH = 4
LCH = 16
NCH = TP // LCH
NEG = -30000.0
TWO_PI = float(2 * np.pi)


class Tok:
    __slots__ = ("name", "w", "r", "excl")

    def __init__(self, name="", excl=False):
        self.name = name
        self.w = None
        self.r = {}
        self.excl = excl


class Prog:
    ENG = ("pe", "act", "dve", "pool", "sp")

    def __init__(self, nc):
        self.nc = nc
        self.ops = {e: [] for e in self.ENG}
        self.cnt = {e: 0 for e in self.ENG}
        self.seen = {e: {} for e in self.ENG}
        self.sems = {}
        self.dcnt = {}
        self._stack = []
        self.nops = 0

    def sem(self, name):
        if name not in self.sems:
            cm = self.nc.semaphore(name)
            h = cm.__enter__()
            self._stack.append(cm)
            self.sems[name] = h
        return self.sems[name]

    def _wait(self, eng, ev):
        sname, val = ev
        if self.seen[eng].get(sname, 0) >= val:
            return
        self.seen[eng][sname] = val
        h = self.sem(sname)
        self.ops[eng].append(lambda e, h=h, val=val: e.wait_ge(h, val))

    def _deps(self, eng, reads, writes, pe_accum=False):
        deps = {}

        def add(ev):
            if ev is None:
                return
            s, v = ev
            if deps.get(s, 0) < v:
                deps[s] = v
        for t in reads:
            add(t.w)
            if t.excl:
                for s_, v_ in t.r.items():
                    if s_ != "e_" + eng:
                        add((s_, v_))
        for t in writes:
            if not (pe_accum and t.w is not None and t.w[0] == "e_pe"):
                add(t.w)
            for s, v in t.r.items():
                if pe_accum and s == "e_pe":
                    continue
                add((s, v))
        for s, v in deps.items():
            self._wait(eng, (s, v))

    def _mark(self, ev, reads, writes):
        s, v = ev
        for t in reads:
            if t.r.get(s, 0) < v:
                t.r[s] = v
        for t in writes:
            t.w = ev
            t.r = {}

    def op(self, eng, fn, reads=(), writes=(), pe_accum=False):
        self._deps(eng, reads, writes, pe_accum)
        sname = "e_" + eng
        h = self.sem(sname)
        self.cnt[eng] += 1
        ev = (sname, self.cnt[eng])
        self.ops[eng].append(lambda e, fn=fn, h=h: fn(e).then_inc(h, 1))
        self._mark(ev, reads, writes)
        self.nops += 1
        return ev

    NDSEM = 40

    def dma(self, q, key, fn, reads=(), writes=()):
        self._deps(q, reads, writes)
        i = getattr(self, "_dnext", 0)
        self._dnext = (i + 1) % self.NDSEM
        sname = f"d{i}"
        prev = self.dcnt.get(sname, 0)
        if prev:
            self._wait(q, (sname, prev))
        h = self.sem(sname)
        self.dcnt[sname] = prev + 16
        ev = (sname, self.dcnt[sname])
        self.ops[q].append(lambda e, fn=fn, h=h: fn(e).then_inc(h, 16))
        self._mark(ev, reads, writes)
        self.nops += 1
        return ev

    def wait_all(self, eng):
        for e in self.ENG:
            if self.cnt[e]:
                self._wait(eng, ("e_" + e, self.cnt[e]))
        for s, v in self.dcnt.items():
            self._wait(eng, (s, v))

    def barrier(self):
        for e in self.ENG:
            self.wait_all(e)

    def emit(self):
        nc = self.nc
        with nc.Block() as block:
            @block.tensor
            def _(e):
                for f in self.ops["pe"]:
                    f(e)

            @block.scalar
            def _(e):
                for f in self.ops["act"]:
                    f(e)

            @block.vector
            def _(e):
                for f in self.ops["dve"]:
                    f(e)

            @block.gpsimd
            def _(e):
                for f in self.ops["pool"]:
                    f(e)

            @block.sync
            def _(e):
                for f in self.ops["sp"]:
                    f(e)

    def close(self):
        while self._stack:
            self._stack.pop().__exit__(None, None, None)


class Mem:
    def __init__(self, nc):
        self.nc = nc
        self.stack = []
        self.n = 0

    def sb(self, name, shape, dt):
        self.n += 1
        cm = self.nc.sbuf_tensor(f"s{self.n}_{name}", list(shape), dt)
        t = cm.__enter__()
        self.stack.append(cm)
        return t

    def ps(self, name, shape, dt=F32):
        self.n += 1
        cm = self.nc.psum_tensor(f"p{self.n}_{name}", list(shape), dt)
        t = cm.__enter__()
        self.stack.append(cm)
        return t

    def mark(self):
        return len(self.stack)

    def release(self, mark):
        while len(self.stack) > mark:
            self.stack.pop().__exit__(None, None, None)


def _s5_layouts(inp):
    L = DEPTH
    a_re, a_im, ldt = inp["s5_a_re"], inp["s5_a_im"], inp["s5_log_dt"]
    b = np.stack([inp["s5_b_re"], inp["s5_b_im"]], 1)
    c = np.stack([inp["s5_c_re"], inp["s5_c_im"]], 1)
    aNL = np.zeros((L, 128, 3, 16), np.float32)
    aPR = np.zeros((L, 128, 4, 3, 128), np.float32)
    BPR = np.zeros((L, 128, 4, 2, 128), np.float32)
    BNL = np.zeros((L, 128, 4, 2, 128), np.float32)
    CNL = np.zeros((L, 128, 4, 2, 4, 32), np.float32)
    dv = np.zeros((L, 128, 4), np.float32)
    for q in range(4):
        for pb in range(4):
            for gl in range(2):
                g = 8 * q + 2 * pb + gl
                sl = slice(gl * 64, gl * 64 + 64)
                aNL[:, sl, 0, q * 4 + pb] = a_re[:, g]
                aNL[:, sl, 1, q * 4 + pb] = a_im[:, g]
                aNL[:, sl, 2, q * 4 + pb] = ldt[:, g][:, None]
                rows = slice(pb * 32, pb * 32 + 32)
                aPR[:, rows, q, 0, sl] = a_re[:, g][:, None, :]
                aPR[:, rows, q, 1, sl] = a_im[:, g][:, None, :]
                aPR[:, rows, q, 2, sl] = ldt[:, g][:, None, None]
                r16 = slice(pb * 32 + gl * 16, pb * 32 + gl * 16 + 16)
                BPR[:, r16, q, :, sl] = np.transpose(b[:, :, g], (0, 3, 1, 2))
                BNL[:, sl, q, :, r16] = np.transpose(b[:, :, g], (0, 2, 1, 3))
                CNL[:, sl, q, :, pb, gl * 16:gl * 16 + 16] = np.transpose(c[:, :, g], (0, 3, 1, 2))
        dv[:, :, q] = inp["s5_d"][:, 8 * q:8 * q + 8].reshape(L, 128)
    return aNL, aPR, BPR, BNL, CNL, dv


def _h0_layout(state_ssm4):
    L = DEPTH
    out = np.zeros((L, 128, 2, 16, 4), np.float32)
    for q in range(4):
        for pb in range(4):
            for gl in range(2):
                g = 8 * q + 2 * pb + gl
                out[:, gl * 64:gl * 64 + 64, :, q * 4 + pb, :] = np.transpose(state_ssm4[:, :, :, g, :], (1, 3, 2, 0))
    return out


def _rope_tables():
    inv = (500000.0 ** (-np.arange(8, dtype=np.float32) / 8)).astype(np.float32)
    pos = np.concatenate([np.arange(2048), 8192 + (np.arange(128) % 4)]).astype(np.float32)
    ang = pos[:, None] * inv[None, :]
    cos = np.cos(ang).astype(np.float32).reshape(17, 128, 8).transpose(1, 0, 2)
    sin = np.sin(ang).astype(np.float32).reshape(17, 128, 8).transpose(1, 0, 2)
    cpos = (np.arange(128) * 64 + 63).astype(np.float32)
    angc = cpos[None, :] * inv[:, None]
    cosC = np.ones((128, 128), np.float32)
    sinC = np.zeros((128, 128), np.float32)
    for g in range(2):
        cosC[g * 64:g * 64 + 8] = np.cos(angc)
        cosC[g * 64 + 8:g * 64 + 16] = np.cos(angc)
        sinC[g * 64:g * 64 + 8] = -np.sin(angc)
        sinC[g * 64 + 8:g * 64 + 16] = np.sin(angc)
    return np.ascontiguousarray(cos), np.ascontiguousarray(sin), cosC, sinC


def _attn_consts():
    t = np.arange(2048)
    n = np.arange(32)
    valid = (64 * n[None, :] + 63) <= t[:, None]
    maskC = np.where(valid, 0.0, NEG).astype(np.float32)
    mask01 = valid.astype(np.float32)
    qblk = t // 64
    f0 = (n[None, :] == 0)
    f1 = (n[None, :] == qblk[:, None])
    f2 = (n[None, :] == qblk[:, None] - 1)
    fut = n[None, :] > qblk[:, None]
    forced = f0 | f1 | f2
    selA = (~(forced | fut)).astype(np.float32)
    selB = np.maximum(np.maximum(f0 * 3e4, f1 * 2e4), f2 * 1e4).astype(np.float32) - fut.astype(np.float32)
    tm = lambda a: np.ascontiguousarray(a.reshape(16, 128, 32).transpose(1, 0, 2))
    cm = np.stack([tm(maskC), tm(mask01), tm(selA), tm(selB)], 1)
    E = np.zeros((128, 16, 128), np.float32)
    for kt in range(16):
        for key in range(128):
            nn = 2 * kt + key // 64
            E[nn, kt, key] = 1.0
            E[64 + nn, kt, key] = 1.0
    kl = np.arange(128)[:, None]
    tl = np.arange(128)[None, :]
    triC = np.where(kl > tl, NEG, 0.0).astype(np.float32)
    triW = np.where(kl <= tl, NEG, 0.0).astype(np.float32)
    return cm, E, np.stack([triC, triW], 1)


def _sample_consts():
    sel = np.zeros((16, 2, 129), np.float32)
    sel[:, 0, :] = 1.0
    for c, v in ((0, 3e4), (127, 1e4), (128, 2e4)):
        sel[:, 0, c] = 0.0
        sel[:, 1, c] = v
    r = np.arange(16)
    Smat = (r[:, None] // 4 == r[None, :] // 4).astype(np.float32)
    masknew = np.full((16, 4, 16), NEG, np.float32)
    delta = np.zeros((16, 4, 4), np.float32)
    for b in range(4):
        for rr in range(16):
            j = rr // 4
            for jp in range(j + 1):
                masknew[rr, b, 4 * b + jp] = 0.0
        for j in range(4):
            delta[4 * b + j, b, j] = 1.0
    i = np.arange(512)
    maskwin = np.where(i[None, :] <= (r[:, None] // 4), NEG, 0.0).astype(np.float32)
    return sel, Smat, masknew, delta, maskwin


def _cmp_layouts(inp):
    L = DEPTH
    w1 = inp["cmp_w1"].reshape(L, 2, 64, 64, 64)
    W1bd = np.zeros((L, 2, 128, 64, 128), np.float32)
    W2bd = np.zeros((L, 3, 128, 128), np.float32)
    peT = np.zeros((L, 2, 128, 64), np.float32)
    perm = np.arange(64)
    perm[0:8] = np.arange(8, 16)
    perm[8:16] = np.arange(0, 8)
    for g in range(2):
        sl = slice(g * 64, g * 64 + 64)
        W1bd[:, :, sl, :, sl] = np.transpose(w1, (0, 1, 3, 2, 4))
        W2bd[:, 0, sl, sl] = inp["cmp_w2"][:, 0]
        W2bd[:, 1, sl, sl] = inp["cmp_w2"][:, 1]
        w2p = inp["cmp_w2"][:, 0][:, :, perm].copy()
        w2p[:, :, 16:] = 0.0
        W2bd[:, 2, sl, sl] = w2p
        peT[:, :, sl, :] = np.transpose(inp["cmp_pe"], (0, 1, 3, 2))
    return W1bd, W2bd, peT


class _Stop(Exception):
    pass


class Buf:
    def __init__(self, t, name, ntok=1):
        self.t = t
        self.tok = Tok(name)
        self.toks = [Tok(f"{name}{i}") for i in range(ntok)] if ntok > 1 else [self.tok]


def build(cfg=None):
    cfg = cfg or {}
    nlayers = cfg.get("nlayers", DEPTH)
    taps = cfg.get("taps", {})
    stop_after = cfg.get("stop_after", None)
    nc = bass.Bass("TRN2", target_bir_lowering=False)
    P = Prog(nc)
    mem = Mem(nc)

    def din(name, shape, dt=F32):
        return nc.dram_tensor(name, list(shape), dt, kind="ExternalInput").ap()

    def dout(name, shape, dt=F32):
        return nc.dram_tensor(name, list(shape), dt, kind="ExternalOutput").ap()

    D = {}
    D["xT0"] = din("xT0", [128, 8, NT])
    D["cT"] = din("cT", [128, 8, 5])
    D["ident"] = din("ident", [128, 128])
    D["ada_w"] = din("ada_w", [DEPTH, 1024, 3072])
    D["ada_bT"] = din("ada_bT", [DEPTH, 128, 24])
    D["norm_gT"] = din("norm_gT", [DEPTH, 128, 8])
    D["final_gT"] = din("final_gT", [128, 8])
    D["w_in"] = din("w_in", [DEPTH, 1024, 4888])
    D["aNL"] = din("aNL", [DEPTH, 128, 3, 16])
    D["aPR"] = din("aPR", [DEPTH, 128, 4, 3, 128])
    D["BPR"] = din("BPR", [DEPTH, 128, 4, 2, 128])
    D["BNL"] = din("BNL", [DEPTH, 128, 4, 2, 128])
    D["CNL"] = din("CNL", [DEPTH, 128, 4, 2, 4, 32])
    D["dv"] = din("dv", [DEPTH, 128, 4])
    D["h0NL"] = din("h0NL", [DEPTH, 128, 2, 16, 4])
    D["glu_w"] = din("glu_w", [DEPTH, 512, 512])
    D["glu_bT"] = din("glu_bT", [DEPTH, 128, 4])
    D["w_s5_out"] = din("w_s5_out", [DEPTH, 512, 1024])
    D["w_nsa_out"] = din("w_nsa_out", [DEPTH, 512, 1024])
    D["w_o"] = din("w_o", [DEPTH, 1024, 1024])
    D["cosT"] = din("cosT", [128, 17, 8])
    D["sinT"] = din("sinT", [128, 17, 8])
    D["swin"] = din("swin", [4, DEPTH, 512, 256])
    D["cm"] = din("cm", [128, 4, 16, 32])
    D["Ecst"] = din("Ecst", [128, 16, 128])
    D["tri"] = din("tri", [128, 2, 128])
    D["ropeC"] = din("ropeC", [128, 2, 128])
    D["W1bd"] = din("W1bd", [DEPTH, 2, 128, 64, 128])
    D["W2bd"] = din("W2bd", [DEPTH, 3, 128, 128])
    D["peT"] = din("peT", [DEPTH, 2, 128, 64])
    D["cache2"] = din("cache2", [cfg.get("npool", 2560) * DEPTH * 128 * 2, 256])
    D["ptab"] = din("ptab", [4, 64], I32)
    D["iot2"] = din("iot2", [128, 1])
    D["s_sel"] = din("s_sel", [16, 2, 129])
    D["s_smat"] = din("s_smat", [16, 16])
    D["s_mnew"] = din("s_mnew", [16, 4, 16])
    D["s_delta"] = din("s_delta", [16, 4, 4])
    D["s_mwin"] = din("s_mwin", [16, 512])
    O = {}
    O["kvP"] = dout("kvP", [DEPTH, TP, 512])
    O["kvS"] = dout("kvS", [DEPTH, 16, 512])
    O["winP"] = dout("winP", [DEPTH, 512, 256])
    O["winS"] = dout("winS", [4, DEPTH, 512, 256])
    O["yT"] = dout("yT", [128, 8, NT])
    O["hlP"] = dout("hlP", [DEPTH, 128, 2, 16])
    O["hlS"] = dout("hlS", [DEPTH, 128, 2, 16, 4])
    TAP = {n: dout("tap_" + n, shp[0], BF16 if shp[1] == "bf16" else F32) for n, shp in taps.items()}
    out_tok = Tok("out")

    def chk(name):
        if stop_after == name:
            raise _Stop()

    def tt(eng, out, in0, in1, op, R, W):
        return P.op(eng, lambda e: e.tensor_tensor(out=out, in0=in0, in1=in1, op=op), R, W)

    def ts(eng, out, in0, s1, s2, op0, op1, R, W):
        if op1 is None:
            return P.op(eng, lambda e: e.tensor_scalar(out=out, in0=in0, scalar1=s1, scalar2=None, op0=op0), R, W)
        return P.op(eng, lambda e: e.tensor_scalar(out=out, in0=in0, scalar1=s1, scalar2=s2, op0=op0, op1=op1), R, W)

    def stt(out, in0, sc, in1, op0, op1, R, W):
        return P.op("dve", lambda e: e.scalar_tensor_tensor(out=out, in0=in0, scalar=sc, in1=in1, op0=op0, op1=op1), R, W)

    def cp(eng, out, in_, R, W):
        return P.op(eng, lambda e: e.tensor_copy(out=out, in_=in_), R, W)

    def act(out, in_, func, R, W, bias=None, scale=None):
        kw = {}
        if bias is not None:
            kw["bias"] = bias
        if scale is not None:
            kw["scale"] = scale
        return P.op("act", lambda e: e.activation(out=out, in_=in_, func=func, **kw), R, W)

    pe_mode = [None]

    def _rnd(v):
        return 32 if v <= 32 else (64 if v <= 64 else 128)

    def pe_drain_if(mode):
        if pe_mode[0] is not None and pe_mode[0] != mode and P.cnt["pe"]:
            P._wait("pe", ("e_pe", P.cnt["pe"]))
        pe_mode[0] = mode

    def mm(out, lhsT, rhs, start, stop, R, W):
        shp = list(lhsT.shape)
        kt_ = _rnd(shp[0])
        pe_drain_if((kt_, _rnd(int(np.prod(shp[1:]))), lhsT.start_partition() if kt_ < 128 else 0))
        return P.op("pe", lambda e: e.matmul(out, lhsT=lhsT, rhs=rhs, start=start, stop=stop), R, W, pe_accum=True)

    def tr(out, in_, ident, R, W):
        shp = list(in_.shape)
        pe_drain_if((_rnd(shp[0]), _rnd(int(np.prod(shp[1:])))))
        return P.op("pe", lambda e: e.transpose(out=out, in_=in_, identity=ident), R, W, pe_accum=True)

    def ms(eng, ap, val, W):
        return P.op(eng, lambda e: e.memset(ap, val), (), W)

    def ld(q, key, out, in_, W, R=()):
        return P.dma(q, key, lambda e: e.dma_start(out=out, in_=in_), R, W)

    def tap(name, ap, R):
        if name in TAP:
            P.dma("sp", "tap", lambda e: e.dma_start(out=TAP[name], in_=ap), R, [out_tok])

    def cmul(eng, ore, oim, xre, xim, yre, yim, t1, t2, R, W, Tt):
        tt(eng, t1, xre, yre, ALU.mult, R, [Tt])
        tt(eng, t2, xim, yim, ALU.mult, R, [Tt])
        tt(eng, ore, t1, t2, ALU.subtract, [Tt], W)
        tt(eng, t1, xre, yim, ALU.mult, R, [Tt])
        tt(eng, t2, xim, yre, ALU.mult, R, [Tt])
        tt(eng, oim, t1, t2, ALU.add, [Tt], W)

    xT = Buf(mem.sb("xT", [128, 8, NT], F32), "xT", 5)
    identf = Buf(mem.sb("identf", [128, 128], F32), "identf")
    identb = Buf(mem.sb("identb", [128, 128], BF16), "identb")
    onesb = Buf(mem.sb("onesb", [128, 128], BF16), "onesb")
    siluC = Buf(mem.sb("siluC", [128, 8, 5], F32), "siluC")
    PS = [Buf(mem.ps(f"ps{i}", [128, 512], F32), f"ps{i}") for i in range(8)]
    for b__ in PS:
        b__.tok.excl = True
    BANKS = [(i * 512, min(NT, (i + 1) * 512)) for i in range(5)]

    ropeT = Buf(mem.sb("ropeT", [128, 2, 17, 8], F32), "ropeT")
    ropeC = Buf(mem.sb("ropeC", [128, 2, 128], F32), "ropeC")
    zerob = Buf(mem.sb("zerob", [128, 512], BF16), "zerob")
    rmk = Buf(mem.sb("rmk", [128, 2], F32), "rmk")
    ld("sp", "cst", ropeC.t[:], D["ropeC"], [ropeC.tok])
    ms("pool", zerob.t[:], 0.0, [zerob.tok])
    ms("pool", rmk.t[64:128, 0:1], 1.0, [rmk.tok])
    ms("pool", rmk.t[64:128, 1:2], 1.0, [rmk.tok])
    ms("pool", rmk.t[64:96, 1:2], 0.0, [rmk.tok])
    ts("pool", rmk.t[64:128, 0:1], rmk.t[64:128, 1:2], -1.0, 1.0, ALU.mult, ALU.add, [rmk.tok], [rmk.tok])
    ld("sp", "cst", ropeT.t[:, 0], D["cosT"], [ropeT.tok])
    ld("sp", "cst", ropeT.t[:, 1], D["sinT"], [ropeT.tok])
    for b_, (c0, c1) in enumerate(BANKS):
        ld("sp", "xin", xT.t[:, :, c0:c1], D["xT0"][:, :, c0:c1], [xT.toks[b_]])
    ld("sp", "cst", identf.t[:], D["ident"], [identf.tok])
    ld("sp", "cst", siluC.t[:], D["cT"], [siluC.tok])
    cp("dve", identb.t[:], identf.t[:], [identf.tok], [identb.tok])
    ms("dve", onesb.t[:], 1.0, [onesb.tok])
    act(siluC.t[:], siluC.t[:], AF.Silu, [siluC.tok], [siluC.tok])

    modT = Buf(mem.sb("modT", [128, 24, 5], F32), "modT")
    s1g = Buf(mem.sb("s1g", [128, 8, 5], F32), "s1g")
    s1gS = Buf(mem.sb("s1gS", [128, 8, 16], F32), "s1gS")
    shS = Buf(mem.sb("shS", [128, 8, 16], F32), "shS")
    gtS = Buf(mem.sb("gtS", [128, 8, 16], F32), "gtS")
    lvec = Buf(mem.sb("lvec", [128, 24 + 8 + 4], F32), "lvec")

    def adaln_phase(l):
        m0 = mem.mark()
        adw = [Buf(mem.sb(f"adw{i}", [128, 8, 512], F32), f"adw{i}") for i in range(2)]
        ld("sp", "lvec", lvec.t[:, 0:24], D["ada_bT"][l], [lvec.tok])
        ld("sp", "lvec", lvec.t[:, 24:32], D["norm_gT"][l], [lvec.tok])
        ld("sp", "lvec", lvec.t[:, 32:36], D["glu_bT"][l], [lvec.tok])
        psA = PS[0]
        wv = D["ada_w"][l].rearrange("(kc p) n -> p kc n", p=128)
        for blk in range(6):
            buf = adw[blk % 2]
            ld("sp", f"adw{blk % 2}", buf.t[:], wv[:, :, blk * 512:(blk + 1) * 512], [buf.tok])
            for oc4 in range(4):
                oc = blk * 4 + oc4
                for kc in range(8):
                    mm(psA.t[:, oc * 5:(oc + 1) * 5], buf.t[:, kc, oc4 * 128:(oc4 + 1) * 128], siluC.t[:, kc, :],
                       kc == 0, kc == 7, [buf.tok, siluC.tok], [psA.tok])
        tt("dve", modT.t[:], psA.t[:, 0:120].rearrange("p (o b) -> p o b", b=5),
           lvec.t[:, 0:24].unsqueeze(2).to_broadcast([128, 24, 5]), ALU.add, [psA.tok, lvec.tok], [modT.tok])
        ts("dve", s1g.t[:], modT.t[:, 8:16, :], 1.0, None, ALU.add, None, [modT.tok], [s1g.tok])
        tt("dve", s1g.t[:], s1g.t[:], lvec.t[:, 24:32].unsqueeze(2).to_broadcast([128, 8, 5]), ALU.mult,
           [s1g.tok, lvec.tok], [s1g.tok])
        for dst, src in ((s1gS, s1g.t[:, :, 1:5]), (shS, modT.t[:, 0:8, 1:5]), (gtS, modT.t[:, 16:24, 1:5])):
            cp("dve", dst.t[:].rearrange("p k (b j) -> p k b j", j=4),
               src.unsqueeze(3).to_broadcast([128, 8, 4, 4]), [modT.tok, s1g.tok], [dst.tok])
        end_phase(m0)

    def end_phase(m0):
        P.barrier()
        mem.release(m0)

    def norm_phase(hT):
        m0 = mem.mark()
        sqs = [Buf(mem.sb(f"sq{i}", [128, 8, 256], BF16), f"sq{i}") for i in range(2)]
        xns = [Buf(mem.sb(f"xn{i}", [128, 8, 256], F32), f"xn{i}") for i in range(2)]
        rts = [Buf(mem.sb(f"rt{i}", [128, 256], F32), f"rt{i}") for i in range(2)]
        chunks = [(i * 256, i * 256 + 256) for i in range(8)] + [(TP, NT)]
        for ci, (c0, c1) in enumerate(chunks):
            w = c1 - c0
            b_ = min(c0 // 512, 4)
            ps = PS[1 + (ci % 2)]
            sq, xn, rt = sqs[ci % 2], xns[ci % 2], rts[ci % 2]
            act(sq.t[:, :, 0:w], xT.t[:, :, c0:c1], AF.Square, [xT.toks[b_]], [sq.tok])
            for kc in range(8):
                mm(ps.t[:, 0:w], onesb.t[:], sq.t[:, kc, 0:w], kc == 0, kc == 7, [onesb.tok, sq.tok], [ps.tok])
            act(rt.t[:, 0:w], ps.t[:, 0:w], AF.Sqrt, [ps.tok], [rt.tok], bias=1e-6, scale=1.0 / 1024.0)
            P.op("dve", lambda e, w=w, rt=rt: e.reciprocal(out=rt.t[:, 0:w], in_=rt.t[:, 0:w]), [rt.tok], [rt.tok])
            tt("dve", xn.t[:, :, 0:w], xT.t[:, :, c0:c1], rt.t[:, 0:w].unsqueeze(1).to_broadcast([128, 8, w]),
               ALU.mult, [xT.toks[b_], rt.tok], [xn.tok])
            if b_ < 4:
                for kc in range(8):
                    act(hT.t[:, kc, c0:c1], xn.t[:, kc, 0:w], AF.Identity, [xn.tok, s1g.tok, modT.tok], [hT.toks[b_]],
                        bias=modT.t[:, kc, 0:1], scale=s1g.t[:, kc, 0:1])
            else:
                tt("dve", xn.t[:, :, 0:w], xn.t[:, :, 0:w], s1gS.t[:], ALU.mult, [xn.tok, s1gS.tok], [xn.tok])
                tt("dve", hT.t[:, :, c0:c1], xn.t[:, :, 0:w], shS.t[:], ALU.add, [xn.tok, shS.tok], [hT.toks[b_]])
        end_phase(m0)

    hlP = Buf(mem.sb("hlP", [128, 2, 16], F32), "hlP")
    hlS = Buf(mem.sb("hlS", [128, 2, 16, 4], F32), "hlS")

    def s5_params(eng, are, aim, ldt, shape, nm):
        T = Tok(nm)
        mk = lambda n, dt=F32: mem.sb(f"{nm}_{n}", shape, dt)
        abr, abi, er, ei = mk("abr"), mk("abi"), mk("er"), mk("ei")
        m1 = mem.mark()
        dt_, ang, mag, r, kf, mk_, sn, cs, t1, t2 = [mk(n) for n in ("dt", "ang", "mag", "r", "kf", "m", "sn", "cs", "t1", "t2")]
        ki = mk("ki", I32)
        R = W = [T]
        A = lambda t: t[:]
        act(A(dt_), ldt, AF.Exp, R, W)
        tt(eng, A(ang), aim, A(dt_), ALU.mult, R, W)
        tt(eng, A(t1), are, A(dt_), ALU.mult, R, W)
        act(A(mag), A(t1), AF.Exp, R, W)

        def red_sin(out, add):
            ts(eng, A(r), A(ang), add, None, ALU.add, None, R, W)
            ts(eng, A(ki), A(r), 1.0 / TWO_PI, None, ALU.mult, None, R, W)
            cp(eng, A(kf), A(ki), R, W)
            ts(eng, A(kf), A(kf), -TWO_PI, None, ALU.mult, None, R, W)
            tt(eng, A(r), A(r), A(kf), ALU.add, R, W)
            ts(eng, A(mk_), A(r), float(np.pi), -TWO_PI, ALU.is_gt, ALU.mult, R, W)
            tt(eng, A(r), A(r), A(mk_), ALU.add, R, W)
            ts(eng, A(mk_), A(r), -float(np.pi), TWO_PI, ALU.is_lt, ALU.mult, R, W)
            tt(eng, A(r), A(r), A(mk_), ALU.add, R, W)
            act(A(out), A(r), AF.Sin, R, W)
        red_sin(sn, 0.0)
        red_sin(cs, float(np.pi / 2))
        tt(eng, A(abr), A(mag), A(cs), ALU.mult, R, W)
        tt(eng, A(abi), A(mag), A(sn), ALU.mult, R, W)
        tt(eng, A(t1), are, are, ALU.mult, R, W)
        tt(eng, A(t2), aim, aim, ALU.mult, R, W)
        tt(eng, A(t1), A(t1), A(t2), ALU.add, R, W)
        P.op("dve", lambda e: e.reciprocal(out=A(t1), in_=A(t1)), R, W)
        ts(eng, A(t2), A(abr), -1.0, None, ALU.add, None, R, W)
        tt(eng, A(er), A(t2), are, ALU.mult, R, W)
        tt(eng, A(mk_), A(abi), aim, ALU.mult, R, W)
        tt(eng, A(er), A(er), A(mk_), ALU.add, R, W)
        tt(eng, A(er), A(er), A(t1), ALU.mult, R, W)
        tt(eng, A(ei), A(abi), are, ALU.mult, R, W)
        tt(eng, A(mk_), A(t2), aim, ALU.mult, R, W)
        tt(eng, A(ei), A(ei), A(mk_), ALU.subtract, R, W)
        tt(eng, A(ei), A(ei), A(t1), ALU.mult, R, W)
        P.barrier()
        mem.release(m1)
        return abr, abi, er, ei, T

    def s5_phase(l, uT, zs5T):
        m0 = mem.mark()
        aNL = mem.sb("aNL", [128, 3, 16], F32)
        aPR = mem.sb("aPR", [128, 4, 3, 128], F32)
        Tin = Tok("s5in")
        ld("sp", "s5in", aNL[:], D["aNL"][l], [Tin])
        ld("sp", "s5in", aPR[:], D["aPR"][l], [Tin])
        P.barrier()
        nabr, nabi, ner, nei, TN = s5_params("dve", aNL[:, 0, :], aNL[:, 1, :], aNL[:, 2, :], [128, 16], "pn")
        pabr, pabi, per_, pei, TPp = s5_params("dve", aPR[:, :, 0, :], aPR[:, :, 1, :], aPR[:, :, 2, :], [128, 4, 128], "pp")
        chk("s5a")
        pw = [(nabr, nabi)]
        tA, tB = mem.sb("pw_t1", [128, 16], F32), mem.sb("pw_t2", [128, 16], F32)
        for k in range(4):
            r_, i_ = mem.sb(f"pw{k}r", [128, 16], F32), mem.sb(f"pw{k}i", [128, 16], F32)
            cmul("dve", r_[:], i_[:], pw[-1][0][:], pw[-1][1][:], pw[-1][0][:], pw[-1][1][:], tA[:], tB[:], [TN], [TN], TN)
            pw.append((r_, i_))
        a4r, a4i = pw[2]
        a16r, a16i = pw[4]
        Bp = mem.sb("Bp", [128, 2, 128], F32)
        Bn = mem.sb("Bn", [128, 2, 128], F32)
        Cn = mem.sb("Cn", [128, 2, 4, 32], F32)
        h0 = mem.sb("h0", [128, 2, 4, 4], F32)
        dvq = mem.sb("dvq", [128, 4], F32)
        TL = Tok("s5ld")
        ld("sp", "s5ld", dvq[:], D["dv"][l], [TL])
        curP = [mem.sb(f"curP{i}", [128, 2, 128], F32) for i in range(2)]
        curC = [mem.sb(f"curC{i}", [128, 2, 4, 32], F32) for i in range(2)]
        tP1, tP2 = mem.sb("tP1", [128, 128], F32), mem.sb("tP2", [128, 128], F32)
        tC1, tC2 = mem.sb("tC1", [128, 4, 32], F32), mem.sb("tC2", [128, 4, 32], F32)
        BAtab = mem.sb("BAtab", [128, 16, 2, 128], BF16)
        CAtab = mem.sb("CAtab", [128, 17, 2, 5, 32], BF16)
        BAtab3 = mem.sb("BAtab3", [128, 16, 2, 128], BF16)
        rmask = mem.sb("rmask", [128, 2], F32)
        BbN = mem.sb("BbN", [128, 2, 128], F32)
        BbNb = mem.sb("BbNb", [128, 2, 5, 32], BF16)
        Ktab = mem.sb("Ktab", [128, 16, 128], BF16)
        yacc = mem.sb("yacc", [128, NT], F32)
        SS = [mem.sb(f"SS{i}", [128, 2, 4, 128], F32) for i in range(2)]
        st = [mem.sb(f"st{i}", [128, 4, 128], F32) for i in range(4)]
        Ak = [mem.sb(f"Ak{i}", [128, 2, 4], F32) for i in range(2)]
        Akt = [mem.sb(f"Akt{i}", [128, 4], F32) for i in range(2)]
        Hb = mem.sb("Hb", [128, 2, 4, 128], BF16)
        Hsb = mem.sb("Hsb", [128, 2, 4, 4], BF16)
        Zs = mem.sb("Zs", [128, 2, 4, 4], F32)
        hs1, hs2 = mem.sb("hs1", [128, 4, 4], F32), mem.sb("hs2", [128, 4, 4], F32)
        g1, g2 = mem.sb("g1", [128, 512], F32), mem.sb("g2", [128, 512], F32)
        TBA, TCA, TK, TBb, TY, TS, TH, TG, TZ = [Tok(n) for n in ("BA", "CA", "K", "Bb", "Y", "S", "H", "G", "Z")]
        Tst = [Tok(f"st{i}") for i in range(2)]
        TSS = [[Tok(f"SS{i}{j}") for j in range(2)] for i in range(2)]
        uv_all = uT.t
        TM_ = Tok("rmask")
        ms("pool", CAtab[:], 0.0, [TCA])
        ms("pool", BbNb[:], 0.0, [TBb])
        ms("pool", BAtab3[:], 0.0, [TBA])
        ms("pool", rmask[64:128, 0:1], 1.0, [TM_])
        ms("pool", rmask[64:128, 1:2], 0.0, [TM_])
        ms("pool", rmask[64:96, 1:2], 1.0, [TM_])
        ts("pool", rmask[64:128, 0:1], rmask[64:128, 1:2], 1.0, None, ALU.mult, None, [TM_], [TM_])
        ts("pool", rmask[64:128, 1:2], rmask[64:128, 0:1], -1.0, 1.0, ALU.mult, ALU.add, [TM_], [TM_])
        for q in range(4):
            qs = slice(q * 4, q * 4 + 4)
            ld("sp", "s5ld", Bp[:], D["BPR"][l][:, q], [TL])
            ld("sp", "s5ld", Bn[:], D["BNL"][l][:, q], [TL])
            ld("sp", "s5ld", Cn[:], D["CNL"][l][:, q], [TL])
            ld("sp", "s5ld", h0[:], D["h0NL"][l][:, :, qs, :], [TL])
            cmul("pool", curP[0][:, 0, :], curP[0][:, 1, :], Bp[:, 0, :], Bp[:, 1, :], per_[:, q, :], pei[:, q, :],
                 tP1[:], tP2[:], [TL, TPp], [TBA], TBA)
            ci = 0
            for j in range(15, -1, -1):
                cp("pool", BAtab[:, j, :, :], curP[ci][:], [TBA], [TBA])
                ts("pool", BAtab3[64:128, j, :, :], curP[ci][64:128], rmask[64:128, 1:2], None, ALU.mult, None, [TBA, TM_], [TBA])
                if j > 0:
                    cmul("pool", curP[1 - ci][:, 0, :], curP[1 - ci][:, 1, :], curP[ci][:, 0, :], curP[ci][:, 1, :],
                         pabr[:, q, :], pabi[:, q, :], tP1[:], tP2[:], [TPp, TBA], [TBA], TBA)
                    ci = 1 - ci
            chk("s5b")
            abr_b = nabr[:, qs].unsqueeze(2).to_broadcast([128, 4, 32])
            abi_b = nabi[:, qs].unsqueeze(2).to_broadcast([128, 4, 32])
            cp("dve", curC[0][:], Cn[:], [TL], [TCA])
            ci = 0
            for d in range(17):
                cp("dve", CAtab[:, d, :, 0:3, :], curC[ci][:, :, 0:3, :], [TCA], [TCA])
                cp("dve", CAtab[:, d, :, 4, :], curC[ci][:, :, 3, :], [TCA], [TCA])
                if d < 16:
                    cmul("dve", curC[1 - ci][:, 0], curC[1 - ci][:, 1], curC[ci][:, 0], curC[ci][:, 1], abr_b, abi_b,
                         tC1[:], tC2[:], [TN, TCA], [TCA], TCA)
                    ci = 1 - ci
            er_b = ner[:, qs].unsqueeze(2).to_broadcast([128, 4, 32])
            ei_b = nei[:, qs].unsqueeze(2).to_broadcast([128, 4, 32])
            v4 = lambda ap: ap.rearrange("p (a b) -> p a b", b=32)
            cmul("dve", v4(BbN[:, 0, :]), v4(BbN[:, 1, :]), v4(Bn[:, 0, :]), v4(Bn[:, 1, :]), er_b, ei_b, tC1[:], tC2[:],
                 [TL, TN], [TBb], TBb)
            BbN4 = BbN[:].rearrange("p r (a b) -> p r a b", b=32)
            cp("dve", BbNb[:, 0, 0:3, :], BbN4[:, 0, 0:3, :], [TBb], [TBb])
            cp("dve", BbNb[:, 0, 4, :], BbN4[:, 0, 3, :], [TBb], [TBb])
            ts("dve", BbNb[:, 1, 0:3, :], BbN4[:, 1, 0:3, :], -1.0, None, ALU.mult, None, [TBb], [TBb])
            ts("dve", BbNb[:, 1, 4, :], BbN4[:, 1, 3, :], -1.0, None, ALU.mult, None, [TBb], [TBb])
            chk("s5c")
            psK = PS[2]
            BI = [0, 1, 2, 4]
            for pb in range(2):
                rows = slice(pb * 32, pb * 32 + 32)
                for ri in range(2):
                    mm(psK.t[rows, :].rearrange("p (d c) -> p d c", c=32), BbNb[:, ri, pb, :],
                       CAtab[:, 0:16, ri, pb, :], ri == 0, ri == 1, [TBb, TCA], [psK.tok])
            k_ = 0
            for pb in (2, 3):
                for ri in range(2):
                    mm(psK.t[64:128, :].rearrange("p (d c) -> p d c", c=32),
                       BbNb[:, ri, pb:pb + 2, :].rearrange("p a b -> p (a b)"),
                       CAtab[:, 0:16, ri, BI[pb], :], k_ == 0, k_ == 3, [TBb, TCA], [psK.tok])
                    k_ += 1
            ms("pool", Ktab[:], 0.0, [TK])
            for pb in range(2):
                rows = slice(pb * 32, pb * 32 + 32)
                cp("dve", Ktab[rows, :, pb * 32:(pb + 1) * 32], psK.t[rows, :].rearrange("p (d c) -> p d c", c=32),
                   [psK.tok], [TK])
            for pb in (2, 3):
                ts("dve", Ktab[64:128, :, pb * 32:(pb + 1) * 32], psK.t[64:128, :].rearrange("p (d c) -> p d c", c=32),
                   rmask[64:128, pb - 2:pb - 1], None, ALU.mult, None, [psK.tok, TM_], [TK])
            stt(Ktab[:, 0, :], identf.t[:], dvq[:, q:q + 1], Ktab[:, 0, :], ALU.mult, ALU.add, [identf.tok, TL, TK], [TK])
            chk("s5d")
            for b_, (c0, c1) in enumerate(BANKS):
                ps = PS[b_ % 2]
                if b_ < 4:
                    Yv = ps.t[:].rearrange("p (n i) -> p n i", i=16)
                    uv = uv_all[:, q, c0:c1].rearrange("p (n i) -> p n i", i=16)
                    nl = 16
                else:
                    Yv = ps.t[:, 0:16].rearrange("p (n i) -> p n i", i=4)
                    uv = uv_all[:, q, c0:c1].rearrange("p (n i) -> p n i", i=4)
                    nl = 4
                for d in range(nl):
                    mm(Yv[:, :, d:nl], Ktab[:, d, :], uv[:, :, 0:nl - d], d == 0, d == nl - 1, [TK, uT.tok], [ps.tok])
                act(yacc[:, c0:c1], ps.t[:, 0:c1 - c0], AF.Copy, [ps.tok], [TY])
            chk("s5e")
            up = uv_all[:, q, 0:TP].rearrange("p (n j) -> p n j", j=16)
            us = uv_all[:, q, TP:NT].rearrange("p (b j) -> p b j", j=4)
            for pb in range(4):
                rows = slice(pb * 32, pb * 32 + 32)
                pz = PS[cfg.get('zbank', 4) + pb]
                tab = BAtab
                if pb == 3:
                    rows = slice(64, 128)
                    tab = BAtab3
                zmode = cfg.get("zmode", 0)
                if zmode == 2 and pb == 3:
                    continue
                if zmode == 3 and pb > 0:
                    continue
                for ri in range(2):
                    if zmode == 5:
                        continue
                    for j in range(16):
                        mm(pz.t[:, ri * 128:(ri + 1) * 128], tab[rows, j, ri, :], up[rows, :, j], j == 0, j == 15,
                           [TBA, uT.tok], [pz.tok])
                    if zmode == 1:
                        continue
                    for j in range(4):
                        mm(pz.t[:, 256 + ri * 4:260 + ri * 4], tab[rows, 12 + j, ri, :], us[rows, :, j], j == 0, j == 3,
                           [TBA, uT.tok], [pz.tok])
                if zmode == 4:
                    continue
                cp("dve", SS[0][:, 0, pb, :], pz.t[:, 0:128], [pz.tok], [TSS[0][0]])
                act(SS[0][:, 1, pb, :], pz.t[:, 128:256], AF.Copy, [pz.tok], [TSS[0][1]])
                cp("dve", Zs[:, :, pb, :], pz.t[:, 256:264].rearrange("p (r b) -> p r b", b=4), [pz.tok], [TZ])
            chk("s5f")
            cp("dve", Ak[0][:, 0, :], a16r[:, qs], [TN], [TG])
            cp("dve", Ak[0][:, 1, :], a16i[:, qs], [TN], [TG])
            cur = 0
            for k in range(7):
                s = 1 << k
                src, dst = SS[cur], SS[1 - cur]
                Tsrc, Tdst = TSS[cur], TSS[1 - cur]
                w_ = 128 - s
                Ar = Ak[k % 2][:, 0, :].unsqueeze(2).to_broadcast([128, 4, w_])
                Ai = Ak[k % 2][:, 1, :].unsqueeze(2).to_broadcast([128, 4, w_])
                tt("dve", st[0][:, :, 0:w_], src[:, 0, :, 0:w_], Ar, ALU.mult, [Tsrc[0], TG], [Tst[0]])
                tt("dve", st[1][:, :, 0:w_], src[:, 1, :, 0:w_], Ai, ALU.mult, [Tsrc[1], TG], [Tst[0]])
                tt("dve", st[0][:, :, 0:w_], st[0][:, :, 0:w_], st[1][:, :, 0:w_], ALU.subtract, [Tst[0]], [Tst[0]])
                tt("dve", dst[:, 0, :, s:128], src[:, 0, :, s:128], st[0][:, :, 0:w_], ALU.add, [Tsrc[0], Tst[0]], [Tdst[0]])
                cp("dve", dst[:, 0, :, 0:s], src[:, 0, :, 0:s], [Tsrc[0]], [Tdst[0]])
                tt("pool", st[2][:, :, 0:w_], src[:, 1, :, 0:w_], Ar, ALU.mult, [Tsrc[1], TG], [Tst[1]])
                tt("pool", st[3][:, :, 0:w_], src[:, 0, :, 0:w_], Ai, ALU.mult, [Tsrc[0], TG], [Tst[1]])
                tt("pool", st[2][:, :, 0:w_], st[2][:, :, 0:w_], st[3][:, :, 0:w_], ALU.add, [Tst[1]], [Tst[1]])
                tt("pool", dst[:, 1, :, s:128], src[:, 1, :, s:128], st[2][:, :, 0:w_], ALU.add, [Tsrc[1], Tst[1]], [Tdst[1]])
                cp("pool", dst[:, 1, :, 0:s], src[:, 1, :, 0:s], [Tsrc[1]], [Tdst[1]])
                if k < 6:
                    a_, b_2 = Ak[k % 2], Ak[1 - k % 2]
                    cmul("dve", b_2[:, 0, :], b_2[:, 1, :], a_[:, 0, :], a_[:, 1, :], a_[:, 0, :], a_[:, 1, :],
                         Akt[0][:], Akt[1][:], [TG], [TG], TG)
                cur = 1 - cur
            Sf, TSf = SS[cur], TSS[cur]
            chk("s5g")
            cp("dve", hlP.t[:, :, qs], Sf[:, :, :, 127], TSf, [hlP.tok])
            ms("pool", Hb[:, :, :, 0:1], 0.0, [TH])
            cp("dve", Hb[:, 0, :, 1:128], Sf[:, 0, :, 0:127], [TSf[0]], [TH])
            ts("dve", Hb[:, 1, :, 1:128], Sf[:, 1, :, 0:127], -1.0, None, ALU.mult, None, [TSf[1]], [TH])
            cp("dve", Hsb[:, 0], h0[:, 0], [TL], [TH])
            ts("dve", Hsb[:, 1], h0[:, 1], -1.0, None, ALU.mult, None, [TL], [TH])
            a4r_b = a4r[:, qs].unsqueeze(2).to_broadcast([128, 4, 4])
            a4i_b = a4i[:, qs].unsqueeze(2).to_broadcast([128, 4, 4])
            cmul("dve", hlS.t[:, 0, qs, :], hlS.t[:, 1, qs, :], h0[:, 0], h0[:, 1], a4r_b, a4i_b, hs1[:], hs2[:],
                 [TL, TN], [hlS.tok], hlS.tok)
            tt("dve", hlS.t[:, :, qs, :], hlS.t[:, :, qs, :], Zs[:], ALU.add, [hlS.tok, TZ], [hlS.tok])
            chk("s5h")
            def inter(outp, c0_, c1_, i, Hop, part):
                if part == 0:
                    for pb in range(2):
                        rows = slice(pb * 32, pb * 32 + 32)
                        for ri in range(2):
                            mm(outp.t[rows, c0_:c1_], CAtab[:, i + 1, ri, pb, :], Hop[:, ri, pb, :],
                               ri == 0, ri == 1, [TCA, TH], [outp.tok])
                else:
                    k_ = 0
                    for pb in (2, 3):
                        for ri in range(2):
                            mm(outp.t[64:128, c0_:c1_], CAtab[:, i + 1, ri, pb:pb + 2, :].rearrange("p a b -> p (a b)"),
                               Hop[:, ri, pb, :], k_ == 0, k_ == 3, [TCA, TH], [outp.tok])
                            k_ += 1
            pss = PS[3]
            for part in range(2):
                for i in range(16):
                    inter(PS[4 + i // 4], (i % 4) * 128, (i % 4 + 1) * 128, i, Hb, part)
                for i in range(4):
                    inter(pss, i * 4, i * 4 + 4, i, Hsb, part)
            chk("s5i")
            yv = yacc[:, 0:TP].rearrange("p (n i) -> p n i", i=16)
            for ib in range(4):
                tt("dve", yv[:, :, ib * 4:ib * 4 + 4], yv[:, :, ib * 4:ib * 4 + 4],
                   PS[4 + ib].t[:].rearrange("p (i n) -> p n i", n=128), ALU.add, [TY, PS[4 + ib].tok], [TY])
            ysv = yacc[:, TP:NT].rearrange("p (b i) -> p b i", i=4)
            tt("dve", ysv, ysv, pss.t[:, 0:16].rearrange("p (i b) -> p b i", b=4), ALU.add, [TY, pss.tok], [TY])
            tap(f"y5scan{q}", yacc[:], [TY])
            for b_, (c0, c1) in enumerate(BANKS):
                w = c1 - c0
                yy = yacc[:, c0:c1]
                tt("dve", g1[:, 0:w], yy, yy, ALU.mult, [TY], [TG])
                ts("dve", g1[:, 0:w], g1[:, 0:w], 0.044715 * 1.5957691216, 1.5957691216, ALU.mult, ALU.add, [TG], [TG])
                tt("dve", g1[:, 0:w], g1[:, 0:w], yy, ALU.mult, [TG, TY], [TG])
                act(g2[:, 0:w], g1[:, 0:w], AF.Sigmoid, [TG], [TG])
                tt("dve", uv_all[:, q, c0:c1], yy, g2[:, 0:w], ALU.mult, [TG, TY], [uT.tok])
        P.dma("sp", "out", lambda e: e.dma_start(out=O["hlP"][l], in_=hlP.t[:]), [hlP.tok], [out_tok])
        P.dma("sp", "out", lambda e: e.dma_start(out=O["hlS"][l], in_=hlS.t[:]), [hlS.tok], [out_tok])
        end_phase(m0)

    def uz_proj_phase(l, hT, uT, zs5T):
        m0 = mem.mark()
        wuz = Buf(mem.sb("wuz", [128, 8, 1024], BF16), "wuz", 2)
        wv = D["w_in"][l].rearrange("(kc p) n -> p kc n", p=128)
        for h_ in range(2):
            for kc in range(8):
                ld("pool", f"wuz{h_}", wuz.t[:, kc, h_ * 512:(h_ + 1) * 512], wv[:, kc, h_ * 512:(h_ + 1) * 512], [wuz.toks[h_]])
        k = 0
        for oc in range(8):
            for b_, (c0, c1) in enumerate(BANKS):
                w = c1 - c0
                ps = PS[k % 4]
                k += 1
                for kc in range(8):
                    mm(ps.t[:, 0:w], wuz.t[:, kc, oc * 128:(oc + 1) * 128], hT.t[:, kc, c0:c1], kc == 0, kc == 7,
                       [wuz.toks[oc // 4], hT.toks[b_]], [ps.tok])
                if oc < 4:
                    act(uT.t[:, oc, c0:c1], ps.t[:, 0:w], AF.Copy, [ps.tok], [uT.tok])
                else:
                    act(zs5T.t[:, oc - 4, c0:c1], ps.t[:, 0:w], AF.Silu, [ps.tok], [zs5T.tok])
        end_phase(m0)

    def glu_phase(l, uT, zs5T):
        m0 = mem.mark()
        wg = Buf(mem.sb("wg", [128, 4, 512], BF16), "wg")
        sg = Buf(mem.sb("sg", [128, 512], F32), "sg")
        for kc in range(4):
            ld("pool", "wg", wg.t[:, kc, :], D["glu_w"][l].rearrange("(kc p) n -> p kc n", p=128)[:, kc, :], [wg.tok])
        k = 0
        for oc in range(4):
            for b_, (c0, c1) in enumerate(BANKS):
                w = c1 - c0
                ps = PS[k % 4]
                k += 1
                for kc in range(4):
                    mm(ps.t[:, 0:w], wg.t[:, kc, oc * 128:(oc + 1) * 128], uT.t[:, kc, c0:c1], kc == 0, kc == 3,
                       [wg.tok, uT.tok], [ps.tok])
                act(sg.t[:, 0:w], ps.t[:, 0:w], AF.Sigmoid, [ps.tok, lvec.tok], [sg.tok], bias=lvec.t[:, 32 + oc:33 + oc])
                tt("dve", zs5T.t[:, oc, c0:c1], zs5T.t[:, oc, c0:c1], sg.t[:, 0:w], ALU.mult, [sg.tok, zs5T.tok], [zs5T.tok])
        for b_, (c0, c1) in enumerate(BANKS):
            tt("dve", uT.t[:, :, c0:c1], uT.t[:, :, c0:c1], zs5T.t[:, :, c0:c1], ALU.mult, [uT.tok, zs5T.tok], [uT.tok])
        end_phase(m0)

    TILES = [(i, i * 128, 128) for i in range(16)] + [(16, TP, 16)]
    Q_OFF, KV_OFF, G_OFF, ZN_OFF, MG_OFF = 1024, 1536, 2304, 2328, 2840

    def rope_inplace(v, ti, rows, H, tmp, Tv, Tt):
        cos = ropeT.t[0:rows, 0, ti, :].unsqueeze(1).to_broadcast([rows, H, 8])
        sin = ropeT.t[0:rows, 1, ti, :].unsqueeze(1).to_broadcast([rows, H, 8])
        x1, x2 = v[:, :, 0:8], v[:, :, 8:16]
        t = [tmp[0:rows, i, 0:H, :] for i in range(4)]
        tt("dve", t[0], x1, cos, ALU.mult, [Tv, ropeT.tok], [Tt])
        tt("dve", t[1], x2, sin, ALU.mult, [Tv, ropeT.tok], [Tt])
        tt("dve", t[2], x1, sin, ALU.mult, [Tv, ropeT.tok], [Tt])
        tt("dve", t[3], x2, cos, ALU.mult, [Tv, ropeT.tok], [Tt])
        tt("dve", x1, t[0], t[1], ALU.subtract, [Tt], [Tv])
        tt("dve", x2, t[2], t[3], ALU.add, [Tt], [Tv])

    def tm_proj_phase(l, hT, A):
        m0 = mem.mark()
        wv = D["w_in"][l].rearrange("(kc p) n -> p kc n", p=128)
        wb = [Buf(mem.sb(f"wtm{i}", [128, 8, 512], BF16), f"wtm{i}") for i in range(2)]
        stg = [Buf(mem.sb(f"stg{i}", [128, 512], F32), f"stg{i}") for i in range(4)]
        rtmp = mem.sb("rtmp", [128, 4, 8, 8], F32)
        Trt = Tok("rtmp")
        nld = [0]

        def load_w(c0, ncol):
            buf = wb[nld[0] % 2]
            nld[0] += 1
            for kc in range(8):
                ld("pool", "wtm" + str(nld[0] % 2), buf.t[:, kc, 0:ncol], wv[:, kc, c0:c0 + ncol], [buf.tok])
            return buf
        k = [0]

        def proj_tile(buf, ncol, t0, rows):
            ps = PS[k[0] % 4]
            k[0] += 1
            for kc in range(8):
                mm(ps.t[0:rows, 0:ncol], hT.t[:, kc, t0:t0 + rows], buf.t[:, kc, 0:ncol], kc == 0, kc == 7,
                   [buf.tok, hT.toks[min(t0 // 512, 4)]], [ps.tok])
            return ps
        wq = load_w(Q_OFF, 512)
        wk = load_w(KV_OFF, 512)
        for (ti, t0, rows) in TILES:
            ps = proj_tile(wq, 512, t0, rows)
            sg_ = stg[ti % 4]
            act(sg_.t[0:rows, :].rearrange("p (hq g d) -> p hq g d", hq=4, g=2),
                ps.t[0:rows, :].rearrange("p (g hq d) -> p hq g d", g=2, hq=4), AF.Copy, [ps.tok], [sg_.tok], scale=0.125)
            rope_inplace(sg_.t[0:rows, :].rearrange("p (h d) -> p h d", d=64), ti, rows, 8, rtmp, sg_.tok, Trt)
            pt = PS[4 + ti % 4]
            for hq in range(4):
                tr(pt.t[:, hq * 128:hq * 128 + rows], sg_.t[0:rows, hq * 128:(hq + 1) * 128], identf.t[0:rows, 0:rows],
                   [sg_.tok, identf.tok], [pt.tok])
            if ti < 16:
                cp("dve", A["qT"].t[:, :, t0:t0 + 128], pt.t[:].rearrange("p (h t) -> p h t", t=128), [pt.tok], [A["qT"].tok])
            else:
                cp("dve", A["qTs"].t[:], pt.t[:].rearrange("p (h t) -> p h t", t=128)[:, :, 0:16], [pt.tok], [A["qTs"].tok])
        ww = load_w(KV_OFF + 512, 280)
        for (ti, t0, rows) in TILES:
            ps = proj_tile(wk, 512, t0, rows)
            sg_ = stg[ti % 4]
            act(sg_.t[0:rows, :], ps.t[0:rows, :], AF.Copy, [ps.tok], [sg_.tok])
            rope_inplace(sg_.t[0:rows, 256:384].rearrange("p (h d) -> p h d", d=64), ti, rows, 2, rtmp, sg_.tok, Trt)
            if ti < 16:
                P.dma("sp", f"stg{ti % 2}", lambda e, sg_=sg_, t0=t0: e.dma_start(out=O["kvP"][l][t0:t0 + 128, :], in_=sg_.t[:]),
                      [sg_.tok], [out_tok])
            else:
                P.dma("sp", f"stg{ti % 2}", lambda e, sg_=sg_: e.dma_start(out=O["kvS"][l], in_=sg_.t[0:16, :]),
                      [sg_.tok], [out_tok])
            pt = PS[4 + ti % 4]
            for s_ in range(3):
                tr(pt.t[:, s_ * 128:s_ * 128 + rows], sg_.t[0:rows, s_ * 128:(s_ + 1) * 128], identf.t[0:rows, 0:rows],
                   [sg_.tok, identf.tok], [pt.tok])
            cp("dve", A["KT3"].t[:, :, t0:t0 + rows], pt.t[:, 0:384].rearrange("p (s t) -> p s t", t=128)[:, :, 0:rows],
               [pt.tok], [A["KT3"].tok])
            cp("pool", A["vsel"].t[0:rows, ti, :, 0:64], sg_.t[0:rows, 384:512].rearrange("p (g d) -> p g d", d=64),
               [sg_.tok], [A["vsel"].tok])
            if ti == 16:
                cp("dve", A["S_kT"].t[:, 0, :], pt.t[:, 256:272], [pt.tok], [A["S_kT"].tok])
                cp("pool", A["S_v"].t[:, 0, :, :], sg_.t[0:16, 384:512].rearrange("p (g d) -> p g d", d=64), [sg_.tok], [A["S_v"].tok])
        wz = load_w(ZN_OFF, 512)
        for (ti, t0, rows) in TILES:
            ps = proj_tile(ww, 280, t0, rows)
            sg_ = stg[ti % 4]
            act(sg_.t[0:rows, 0:256], ps.t[0:rows, 0:256], AF.Copy, [ps.tok], [sg_.tok])
            act(A["gsig"].t[0:rows, ti, :], ps.t[0:rows, 256:280], AF.Sigmoid, [ps.tok], [A["gsig"].tok])
            rope_inplace(sg_.t[0:rows, 0:128].rearrange("p (h d) -> p h d", d=64), ti, rows, 2, rtmp, sg_.tok, Trt)
            if 12 <= ti < 16:
                P.dma("sp", f"stg{ti % 2}", lambda e, sg_=sg_, ti=ti: e.dma_start(
                    out=O["winP"][l][(ti - 12) * 128:(ti - 11) * 128, :], in_=sg_.t[:, 0:256]), [sg_.tok], [out_tok])
            elif ti == 16:
                for b in range(4):
                    P.dma("sp", f"stg{ti % 2}", lambda e, sg_=sg_, b=b: e.dma_start(
                        out=O["winS"][b, l, 508:512, :], in_=sg_.t[4 * b:4 * b + 4, 0:256]), [sg_.tok], [out_tok])
                    P.dma("sp", "wcopy", lambda e, b=b: e.dma_start(
                        out=O["winS"][b, l, 0:508, :], in_=D["swin"][b, l, 4:512, :]), [], [out_tok])
            pt = PS[4 + ti % 4]
            tr(pt.t[:, 0:rows], sg_.t[0:rows, 0:128], identf.t[0:rows, 0:rows], [sg_.tok, identf.tok], [pt.tok])
            cp("dve", A["kwinT"].t[:, t0:t0 + rows], pt.t[:, 0:rows], [pt.tok], [A["kwinT"].tok])
            cp("pool", A["vwin"].t[0:rows, ti, :, 0:64], sg_.t[0:rows, 128:256].rearrange("p (g d) -> p g d", d=64),
               [sg_.tok], [A["vwin"].tok])
            if ti == 16:
                cp("dve", A["S_kT"].t[:, 1, :], pt.t[:, 0:16], [pt.tok], [A["S_kT"].tok])
                cp("pool", A["S_v"].t[:, 1, :, :], sg_.t[0:16, 128:256].rearrange("p (g d) -> p g d", d=64), [sg_.tok], [A["S_v"].tok])
        kk = 0
        for oc in range(4):
            for b_, (c0, c1) in enumerate(BANKS):
                w = c1 - c0
                ps = PS[kk % 4]
                kk += 1
                for kc in range(8):
                    mm(ps.t[:, 0:w], wz.t[:, kc, oc * 128:(oc + 1) * 128], hT.t[:, kc, c0:c1], kc == 0, kc == 7,
                       [wz.tok, hT.toks[b_]], [ps.tok])
                act(A["zsT"].t[:, oc, c0:c1], ps.t[:, 0:w], AF.Silu, [ps.tok], [A["zsT"].tok])
        end_phase(m0)

    def compress_setup(l, A, W1, W2):
        for v in range(3):
            ld("pool", "w2", W2.t[:, v, :], D["W2bd"][l, v], [W2.tok])

    def load_w1(l, strm, W1):
        for c4 in range(16):
            ld("pool", "w1", W1.t[:, c4 * 4:(c4 + 1) * 4, :], D["W1bd"][l, strm][:, c4 * 4:(c4 + 1) * 4, :], [W1.tok])

    def compress(l, strm, rowsT, nblk, W1, psC, Trows):
        rv = rowsT.rearrange("p (n j) -> p n j", j=64)
        for j in range(64):
            mm(psC.t[:, 0:nblk + 1], W1.t[:, j, :], rv[:, :, j], j == 0, j == 63, [W1.tok, Trows], [psC.tok])

    def attn_prompt_phase(l, A):
        m0 = mem.mark()
        cmC = Buf(mem.sb("cmC", [128, 4, 16, 32], F32), "cmC")
        Ecst = Buf(mem.sb("Ecst", [128, 16, 128], BF16), "Ecst")
        triB = Buf(mem.sb("triB", [128, 2, 128], BF16), "triB")
        ld("sp", "cst", cmC.t[:], D["cm"], [cmC.tok])
        for kt4 in range(4):
            ld("pool", "cstp", Ecst.t[:, kt4 * 4:(kt4 + 1) * 4, :], D["Ecst"][:, kt4 * 4:(kt4 + 1) * 4, :], [Ecst.tok])
        ld("pool", "cstp", triB.t[:], D["tri"], [triB.tok])
        W1l = [Buf(mem.sb(f"W1_{i}", [128, 64, 128], BF16), f"W1_{i}") for i in range(2)]
        W2 = Buf(mem.sb("W2", [128, 3, 128], BF16), "W2")
        for i_ in range(2):
            load_w1(l, i_, W1l[i_])
        cb = mem.sb("cb", [128, 2], F32)
        silk = mem.sb("silk", [128, 32], BF16)
        silv4 = mem.sb("silv4", [128, 4, 32], BF16)
        kcT = mem.sb("kcT", [128, 32], BF16)
        Vbd2 = mem.sb("Vbd2", [128, 2, 128], BF16)
        kt1, kt2 = mem.sb("kt1", [128, 32], F32), mem.sb("kt2", [128, 32], F32)
        MT = mem.sb("MT", [128, TP], BF16)
        Tc = Tok("cmp")
        TMT = Tok("MT")
        compress_setup(l, A, None, W2)
        for strm in range(2):
            ld("pool", "pe", A["KT3"].t[:, strm, 2048:2112], D["peT"][l, strm], [A["KT3"].tok])
        for strm in range(2):
            psC = PS[2]
            compress(l, strm, A["KT3"].t[:, strm, 0:2112], 32, W1l[strm], psC, A["KT3"].tok)
            cp("dve", cb[:, strm:strm + 1], psC.t[:, 32:33], [psC.tok], [Tc])
            if strm == 0:
                act(silk[:], psC.t[:, 0:32], AF.Silu, [psC.tok, Tc], [Tc], bias=cb[:, 0:1])
            else:
                act(silv4[:], psC.t[:, 0:32].unsqueeze(1).to_broadcast([128, 4, 32]), AF.Silu, [psC.tok, Tc], [Tc], bias=cb[:, 1:2])
        ps = PS[3]
        mm(ps.t[:, 0:32], W2.t[:, 0, :], silk[:], True, True, [W2.tok, Tc], [ps.tok])
        mm(ps.t[:, 32:64], W2.t[:, 2, :], silk[:], True, True, [W2.tok, Tc], [ps.tok])
        tt("dve", kt1[:], ps.t[:, 0:32], ropeC.t[:, 0, 0:32], ALU.mult, [ps.tok, ropeC.tok], [Tc])
        tt("dve", kt2[:], ps.t[:, 32:64], ropeC.t[:, 1, 0:32], ALU.mult, [ps.tok, ropeC.tok], [Tc])
        tt("dve", kcT[:], kt1[:], kt2[:], ALU.add, [Tc], [Tc])
        ps2 = PS[1]
        mm(ps2.t[:, 0:128], silv4[:].rearrange("p a b -> p (a b)"), W2.t[:, 1, :], True, True, [W2.tok, Tc], [ps2.tok])
        ms("pool", Vbd2[:], 0.0, [Tc])
        for g in range(2):
            gs = slice(g * 64, g * 64 + 64)
            cp("dve", Vbd2[0:32, g, 0:64], ps2.t[0:32, gs], [ps2.tok], [Tc])
            cp("dve", Vbd2[32:64, g, 64:128], ps2.t[32:64, gs], [ps2.tok], [Tc])
            ts("dve", Vbd2[64:128, g, 0:64], ps2.t[64:128, gs], rmk.t[64:128, 0:1], None, ALU.mult, None, [ps2.tok, rmk.tok], [Tc])
            ts("dve", Vbd2[64:128, g, 64:128], ps2.t[64:128, gs], rmk.t[64:128, 1:2], None, ALU.mult, None, [ps2.tok, rmk.tok], [Tc])
        tap("kcT", kcT[:], [Tc])
        tap("Vbd2", Vbd2[:], [Tc])
        chk("ap0")
        sc = mem.sb("sc", [128, 8, 32], F32)
        mx = mem.sb("mx", [128, 8], F32)
        imp = mem.sb("imp", [128, 2, 32], F32)
        wk = mem.sb("wk", [128, 2, 32], F32)
        m8 = mem.sb("m8", [128, 2, 16], F32)
        t1 = mem.sb("t1", [128, 2, 64], F32)
        t2 = mem.sb("t2", [128, 2, 32], F32)
        pT = mem.sb("pT", [128, 2, 128], BF16)
        oacc = mem.sb("oacc", [128, 8, 64], F32)
        tmpo = mem.sb("tmpo", [128, 4, 64], F32)
        rc = mem.sb("rc", [128, 4], F32)
        pbuf = [Buf(mem.sb(f"pbuf{i}", [128, 512], BF16), f"pbuf{i}") for i in range(4)]
        Ts, To, Tr = Tok("sc"), Tok("oacc"), Tok("rc")
        ms("pool", t1[:], 0.0, [Ts])
        npb = [0]

        def dense_branch(i, g, t0, KTap, Vbuf, pso, kts, blockmask, gate_off):
            first = True
            for c0_ in range(0, len(kts), 4):
                grp = kts[c0_:c0_ + 4]
                for k_, kt in enumerate(grp):
                    pss = PS[k_]
                    tri = 0 if kt == i else (1 if kt == i - 4 and not blockmask else None)
                    mm(pss.t[:], KTap[g * 64:(g + 1) * 64, kt * 128:(kt + 1) * 128], A["qT"].t[g * 64:(g + 1) * 64, :, t0:t0 + 128],
                       True, (not blockmask) and tri is None, [A["KT3"].tok, A["kwinT"].tok, A["qT"].tok], [pss.tok])
                    if blockmask:
                        mm(pss.t[:], Ecst.t[64 * g:64 * g + 64, kt, :],
                           MT[64 * g:64 * g + 64, t0:t0 + 128].unsqueeze(1).to_broadcast([64, 4, 128]),
                           False, tri is None, [Ecst.tok, TMT], [pss.tok])
                for k_, kt in enumerate(grp):
                    pss = PS[k_]
                    tri = 0 if kt == i else (1 if kt == i - 4 and not blockmask else None)
                    if tri is not None:
                        mm(pss.t[:], identb.t[:], triB.t[:, tri, :].unsqueeze(1).to_broadcast([128, 4, 128]),
                           False, True, [identb.tok, triB.tok], [pss.tok])
                if first:
                    mm(pso.t[:, 0:260], zerob.t[:, 0:128], zerob.t[:, 0:260], True, False, [zerob.tok], [pso.tok])
                    first = False
                for k_, kt in enumerate(grp):
                    pss = PS[k_]
                    pb_ = pbuf[k_]
                    act(pb_.t[:], pss.t[:], AF.Exp, [pss.tok], [pb_.tok])
                    for hq in range(4):
                        mm(pso.t[:, hq * 65:(hq + 1) * 65], pb_.t[:, hq * 128:(hq + 1) * 128], Vbuf.t[:, kt, g, :],
                           False, kt == kts[-1], [pb_.tok, Vbuf.tok], [pso.tok])
            pv = pso.t[:, 0:260].rearrange("p (h c) -> p h c", c=65)
            P.op("dve", lambda e: e.reciprocal(out=rc[:], in_=pv[:, :, 64]), [pso.tok], [Tr])
            tt("dve", rc[:], rc[:], A["gsig"].t[:, i, gate_off + g * 4:gate_off + g * 4 + 4], ALU.mult, [Tr, A["gsig"].tok], [Tr])
            tt("dve", tmpo[:], pv[:, :, 0:64], rc[:].unsqueeze(2).to_broadcast([128, 4, 64]), ALU.mult, [pso.tok, Tr], [Tr])
            tt("dve", oacc[:, g * 4:(g + 1) * 4, :], oacc[:, g * 4:(g + 1) * 4, :], tmpo[:], ALU.add, [To, Tr], [To])

        for i in range(16):
            t0 = i * 128
            psS = PS[0]
            for g in range(2):
                for hq in range(4):
                    h = g * 4 + hq
                    mm(psS.t[:, h * 32:(h + 1) * 32], A["qT"].t[g * 64:(g + 1) * 64, hq, t0:t0 + 128], kcT[g * 64:(g + 1) * 64, :],
                       True, True, [A["qT"].tok, Tc], [psS.tok])
            bc8 = lambda ap: ap.unsqueeze(1).to_broadcast([128, 8, 32])
            tt("dve", sc[:], psS.t[:, 0:256].rearrange("p (h n) -> p h n", n=32), bc8(cmC.t[:, 0, i, :]), ALU.add,
               [psS.tok, cmC.tok], [Ts])
            P.op("dve", lambda e: e.tensor_reduce(out=mx[:], in_=sc[:], axis=AX.X, op=ALU.max), [Ts], [Ts])
            tt("dve", sc[:], sc[:], mx[:].unsqueeze(2).to_broadcast([128, 8, 32]), ALU.subtract, [Ts], [Ts])
            act(sc[:], sc[:], AF.Exp, [Ts], [Ts])
            P.op("dve", lambda e: e.tensor_reduce(out=mx[:], in_=sc[:], axis=AX.X, op=ALU.add), [Ts], [Ts])
            P.op("dve", lambda e: e.reciprocal(out=mx[:], in_=mx[:]), [Ts], [Ts])
            tt("dve", sc[:], sc[:], mx[:].unsqueeze(2).to_broadcast([128, 8, 32]), ALU.mult, [Ts], [Ts])
            tt("dve", sc[:], sc[:], bc8(cmC.t[:, 1, i, :]), ALU.mult, [Ts, cmC.tok], [Ts])
            P.op("dve", lambda e: e.tensor_reduce(out=imp[:], in_=sc[:].rearrange("p (g q) n -> p g n q", g=2), axis=AX.X, op=ALU.add),
                 [Ts], [Ts])
            bc2 = lambda ap: ap.unsqueeze(1).to_broadcast([128, 2, 32])
            tt("dve", imp[:], imp[:], bc2(cmC.t[:, 2, i, :]), ALU.mult, [Ts, cmC.tok], [Ts])
            tt("dve", imp[:], imp[:], bc2(cmC.t[:, 3, i, :]), ALU.add, [Ts, cmC.tok], [Ts])
            for g in range(2):
                P.op("dve", lambda e, g=g: e.max(out=m8[:, g, 0:8], in_=imp[:, g, :]), [Ts], [Ts])
                P.op("dve", lambda e, g=g: e.match_replace(out=wk[:, g, :], in_to_replace=m8[:, g, 0:8], in_values=imp[:, g, :],
                                                           imm_value=-1e9), [Ts], [Ts])
                P.op("dve", lambda e, g=g: e.max(out=m8[:, g, 8:16], in_=wk[:, g, :]), [Ts], [Ts])
                ts("dve", t1[:, g, 0:32], imp[:, g, :], m8[:, g, 15:16], None, ALU.is_ge, None, [Ts], [Ts])
            ts("dve", t2[:], imp[:], -0.5, None, ALU.is_gt, None, [Ts], [Ts])
            tt("dve", t1[:, :, 0:32], t1[:, :, 0:32], t2[:], ALU.mult, [Ts], [Ts])
            ts("dve", t1[:, :, 0:32], t1[:, :, 0:32], -NEG, NEG, ALU.mult, ALU.add, [Ts], [Ts])
            ptm = PS[0]
            tr(ptm.t[:, 0:128], t1[:].rearrange("p g n -> p (g n)"), identf.t[:], [Ts, identf.tok], [ptm.tok])
            cp("dve", MT[:, t0:t0 + 128], ptm.t[:, 0:128], [ptm.tok], [TMT])
            for g in range(2):
                tr(ptm.t[:, 128 + g * 128:256 + g * 128], sc[:, g * 4:(g + 1) * 4, :].rearrange("p h n -> p (h n)"), identf.t[:],
                   [Ts, identf.tok], [ptm.tok])
            cp("dve", pT[:], ptm.t[:, 128:384].rearrange("p (g t) -> p g t", g=2), [ptm.tok], [Ts])
            psO = PS[1]
            for g in range(2):
                for pr in range(2):
                    mm(psO.t[:, (g * 2 + pr) * 128:(g * 2 + pr + 1) * 128], pT[pr * 64:(pr + 1) * 64, g, :],
                       Vbd2[pr * 64:(pr + 1) * 64, g, :], True, True, [Ts, Tc], [psO.tok])
            tt("dve", oacc[:], psO.t[:].rearrange("p (h d) -> p h d", d=64),
               A["gsig"].t[:, i, 0:8].unsqueeze(2).to_broadcast([128, 8, 64]), ALU.mult, [psO.tok, A["gsig"].tok], [To])
            if "oc" in TAP:
                P.dma("sp", "tap", lambda e, i=i: e.dma_start(out=TAP["oc"][:, i], in_=oacc[:]), [To], [out_tok])
            for g in range(2):
                if not cfg.get("nosel"):
                    dense_branch(i, g, t0, A["KT3"].t[:, 2, :], A["vsel"], PS[4 + g], list(range(i + 1)), True, 8)
            for g in range(2):
                if not cfg.get("nowin"):
                    dense_branch(i, g, t0, A["kwinT"].t, A["vwin"], PS[6 + g], list(range(max(0, i - 4), i + 1)), False, 16)
            if "o" in TAP:
                P.dma("sp", "tap", lambda e, i=i: e.dma_start(out=TAP["o"][:, i], in_=oacc[:]), [To], [out_tok])
            pto = PS[0]
            for c in range(4):
                tr(pto.t[:, c * 128:(c + 1) * 128], oacc[:].rearrange("p h d -> p (h d)")[:, c * 128:(c + 1) * 128], identf.t[:],
                   [To, identf.tok], [pto.tok])
            zv = A["zsT"].t[:, :, t0:t0 + 128]
            tt("dve", zv, pto.t[:].rearrange("p (c t) -> p c t", t=128), zv, ALU.mult, [pto.tok, A["zsT"].tok], [A["zsT"].tok])
        end_phase(m0)

    def attn_sample_phase(l, A):
        m0 = mem.mark()
        W1l = [Buf(mem.sb(f"W1s_{i}", [128, 64, 128], BF16), f"W1s_{i}") for i in range(2)]
        W2 = Buf(mem.sb("W2s", [128, 3, 128], BF16), "W2s")
        for i_ in range(2):
            load_w1(l, i_, W1l[i_])
        compress_setup(l, A, None, W2)
        selc = mem.sb("selc", [16, 2, 129], F32)
        smat = mem.sb("smat", [16, 16], F32)
        mnew = mem.sb("mnew", [16, 4, 16], F32)
        delt = mem.sb("delt", [16, 4, 4], F32)
        mwin = mem.sb("mwin", [16, 512], F32)
        iot2 = mem.sb("iot2", [128, 1], F32)
        Tcs = Tok("scst")
        for dst, src in ((selc, "s_sel"), (smat, "s_smat"), (mnew, "s_mnew"), (delt, "s_delta"), (mwin, "s_mwin"), (iot2, "iot2")):
            ld("sp", "c", dst[:], D[src], [Tcs])
        rowsK = Buf(mem.sb("rowsK", [128, 8256], BF16), "rowsK")
        rowsV = Buf(mem.sb("rowsV", [128, 8256], BF16), "rowsV")
        vselp = rowsV.t[:, 0:8192].rearrange("p (g t) -> p g t", t=128)
        pgb = [Buf(mem.sb(f"pgb{i}", [128, 256], F32), f"pgb{i}") for i in range(4)]
        pti = mem.sb("pti", [128, 64], I32)
        ptf = mem.sb("ptf", [128, 64], F32)
        idx = [mem.sb(f"idx{h}", [128, 64], I32) for h in range(2)]
        Ti = Tok("idx")
        cb = mem.sb("cbs", [128, 2], F32)
        silk = mem.sb("silks", [128, 128], BF16)
        silv = mem.sb("silvs", [128, 128], BF16)
        kcT = mem.sb("kcTs", [128, 128], BF16)
        vcS = mem.sb("vcS", [128, 128], BF16)
        kt1, kt2 = mem.sb("kt1s", [128, 128], F32), mem.sb("kt2s", [128, 128], F32)
        Tc = Tok("cmps")
        sc = mem.sb("scs", [16, 2, 128], F32)
        mx = mem.sb("mxs", [16, 2], F32)
        impx = mem.sb("impx", [16, 2, 129], F32)
        wk = mem.sb("wks", [16, 2, 129], F32)
        m8 = mem.sb("m8s", [16, 2, 16], F32)
        madd = mem.sb("madd", [16, 2, 129], F32)
        Ts = Tok("scs")
        sch = mem.sb("sch", [16, 2048], F32)
        snew = mem.sb("snew", [16, 16], F32)
        mxc = mem.sb("mxc", [16, 8], F32)
        gmx = mem.sb("gmx", [16, 1], F32)
        sums = mem.sb("sums", [16, 8], F32)
        PTb = mem.sb("PTb", [128, 16, 16], BF16)
        PTn = mem.sb("PTn", [16, 16], BF16)
        Tp = Tok("sch")
        swt = mem.sb("swt", [128, 4, 256], F32)
        kwT = mem.sb("kwTs", [128, 512], BF16)
        vwS = mem.sb("vwS", [128, 4, 128], BF16)
        swn = mem.sb("swn", [16, 512], F32)
        Tw = Tok("win")
        grep = mem.sb("grep", [16, 16], F32)
        gsb = mem.sb("gsb", [128, 16], F32)
        rsum = mem.sb("rsum", [128, 16], F32)
        oTs = mem.sb("oTs", [128, 2, 2, 4], F32)
        otmp = mem.sb("otmp", [128, 8], F32)
        To = Tok("oTs")
        ones_f = mem.sb("ones_f", [16, 128], F32)
        ms("dve", ones_f[:], 1.0, [Tcs])

        qsb = mem.sb("qsb", [128, 4, 16], BF16)
        cp("dve", qsb[:].rearrange("p b (j h) -> p b j h", h=4), A["qTs"].t[:].rearrange("p h (b j) -> p b j h", j=4),
           [A["qTs"].tok], [A["qTs"].tok])

        def qop(b, g):
            return qsb[g * 64:(g + 1) * 64, b, :]

        def finish_branch(b, g, branch, psO, psSum, first):
            gcol = branch * 8 + g * 4
            tt("dve", grep[:].rearrange("p (j h) -> p j h", h=4), delt[:, b, :].unsqueeze(2).to_broadcast([16, 4, 4]),
               A["gsig"].t[0:16, 16, gcol:gcol + 4].unsqueeze(1).to_broadcast([16, 4, 4]), ALU.mult, [Tcs, A["gsig"].tok], [To])
            pg_ = PS[7]
            mm(pg_.t[:, 0:16], ones_f[:], grep[:], True, True, [To, Tcs], [pg_.tok])
            P.op("dve", lambda e: e.reciprocal(out=rsum[:], in_=psSum.t[:, 0:16]), [psSum.tok], [To])
            tt("dve", gsb[:], rsum[:], pg_.t[:, 0:16], ALU.mult, [To, pg_.tok], [To])
            gv = gsb[:].rearrange("p (j r q) -> p j r q", r=2, q=2)
            for par in range(2):
                rows = slice(par * 64, par * 64 + 64)
                ov = psO.t[rows, 0:8].rearrange("p (j r) -> p r j", r=2)
                gvv = gv[rows, :, :, par].rearrange("p j r -> p r j")
                if first:
                    tt("dve", oTs[rows, g, :, :], ov, gvv, ALU.mult, [psO.tok, To], [To])
                else:
                    tt("dve", otmp[rows, :].rearrange("p (r j) -> p r j", j=4), ov, gvv, ALU.mult, [psO.tok, To], [To])
                    tt("dve", oTs[rows, g, :, :], oTs[rows, g, :, :], otmp[rows, :].rearrange("p (r j) -> p r j", j=4),
                       ALU.add, [To], [To])

        def pv_T(tiles, psO, psSum):
            n = len(tiles)
            for par in range(2):
                for k, (Vap, PTap) in enumerate(tiles):
                    rhs = PTap.rearrange("p (j r q) -> p j r q", r=2, q=2)[:, :, :, par]
                    mm(psO.t[par * 64:(par + 1) * 64, 0:8], Vap, rhs, k == 0, k == n - 1, [Tp, Tc, Tw, rowsV.tok, A["S_v"].tok], [psO.tok])
            for k, (Vap, PTap) in enumerate(tiles):
                K_ = list(PTap.shape)[0]
                mm(psSum.t[:, 0:16], onesb.t[0:K_, :], PTap, k == 0, k == n - 1, [Tp, Tc, Tw, onesb.tok], [psSum.tok])

        def transposes(src_ap_fn, ntile, kw, dstPT, Tsrc):
            ptp = PS[4]
            for t in range(ntile):
                tr(ptp.t[0:kw, t * 16:(t + 1) * 16], src_ap_fn(t), identf.t[0:16, 0:16], [Tsrc, identf.tok], [ptp.tok])
            cp("dve", dstPT[0:kw, 0:ntile, :], ptp.t[0:kw, 0:ntile * 16].rearrange("p (t c) -> p t c", c=16), [ptp.tok], [Tp])

        cache = D["cache2"]
        for b in range(4):
            P.dma("sp", "c", lambda e, b=b: e.dma_start(out=pti[:], in_=D["ptab"][b:b + 1, :].to_broadcast([128, 64])), [Ti], [Ti])
            cp("dve", ptf[:], pti[:], [Ti], [Ti])
            ts("dve", ptf[:], ptf[:], 1024.0, float(l * 256), ALU.mult, ALU.add, [Ti], [Ti])
            ts("dve", ptf[:], ptf[:], iot2[:, 0:1], None, ALU.add, None, [Ti, Tcs], [Ti])
            cp("dve", idx[0][:], ptf[:], [Ti], [Ti])
            ts("dve", ptf[:], ptf[:], 1.0, None, ALU.add, None, [Ti], [Ti])
            cp("dve", idx[1][:], ptf[:], [Ti], [Ti])
            for strm in range(2):
                ld("pool", "pe", (rowsK if strm == 0 else rowsV).t[:, 8192:8256], D["peT"][l, strm], [(rowsK if strm == 0 else rowsV).tok])
            for pp in range(32):
                ptp = PS[pp % 2]
                for k in range(2):
                    pg = pp * 2 + k
                    buf = pgb[pg % 4]
                    P.dma("pool", "g", lambda e, buf=buf, pg=pg: e.indirect_dma_start(
                        out=buf.t[:], out_offset=None, in_=cache[:, :],
                        in_offset=bass.IndirectOffsetOnAxis(ap=idx[0][:, pg:pg + 1], axis=0)), [Ti], [buf.tok])
                    for s_ in range(2):
                        tr(ptp.t[:, (k * 2 + s_) * 128:(k * 2 + s_ + 1) * 128], buf.t[:, s_ * 128:(s_ + 1) * 128], identf.t[:],
                           [buf.tok, identf.tok], [ptp.tok])
                pv4 = ptp.t[:].rearrange("p (k s t) -> p s k t", k=2, s=2)
                cp("dve", rowsK.t[:, pp * 256:(pp + 1) * 256].rearrange("p (k t) -> p k t", k=2), pv4[:, 0], [ptp.tok], [rowsK.tok])
                act(rowsV.t[:, pp * 256:(pp + 1) * 256].rearrange("p (k t) -> p k t", k=2), pv4[:, 1], AF.Copy, [ptp.tok], [rowsV.tok])
            chk("as1")
            for strm in range(2):
                psC = PS[2]
                rb = rowsK if strm == 0 else rowsV
                compress(l, strm, rb.t[:, 0:8256], 128, W1l[strm], psC, rb.tok)
                cp("dve", cb[:, strm:strm + 1], psC.t[:, 128:129], [psC.tok], [Tc])
                act((silk if strm == 0 else silv)[:], psC.t[:, 0:128], AF.Silu, [psC.tok, Tc], [Tc], bias=cb[:, strm:strm + 1])
            ps = PS[3]
            mm(ps.t[:, 0:128], W2.t[:, 0, :], silk[:], True, True, [W2.tok, Tc], [ps.tok])
            mm(ps.t[:, 128:256], W2.t[:, 2, :], silk[:], True, True, [W2.tok, Tc], [ps.tok])
            tt("dve", kt1[:], ps.t[:, 0:128], ropeC.t[:, 0, :], ALU.mult, [ps.tok, ropeC.tok], [Tc])
            tt("dve", kt2[:], ps.t[:, 128:256], ropeC.t[:, 1, :], ALU.mult, [ps.tok, ropeC.tok], [Tc])
            tt("dve", kcT[:], kt1[:], kt2[:], ALU.add, [Tc], [Tc])
            mm(ps.t[:, 256:384], silv[:], W2.t[:, 1, :], True, True, [W2.tok, Tc], [ps.tok])
            cp("dve", vcS[:], ps.t[:, 256:384], [ps.tok], [Tc])
            psS = PS[3]
            for g in range(2):
                mm(psS.t[0:16, g * 128:(g + 1) * 128], qop(b, g), kcT[g * 64:(g + 1) * 64, :], True, True, [A["qTs"].tok, Tc], [psS.tok])
            cp("dve", sc[:], psS.t[0:16, 0:256].rearrange("p (g n) -> p g n", g=2), [psS.tok], [Ts])
            P.op("dve", lambda e: e.tensor_reduce(out=mx[:], in_=sc[:], axis=AX.X, op=ALU.max), [Ts], [Ts])
            tt("dve", sc[:], sc[:], mx[:].unsqueeze(2).to_broadcast([16, 2, 128]), ALU.subtract, [Ts], [Ts])
            act(sc[:], sc[:], AF.Exp, [Ts], [Ts])
            P.op("dve", lambda e: e.tensor_reduce(out=mx[:], in_=sc[:], axis=AX.X, op=ALU.add), [Ts], [Ts])
            P.op("dve", lambda e: e.reciprocal(out=mx[:], in_=mx[:]), [Ts], [Ts])
            tt("dve", sc[:], sc[:], mx[:].unsqueeze(2).to_broadcast([16, 2, 128]), ALU.mult, [Ts], [Ts])
            psI = PS[2]
            mm(psI.t[0:16, 0:256], smat[:], sc[:].rearrange("p g n -> p (g n)"), True, True, [Ts, Tcs], [psI.tok])
            tt("dve", impx[:, :, 0:128], psI.t[0:16, 0:256].rearrange("p (g n) -> p g n", g=2),
               selc[:, 0, 0:128].unsqueeze(1).to_broadcast([16, 2, 128]), ALU.mult, [psI.tok, Tcs], [Ts])
            tt("dve", impx[:, :, 0:128], impx[:, :, 0:128], selc[:, 1, 0:128].unsqueeze(1).to_broadcast([16, 2, 128]), ALU.add, [Ts, Tcs], [Ts])
            cp("dve", impx[:, :, 128:129], selc[:, 1, 128:129].unsqueeze(1).to_broadcast([16, 2, 1]), [Tcs], [Ts])
            for g in range(2):
                P.op("dve", lambda e, g=g: e.max(out=m8[:, g, 0:8], in_=impx[:, g, :]), [Ts], [Ts])
                P.op("dve", lambda e, g=g: e.match_replace(out=wk[:, g, :], in_to_replace=m8[:, g, 0:8], in_values=impx[:, g, :],
                                                           imm_value=-1e9), [Ts], [Ts])
                P.op("dve", lambda e, g=g: e.max(out=m8[:, g, 8:16], in_=wk[:, g, :]), [Ts], [Ts])
                ts("dve", madd[:, g, :], impx[:, g, :], m8[:, g, 15:16], None, ALU.is_ge, None, [Ts], [Ts])
            ts("dve", madd[:], madd[:], -NEG, NEG, ALU.mult, ALU.add, [Ts], [Ts])
            for g in range(2):
                transposes(lambda t, g=g: sc[:, g, :], 1, 128, PTb, Ts)
                pv_T([(vcS[:, g * 64:(g + 1) * 64], PTb[:, 0, :])], PS[5], PS[6])
                finish_branch(b, g, 0, PS[5], PS[6], True)
            chk("as2")
            for pq in range(16):
                ptp = PS[pq % 2]
                for k in range(4):
                    pg = pq * 4 + k
                    buf = pgb[pg % 4]
                    P.dma("pool", "g", lambda e, buf=buf, pg=pg: e.indirect_dma_start(
                        out=buf.t[:], out_offset=None, in_=cache[:, :],
                        in_offset=bass.IndirectOffsetOnAxis(ap=idx[1][:, pg:pg + 1], axis=0)), [Ti], [buf.tok])
                    tr(ptp.t[:, k * 128:(k + 1) * 128], buf.t[:, 0:128], identf.t[:], [buf.tok, identf.tok], [ptp.tok])
                    act(vselp[:, pg, :], buf.t[:, 128:256], AF.Copy, [buf.tok], [rowsV.tok])
                cp("dve", rowsK.t[:, pq * 512:(pq + 1) * 512], ptp.t[:], [ptp.tok], [rowsK.tok])
            chk("as3")
            P.dma("sp", "w", lambda e, b=b: e.dma_start(out=swt[:], in_=D["swin"][b, l].rearrange("(t p) c -> p t c", p=128)), [Tw], [Tw])
            ptw = PS[2]
            for t in range(4):
                tr(ptw.t[:, t * 128:(t + 1) * 128], swt[:, t, 0:128], identf.t[:], [Tw, identf.tok], [ptw.tok])
            cp("dve", kwT[:], ptw.t[:], [ptw.tok], [Tw])
            cp("dve", vwS[:], swt[:, :, 128:256], [Tw], [Tw])
            for g in range(2):
                def sel_chunk(c):
                    for k in range(4):
                        pk = PS[k]
                        mm(pk.t[0:16, :], qop(b, g), rowsK.t[g * 64:(g + 1) * 64, c * 2048 + k * 512:c * 2048 + (k + 1) * 512],
                           True, True, [A["qTs"].tok, rowsK.tok], [pk.tok])
                        tt("dve", sch[:, k * 512:(k + 1) * 512].rearrange("p (n e) -> p n e", e=64),
                           pk.t[0:16, :].rearrange("p (n e) -> p n e", e=64),
                           madd[:, g, c * 32 + k * 8:c * 32 + k * 8 + 8].unsqueeze(2).to_broadcast([16, 8, 64]), ALU.add,
                           [pk.tok, Ts], [Tp])
                pn = PS[7]
                mm(pn.t[0:16, 0:16], qop(b, g), A["S_kT"].t[g * 64:(g + 1) * 64, 0, :], True, True, [A["qTs"].tok, A["S_kT"].tok], [pn.tok])
                tt("dve", snew[:], pn.t[0:16, 0:16], mnew[:, b, :], ALU.add, [pn.tok, Tcs], [Tp])
                ts("dve", snew[:], snew[:], madd[:, g, 128:129], None, ALU.add, None, [Tp, Ts], [Tp])
                P.op("dve", lambda e: e.tensor_reduce(out=mxc[:, 4:5], in_=snew[:], axis=AX.X, op=ALU.max), [Tp], [Tp])
                for c in range(4):
                    sel_chunk(c)
                    P.op("dve", lambda e, c=c: e.tensor_reduce(out=mxc[:, c:c + 1], in_=sch[:], axis=AX.X, op=ALU.max), [Tp], [Tp])
                P.op("dve", lambda e: e.tensor_reduce(out=gmx[:], in_=mxc[:, 0:5], axis=AX.X, op=ALU.max), [Tp], [Tp])
                ts("dve", gmx[:], gmx[:], -1.0, None, ALU.mult, None, [Tp], [Tp])
                tiles = []
                psO, psSum = PS[5], PS[6]
                first = True
                for c in range(4):
                    sel_chunk(c)
                    act(sch[:], sch[:], AF.Exp, [Tp], [Tp], bias=gmx[:, 0:1])
                    transposes(lambda t: sch[:, t * 128:(t + 1) * 128], 16, 128, PTb, Tp)
                    for par in range(2):
                        for t in range(16):
                            rhs = PTb[:, t, :].rearrange("p (j r q) -> p j r q", r=2, q=2)[:, :, :, par]
                            mm(psO.t[par * 64:(par + 1) * 64, 8 * c:8 * c + 8], vselp[:, c * 16 + t, g * 64:(g + 1) * 64], rhs,
                               t == 0, t == 15, [Tp, rowsV.tok], [psO.tok])
                    for t in range(16):
                        mm(psSum.t[:, 16 * c:16 * c + 16], onesb.t[:], PTb[:, t, :], t == 0, t == 15, [Tp, onesb.tok], [psSum.tok])
                act(snew[:], snew[:], AF.Exp, [Tp], [Tp], bias=gmx[:, 0:1])
                ptn = PS[4]
                tr(ptn.t[0:16, 0:16], snew[:], identf.t[0:16, 0:16], [Tp, identf.tok], [ptn.tok])
                cp("dve", PTn[:], ptn.t[0:16, 0:16], [ptn.tok], [Tp])
                for par in range(2):
                    rhs = PTn[:].rearrange("p (j r q) -> p j r q", r=2, q=2)[:, :, :, par]
                    mm(psO.t[par * 64:(par + 1) * 64, 32:40], A["S_v"].t[:, 0, g, :], rhs, True, True, [Tp, A["S_v"].tok], [psO.tok])
                mm(psSum.t[:, 64:80], onesb.t[0:16, :], PTn[:], True, True, [Tp, onesb.tok], [psSum.tok])
                cp("dve", otmp[:, 0:8], psO.t[:, 0:8], [psO.tok], [To])
                cp("dve", rsum[:], psSum.t[:, 0:16], [psSum.tok], [To])
                for k in range(1, 5):
                    tt("dve", otmp[:, 0:8], psO.t[:, 8 * k:8 * k + 8], otmp[:, 0:8], ALU.add, [psO.tok, To], [To])
                    tt("dve", rsum[:], psSum.t[:, 16 * k:16 * k + 16], rsum[:], ALU.add, [psSum.tok, To], [To])
                finish_from_sbuf(b, g, 1, otmp, rsum, finish_branch, grep, delt, gsb, oTs, To, Tcs, A, ones_f)
                pw_ = PS[0]
                mm(pw_.t[0:16, :], qop(b, g), kwT[g * 64:(g + 1) * 64, :], True, True, [A["qTs"].tok, Tw], [pw_.tok])
                tt("dve", swn[:], pw_.t[0:16, :], mwin[:], ALU.add, [pw_.tok, Tcs], [Tp])
                pn = PS[7]
                mm(pn.t[0:16, 0:16], qop(b, g), A["S_kT"].t[g * 64:(g + 1) * 64, 1, :], True, True, [A["qTs"].tok, A["S_kT"].tok], [pn.tok])
                tt("dve", snew[:], pn.t[0:16, 0:16], mnew[:, b, :], ALU.add, [pn.tok, Tcs], [Tp])
                P.op("dve", lambda e: e.tensor_reduce(out=mxc[:, 0:1], in_=swn[:], axis=AX.X, op=ALU.max), [Tp], [Tp])
                P.op("dve", lambda e: e.tensor_reduce(out=mxc[:, 1:2], in_=snew[:], axis=AX.X, op=ALU.max), [Tp], [Tp])
                P.op("dve", lambda e: e.tensor_reduce(out=gmx[:], in_=mxc[:, 0:2], axis=AX.X, op=ALU.max), [Tp], [Tp])
                ts("dve", gmx[:], gmx[:], -1.0, None, ALU.mult, None, [Tp], [Tp])
                act(swn[:], swn[:], AF.Exp, [Tp], [Tp], bias=gmx[:, 0:1])
                act(snew[:], snew[:], AF.Exp, [Tp], [Tp], bias=gmx[:, 0:1])
                transposes(lambda t: swn[:, t * 128:(t + 1) * 128], 4, 128, PTb, Tp)
                ptn = PS[4]
                tr(ptn.t[0:16, 256:272], snew[:], identf.t[0:16, 0:16], [Tp, identf.tok], [ptn.tok])
                cp("dve", PTn[:], ptn.t[0:16, 256:272], [ptn.tok], [Tp])
                tiles = [(vwS[:, t, g * 64:(g + 1) * 64], PTb[:, t, :]) for t in range(4)] + [(A["S_v"].t[:, 1, g, :], PTn[:])]
                pv_T(tiles, PS[5], PS[6])
                finish_branch(b, g, 2, PS[5], PS[6], False)
            zv = A["zsT"].t[:, :, TP + 4 * b:TP + 4 * b + 4]
            tt("dve", zv, zv, oTs[:].rearrange("p g r j -> p (g r) j"), ALU.mult, [To, A["zsT"].tok], [A["zsT"].tok])
        end_phase(m0)

    def finish_from_sbuf(b, g, branch, osb, ssb, finish_branch, grep, delt, gsb, oTs, To, Tcs, A, ones_f):
        gcol = branch * 8 + g * 4
        tt("dve", grep[:].rearrange("p (j h) -> p j h", h=4), delt[:, b, :].unsqueeze(2).to_broadcast([16, 4, 4]),
           A["gsig"].t[0:16, 16, gcol:gcol + 4].unsqueeze(1).to_broadcast([16, 4, 4]), ALU.mult, [Tcs, A["gsig"].tok], [To])
        pg_ = PS[7]
        mm(pg_.t[:, 0:16], ones_f[:], grep[:], True, True, [To, Tcs], [pg_.tok])
        P.op("dve", lambda e: e.reciprocal(out=ssb[:], in_=ssb[:]), [To], [To])
        tt("dve", gsb[:], ssb[:], pg_.t[:, 0:16], ALU.mult, [To, pg_.tok], [To])
        gv = gsb[:].rearrange("p (j r q) -> p j r q", r=2, q=2)
        for par in range(2):
            rows = slice(par * 64, par * 64 + 64)
            ov = osb[rows, 0:8].rearrange("p (j r) -> p r j", r=2)
            gvv = gv[rows, :, :, par].rearrange("p j r -> p r j")
            tt("dve", osb[rows, 0:8].rearrange("p (j r) -> p r j", r=2), ov, gvv, ALU.mult, [To], [To])
            tt("dve", oTs[rows, g, :, :], oTs[rows, g, :, :], osb[rows, 0:8].rearrange("p (j r) -> p r j", r=2), ALU.add, [To], [To])

    def merge_phase(l, hT, y5gT, onsaT):
        m0 = mem.mark()
        mg = Buf(mem.sb("mg", [128, 8, NT], BF16), "mg", 5)
        wv = D["w_in"][l].rearrange("(kc p) n -> p kc n", p=128)
        w5v = D["w_s5_out"][l].rearrange("(kc p) n -> p kc n", p=128)
        wnv = D["w_nsa_out"][l].rearrange("(kc p) n -> p kc n", p=128)
        wov = D["w_o"][l].rearrange("(kc p) n -> p kc n", p=128)
        wms = Buf(mem.sb("wms", [128, 8, 256], BF16), "wms")
        wmn = Buf(mem.sb("wmn", [128, 8, 256], BF16), "wmn")
        w5 = Buf(mem.sb("w5", [128, 4, 256], BF16), "w5")
        wn = Buf(mem.sb("wn", [128, 4, 256], BF16), "wn")
        wo = Buf(mem.sb("wo", [128, 8, 256], BF16), "wo")
        sg1s = [Buf(mem.sb(f"sg1_{i}", [128, 512], F32), f"sg1_{i}") for i in range(2)]
        sg2s = [Buf(mem.sb(f"sg2_{i}", [128, 512], F32), f"sg2_{i}") for i in range(2)]
        sg1, sg2 = sg1s[0], sg2s[0]
        it_ = 0
        for grp in range(4):
            c0g = grp * 256
            for kc in range(8):
                ld("pool", "w", wms.t[:, kc, :], wv[:, kc, MG_OFF + c0g:MG_OFF + c0g + 256], [wms.tok])
                ld("pool", "w", wmn.t[:, kc, :], wv[:, kc, MG_OFF + 1024 + c0g:MG_OFF + 1024 + c0g + 256], [wmn.tok])
            for kc in range(4):
                ld("pool", "w", w5.t[:, kc, :], w5v[:, kc, c0g:c0g + 256], [w5.tok])
                ld("pool", "w", wn.t[:, kc, :], wnv[:, kc, c0g:c0g + 256], [wn.tok])
            for o2 in range(2):
                oc = grp * 2 + o2
                cs = slice(o2 * 128, (o2 + 1) * 128)
                for b_, (c0, c1) in enumerate(BANKS):
                    w = c1 - c0
                    o4 = 4 * (it_ % 2)
                    p1, p2, p3, p4 = PS[o4], PS[o4 + 1], PS[o4 + 2], PS[o4 + 3]
                    sg1, sg2 = sg1s[it_ % 2], sg2s[it_ % 2]
                    it_ += 1
                    for kc in range(8):
                        mm(p1.t[:, 0:w], wms.t[:, kc, cs], hT.t[:, kc, c0:c1], kc == 0, kc == 7, [wms.tok, hT.toks[b_]], [p1.tok])
                    for kc in range(8):
                        mm(p2.t[:, 0:w], wmn.t[:, kc, cs], hT.t[:, kc, c0:c1], kc == 0, kc == 7, [wmn.tok, hT.toks[b_]], [p2.tok])
                    for kc in range(4):
                        mm(p3.t[:, 0:w], w5.t[:, kc, cs], y5gT.t[:, kc, c0:c1], kc == 0, kc == 3, [w5.tok, y5gT.tok], [p3.tok])
                    for kc in range(4):
                        mm(p4.t[:, 0:w], wn.t[:, kc, cs], onsaT.t[:, kc, c0:c1], kc == 0, kc == 3, [wn.tok, onsaT.tok], [p4.tok])
                    act(sg1.t[:, 0:w], p1.t[:, 0:w], AF.Sigmoid, [p1.tok], [sg1.tok])
                    act(sg2.t[:, 0:w], p2.t[:, 0:w], AF.Sigmoid, [p2.tok], [sg2.tok])
                    tt("dve", sg1.t[:, 0:w], sg1.t[:, 0:w], p3.t[:, 0:w], ALU.mult, [sg1.tok, p3.tok], [sg1.tok])
                    tt("dve", sg2.t[:, 0:w], sg2.t[:, 0:w], p4.t[:, 0:w], ALU.mult, [sg2.tok, p4.tok], [sg2.tok])
                    tt("dve", mg.t[:, oc, c0:c1], sg1.t[:, 0:w], sg2.t[:, 0:w], ALU.add, [sg1.tok, sg2.tok], [mg.toks[b_]])
        for grp in range(4):
            c0g = grp * 256
            for kc in range(8):
                ld("pool", "w", wo.t[:, kc, :], wov[:, kc, c0g:c0g + 256], [wo.tok])
            for o2 in range(2):
                oc = grp * 2 + o2
                cs = slice(o2 * 128, (o2 + 1) * 128)
                for b_, (c0, c1) in enumerate(BANKS):
                    w = c1 - c0
                    ps = PS[4 + (b_ % 2)]
                    for kc in range(8):
                        mm(ps.t[:, 0:w], wo.t[:, kc, cs], mg.t[:, kc, c0:c1], kc == 0, kc == 7, [wo.tok, mg.toks[b_]], [ps.tok])
                    if b_ < 4:
                        stt(xT.t[:, oc, c0:c1], ps.t[:, 0:w], modT.t[:, 16 + oc, 0:1], xT.t[:, oc, c0:c1], ALU.mult, ALU.add,
                            [ps.tok, modT.tok, xT.toks[b_]], [xT.toks[b_]])
                    else:
                        tt("dve", sg1.t[:, 0:w], ps.t[:, 0:w], gtS.t[:, oc, :], ALU.mult, [ps.tok, gtS.tok], [sg1.tok])
                        tt("dve", xT.t[:, oc, c0:c1], xT.t[:, oc, c0:c1], sg1.t[:, 0:w], ALU.add, [sg1.tok, xT.toks[b_]], [xT.toks[b_]])
        end_phase(m0)

    def layer(l):
        if stop_after == "pro":
            return
        adaln_phase(l)
        tap("modT", modT.t[:], [modT.tok])
        if stop_after == "ada":
            return
        mL = mem.mark()
        uT = Buf(mem.sb("uT", [128, 4, NT], BF16), "uT")
        zs5T = Buf(mem.sb("zs5T", [128, 4, NT], BF16), "zs5T")
        m1 = mem.mark()
        hT = Buf(mem.sb("hT", [128, 8, NT], BF16), "hT", 5)
        norm_phase(hT)
        tap("hT", hT.t[:], hT.toks)
        if stop_after == "norm":
            return
        uz_proj_phase(l, hT, uT, zs5T)
        end_phase(m1)
        tap("uT", uT.t[:], [uT.tok])
        if stop_after == "uz":
            return
        s5_phase(l, uT, zs5T)
        if stop_after == "s5":
            return
        glu_phase(l, uT, zs5T)
        tap("y5g", uT.t[:], [uT.tok])
        if stop_after == "glu":
            raise _Stop()
        A = {}
        mA = mem.mark()
        A["qTs"] = Buf(mem.sb("qTs", [128, 4, 16], BF16), "qTs")
        A["S_kT"] = Buf(mem.sb("S_kT", [128, 2, 16], BF16), "S_kT")
        A["S_v"] = Buf(mem.sb("S_v", [16, 2, 2, 64], BF16), "S_v")
        A["gsig"] = Buf(mem.sb("gsig", [128, 17, 24], F32), "gsig")
        mB = mem.mark()
        A["qT"] = Buf(mem.sb("qT", [128, 4, TP], BF16), "qT")
        A["KT3"] = Buf(mem.sb("KT3", [128, 3, 2112], BF16), "KT3")
        A["kwinT"] = Buf(mem.sb("kwinT", [128, NT], BF16), "kwinT")
        A["vsel"] = Buf(mem.sb("vsel", [128, 17, 2, 65], BF16), "vsel")
        A["vwin"] = Buf(mem.sb("vwin", [128, 17, 2, 65], BF16), "vwin")
        A["zsT"] = zs5T
        ms("pool", A["vsel"].t[:], 1.0, [A["vsel"].tok])
        ms("pool", A["vwin"].t[:], 1.0, [A["vwin"].tok])
        m2 = mem.mark()
        hT = Buf(mem.sb("hT", [128, 8, NT], BF16), "hT", 5)
        norm_phase(hT)
        tm_proj_phase(l, hT, A)
        end_phase(m2)
        for n_ in ("qT", "KT3", "kwinT", "zsT", "gsig", "vsel", "vwin", "qTs"):
            tap(n_, A[n_].t[:], [A[n_].tok])
        if stop_after == "tm":
            raise _Stop()
        attn_prompt_phase(l, A)
        tap("onsaT", A["zsT"].t[:], [A["zsT"].tok])
        if stop_after == "ap":
            raise _Stop()
        end_phase(mB)
        attn_sample_phase(l, A)
        tap("onsaTs", A["zsT"].t[:, :, TP:NT], [A["zsT"].tok])
        if stop_after == "as":
            raise _Stop()
        end_phase(mA)
        m3 = mem.mark()
        hT = Buf(mem.sb("hT", [128, 8, NT], BF16), "hT", 5)
        norm_phase(hT)
        merge_phase(l, hT, uT, A["zsT"])
        end_phase(m3)
        tap(f"x{l}", xT.t[:], xT.toks)
        end_phase(mL)

    def final_phase():
        m0 = mem.mark()
        sq = Buf(mem.sb("sq", [128, 8, 256], BF16), "sq")
        xn = Buf(mem.sb("xn", [128, 8, 256], F32), "xn")
        rt = Buf(mem.sb("rt", [128, 256], F32), "rt")
        yo = [Buf(mem.sb(f"yo{i}", [128, 8, 256], F32), f"yo{i}") for i in range(2)]
        fg = Buf(mem.sb("fg", [128, 8], F32), "fg")
        ld("sp", "c", fg.t[:], D["final_gT"], [fg.tok])
        chunks = [(i * 256, i * 256 + 256) for i in range(8)] + [(TP, NT)]
        for ci, (c0, c1) in enumerate(chunks):
            w = c1 - c0
            b_ = min(c0 // 512, 4)
            ps = PS[1 + (ci % 2)]
            y_ = yo[ci % 2]
            act(sq.t[:, :, 0:w], xT.t[:, :, c0:c1], AF.Square, [xT.toks[b_]], [sq.tok])
            for kc in range(8):
                mm(ps.t[:, 0:w], onesb.t[:], sq.t[:, kc, 0:w], kc == 0, kc == 7, [onesb.tok, sq.tok], [ps.tok])
            act(rt.t[:, 0:w], ps.t[:, 0:w], AF.Sqrt, [ps.tok], [rt.tok], bias=1e-6, scale=1.0 / 1024.0)
            P.op("dve", lambda e, w=w: e.reciprocal(out=rt.t[:, 0:w], in_=rt.t[:, 0:w]), [rt.tok], [rt.tok])
            tt("dve", xn.t[:, :, 0:w], xT.t[:, :, c0:c1], rt.t[:, 0:w].unsqueeze(1).to_broadcast([128, 8, w]),
               ALU.mult, [xT.toks[b_], rt.tok], [xn.tok])
            tt("dve", y_.t[:, :, 0:w], xn.t[:, :, 0:w], fg.t[:].unsqueeze(2).to_broadcast([128, 8, w]), ALU.mult,
               [xn.tok, fg.tok], [y_.tok])
            P.dma("sp", "y", lambda e, y_=y_, c0=c0, c1=c1, w=w: e.dma_start(out=O["yT"][:, :, c0:c1], in_=y_.t[:, :, 0:w]),
                  [y_.tok], [out_tok])
        end_phase(m0)

    try:
        for l in range(nlayers):
            layer(l)
        final_phase()
    except _Stop:
        pass

    print('OPCOUNTS', P.cnt, P.dcnt)
    P.barrier()
    P.emit()
    P.close()
    mem.release(0)
    return nc


_NC_CACHE = {}


def _shared_inputs(inp):
    m = {}
    m["ident"] = np.eye(128, dtype=np.float32)
    m["ada_w"] = np.ascontiguousarray(inp["ada_w"], np.float32)
    m["ada_bT"] = np.ascontiguousarray(inp["ada_b"].reshape(DEPTH, 24, 128).transpose(0, 2, 1))
    m["norm_gT"] = np.ascontiguousarray(inp["norm_g"].reshape(DEPTH, 8, 128).transpose(0, 2, 1))
    m["final_gT"] = np.ascontiguousarray(inp["final_g"].reshape(8, 128).T)
    m["w_in"] = np.ascontiguousarray(inp["w_in"], np.float32)
    aNL, aPR, BPR, BNL, CNL, dv = _s5_layouts(inp)
    m.update(aNL=aNL, aPR=aPR, BPR=BPR, BNL=BNL, CNL=CNL, dv=dv)
    m["glu_w"] = np.ascontiguousarray(inp["s5_glu_w"], np.float32)
    m["glu_bT"] = np.ascontiguousarray(inp["s5_glu_b"].reshape(DEPTH, 4, 128).transpose(0, 2, 1))
    m["w_s5_out"] = np.ascontiguousarray(inp["w_s5_out"], np.float32)
    m["w_nsa_out"] = np.ascontiguousarray(inp["w_nsa_out"], np.float32)
    m["w_o"] = np.ascontiguousarray(inp["w_o"], np.float32)
    cos, sin, cosC, sinC = _rope_tables()
    m["cosT"] = cos
    m["sinT"] = sin
    m["ropeC"] = np.ascontiguousarray(np.stack([cosC, sinC], 1))
    cm, E, tri = _attn_consts()
    m["cm"] = cm
    m["Ecst"] = E
    m["tri"] = tri
    W1bd, W2bd, peT = _cmp_layouts(inp)
    m["W1bd"] = W1bd
    m["W2bd"] = W2bd
    m["peT"] = peT
    npool = inp["cache_kv"].shape[0]
    m["cache2"] = np.ascontiguousarray(inp["cache_kv"], np.float32).reshape(npool * DEPTH * 128 * 2, 256)
    m["iot2"] = (2 * np.arange(128, dtype=np.float32)).reshape(128, 1)
    sel, Smat, masknew, delta, maskwin = _sample_consts()
    m.update(s_sel=sel, s_smat=Smat, s_mnew=masknew, s_delta=delta, s_mwin=maskwin)
    return m


def _core_inputs(inp, c, shared):
    m = dict(shared)
    xp = inp["x_prompt"][c]
    xs = inp["x_sample"][4 * c:4 * c + 4].reshape(16, 1024)
    x = np.concatenate([xp, xs], 0)
    m["xT0"] = np.ascontiguousarray(x.T.reshape(8, 128, NT).transpose(1, 0, 2))
    cc = np.concatenate([inp["c_prompt"][c:c + 1], inp["c_sample"][4 * c:4 * c + 4]], 0)
    m["cT"] = np.ascontiguousarray(cc.T.reshape(8, 128, 5).transpose(1, 0, 2))
    m["h0NL"] = _h0_layout(inp["state_ssm"][4 * c:4 * c + 4])
    m["swin"] = np.ascontiguousarray(inp["state_win"][4 * c:4 * c + 4].reshape(4, DEPTH, 512, 256))
    m["ptab"] = np.ascontiguousarray(inp["page_table"][4 * c:4 * c + 4].astype(np.int32))
    return m


def _assemble(results, ncores):
    B, DB = ncores, 4 * ncores
    y_p = np.zeros((B, TP, 1024), np.float32)
    y_s = np.zeros((DB, 4, 1024), np.float32)
    kv_p = np.zeros((B, DEPTH, TP, 4, 2, 64), np.float32)
    kv_s = np.zeros((DB, DEPTH, 4, 4, 2, 64), np.float32)
    win_p = np.zeros((B, DEPTH, 512, 2, 2, 64), np.float32)
    win_s = np.zeros((DB, DEPTH, 512, 2, 2, 64), np.float32)
    ssm_p = np.zeros((B, DEPTH, 2, 32, 64), np.float32)
    ssm_s = np.zeros((DB, DEPTH, 2, 32, 64), np.float32)
    for c, r in enumerate(results):
        yT = np.asarray(r["yT"])
        y = yT.transpose(2, 1, 0).reshape(NT, 1024)
        y_p[c] = y[:TP]
        y_s[4 * c:4 * c + 4] = y[TP:].reshape(4, 4, 1024)
        kv_p[c] = np.asarray(r["kvP"]).reshape(DEPTH, TP, 4, 2, 64)
        kv_s[4 * c:4 * c + 4] = np.asarray(r["kvS"]).reshape(DEPTH, 4, 4, 4, 2, 64).transpose(1, 0, 2, 3, 4, 5)
        win_p[c] = np.asarray(r["winP"]).reshape(DEPTH, 512, 2, 2, 64)
        win_s[4 * c:4 * c + 4] = np.asarray(r["winS"]).reshape(4, DEPTH, 512, 2, 2, 64)
        hlP = np.asarray(r["hlP"])
        hlS = np.asarray(r["hlS"])
        for q in range(4):
            for pb in range(4):
                for gl in range(2):
                    g = 8 * q + 2 * pb + gl
                    ssm_p[c, :, :, g, :] = hlP[:, gl * 64:gl * 64 + 64, :, q * 4 + pb].transpose(0, 2, 1)
                    ssm_s[4 * c:4 * c + 4, :, :, g, :] = hlS[:, gl * 64:gl * 64 + 64, :, q * 4 + pb, :].transpose(3, 0, 2, 1)
    return (y_p, y_s, kv_p, kv_s, win_p, win_s, ssm_p, ssm_s)


def kernel(**inputs):
    inp = {k: np.asarray(v) for k, v in inputs.items()}
    ncores = inp["x_prompt"].shape[0]
    npool = int(inp["cache_kv"].shape[0])
    key = (npool,)
    if key not in _NC_CACHE:
        _NC_CACHE[key] = build({"npool": npool})
    nc = _NC_CACHE[key]
    shared = _shared_inputs(inp)
    in_maps = [_core_inputs(inp, c, shared) for c in range(ncores)]
    res = run_bass_kernel_spmd(nc, in_maps, core_ids=list(range(ncores)))
    return _assemble(res.results, ncores)
```

```python
import numpy as np
import concourse.bass as bass
import concourse.mybir as mybir
from concourse.bass_utils import run_bass_kernel_spmd

F32 = mybir.dt.float32
BF16 = mybir.dt.bfloat16
I32 = mybir.dt.int32
AF = mybir.ActivationFunctionType
ALU = mybir.AluOpType
AX = mybir.AxisListType

NT = 2064
TP = 2048
DEPTH = 4
LCH = 16
NCH = TP // LCH
NEG = -30000.0
TWO_PI = float(2 * np.pi)


class Tok:
    __slots__ = ("name", "w", "r", "excl")

    def __init__(self, name="", excl=False):
        self.name = name
        self.w = None
        self.r = {}
        self.excl = excl


class Prog:
    ENG = ("pe", "act", "dve", "pool", "sp")

    def __init__(self, nc):
        self.nc = nc
        self.ops = {e: [] for e in self.ENG}
        self.cnt = {e: 0 for e in self.ENG}
        self.seen = {e: {} for e in self.ENG}
        self.sems = {}
        self.dcnt = {}
        self._stack = []
        self.nops = 0

    def sem(self, name):
        if name not in self.sems:
            cm = self.nc.semaphore(name)
            h = cm.__enter__()
            self._stack.append(cm)
            self.sems[name] = h
        return self.sems[name]

    def _wait(self, eng, ev):
        sname, val = ev
        if self.seen[eng].get(sname, 0) >= val:
            return
        self.seen[eng][sname] = val
        h = self.sem(sname)
        self.ops[eng].append(lambda e, h=h, val=val: e.wait_ge(h, val))

    def _deps(self, eng, reads, writes, pe_accum=False):
        deps = {}

        def add(ev):
            if ev is None:
                return
            s, v = ev
            if deps.get(s, 0) < v:
                deps[s] = v
        for t in reads:
            add(t.w)
            if t.excl:
                for s_, v_ in t.r.items():
                    if s_ != "e_" + eng:
                        add((s_, v_))
        for t in writes:
            if not (pe_accum and t.w is not None and t.w[0] == "e_pe"):
                add(t.w)
            for s, v in t.r.items():
                if pe_accum and s == "e_pe":
                    continue
                add((s, v))
        for s, v in deps.items():
            self._wait(eng, (s, v))

    def _mark(self, ev, reads, writes):
        s, v = ev
        for t in reads:
            if t.r.get(s, 0) < v:
                t.r[s] = v
        for t in writes:
            t.w = ev
            t.r = {}

    def op(self, eng, fn, reads=(), writes=(), pe_accum=False):
        self._deps(eng, reads, writes, pe_accum)
        sname = "e_" + eng
        h = self.sem(sname)
        self.cnt[eng] += 1
        ev = (sname, self.cnt[eng])
        self.ops[eng].append(lambda e, fn=fn, h=h: fn(e).then_inc(h, 1))
        self._mark(ev, reads, writes)
        self.nops += 1
        return ev

    NDSEM = 40

    def dma(self, q, key, fn, reads=(), writes=()):
        self._deps(q, reads, writes)
        i = getattr(self, "_dnext", 0)
        self._dnext = (i + 1) % self.NDSEM
        sname = f"d{i}"
        prev = self.dcnt.get(sname, 0)
        if prev:
            self._wait(q, (sname, prev))
        h = self.sem(sname)
        self.dcnt[sname] = prev + 16
        ev = (sname, self.dcnt[sname])
        self.ops[q].append(lambda e, fn=fn, h=h: fn(e).then_inc(h, 16))
        self._mark(ev, reads, writes)
        self.nops += 1
        return ev

    def wait_all(self, eng):
        for e in self.ENG:
            if self.cnt[e]:
                self._wait(eng, ("e_" + e, self.cnt[e]))
        for s, v in self.dcnt.items():
            self._wait(eng, (s, v))

    def barrier(self):
        for e in self.ENG:
            self.wait_all(e)

    def emit(self):
        nc = self.nc
        with nc.Block() as block:
            @block.tensor
            def _(e):
                for f in self.ops["pe"]:
                    f(e)

            @block.scalar
            def _(e):
                for f in self.ops["act"]:
                    f(e)

            @block.vector
            def _(e):
                for f in self.ops["dve"]:
                    f(e)

            @block.gpsimd
            def _(e):
                for f in self.ops["pool"]:
                    f(e)

            @block.sync
            def _(e):
                for f in self.ops["sp"]:
                    f(e)

    def close(self):
        while self._stack:
            self._stack.pop().__exit__(None, None, None)


class Mem:
    def __init__(self, nc):
        self.nc = nc
        self.stack = []
        self.n = 0

    def sb(self, name, shape, dt):
        self.n += 1
        cm = self.nc.sbuf_tensor(f"s{self.n}_{name}", list(shape), dt)
        t = cm.__enter__()
        self.stack.append(cm)
        return t

    def ps(self, name, shape, dt=F32):
        self.n += 1
        cm = self.nc.psum_tensor(f"p{self.n}_{name}", list(shape), dt)
        t = cm.__enter__()
        self.stack.append(cm)
        return t

    def mark(self):
        return len(self.stack)

    def release(self, mark):
        while len(self.stack) > mark:
            self.stack.pop().__exit__(None, None, None)


def _s5_layouts(inp):
    L = DEPTH
    a_re, a_im, ldt = inp["s5_a_re"], inp["s5_a_im"], inp["s5_log_dt"]
    b = np.stack([inp["s5_b_re"], inp["s5_b_im"]], 1)
    c = np.stack([inp["s5_c_re"], inp["s5_c_im"]], 1)
    aNL = np.zeros((L, 128, 3, 16), np.float32)
    aPR = np.zeros((L, 128, 4, 3, 128), np.float32)
    BPR = np.zeros((L, 128, 4, 2, 128), np.float32)
    BNL = np.zeros((L, 128, 4, 2, 128), np.float32)
    CNL = np.zeros((L, 128, 4, 2, 4, 32), np.float32)
    dv = np.zeros((L, 128, 4), np.float32)
    for q in range(4):
        for pb in range(4):
            for gl in range(2):
                g = 8 * q + 2 * pb + gl
                sl = slice(gl * 64, gl * 64 + 64)
                aNL[:, sl, 0, q * 4 + pb] = a_re[:, g]
                aNL[:, sl, 1, q * 4 + pb] = a_im[:, g]
                aNL[:, sl, 2, q * 4 + pb] = ldt[:, g][:, None]
                rows = slice(pb * 32, pb * 32 + 32)
                aPR[:, rows, q, 0, sl] = a_re[:, g][:, None, :]
                aPR[:, rows, q, 1, sl] = a_im[:, g][:, None, :]
                aPR[:, rows, q, 2, sl] = ldt[:, g][:, None, None]
                r16 = slice(pb * 32 + gl * 16, pb * 32 + gl * 16 + 16)
                BPR[:, r16, q, :, sl] = np.transpose(b[:, :, g], (0, 3, 1, 2))
                BNL[:, sl, q, :, r16] = np.transpose(b[:, :, g], (0, 2, 1, 3))
                CNL[:, sl, q, :, pb, gl * 16:gl * 16 + 16] = np.transpose(c[:, :, g], (0, 3, 1, 2))
        dv[:, :, q] = inp["s5_d"][:, 8 * q:8 * q + 8].reshape(L, 128)
    return aNL, aPR, BPR, BNL, CNL, dv


def _h0_layout(state_ssm4):
    L = DEPTH
    out = np.zeros((L, 128, 2, 16, 4), np.float32)
    for q in range(4):
        for pb in range(4):
            for gl in range(2):
                g = 8 * q + 2 * pb + gl
                out[:, gl * 64:gl * 64 + 64, :, q * 4 + pb, :] = np.transpose(state_ssm4[:, :, :, g, :], (1, 3, 2, 0))
    return out


def _rope_tables():
    inv = (500000.0 ** (-np.arange(8, dtype=np.float32) / 8)).astype(np.float32)
    pos = np.concatenate([np.arange(2048), 8192 + (np.arange(128) % 4)]).astype(np.float32)
    ang = pos[:, None] * inv[None, :]
    cos = np.cos(ang).astype(np.float32).reshape(17, 128, 8).transpose(1, 0, 2)
    sin = np.sin(ang).astype(np.float32).reshape(17, 128, 8).transpose(1, 0, 2)
    cpos = (np.arange(128) * 64 + 63).astype(np.float32)
    angc = cpos[None, :] * inv[:, None]
    cosC = np.ones((128, 128), np.float32)
    sinC = np.zeros((128, 128), np.float32)
    for g in range(2):
        cosC[g * 64:g * 64 + 8] = np.cos(angc)
        cosC[g * 64 + 8:g * 64 + 16] = np.cos(angc)
        sinC[g * 64:g * 64 + 8] = -np.sin(angc)
        sinC[g * 64 + 8:g * 64 + 16] = np.sin(angc)
    return np.ascontiguousarray(cos), np.ascontiguousarray(sin), cosC, sinC


def _attn_consts():
    t = np.arange(2048)
    n = np.arange(32)
    valid = (64 * n[None, :] + 63) <= t[:, None]
    maskC = np.where(valid, 0.0, NEG).astype(np.float32)
    mask01 = valid.astype(np.float32)
    qblk = t // 64
    f0 = (n[None, :] == 0)
    f1 = (n[None, :] == qblk[:, None])
    f2 = (n[None, :] == qblk[:, None] - 1)
    fut = n[None, :] > qblk[:, None]
    forced = f0 | f1 | f2
    selA = (~(forced | fut)).astype(np.float32)
    selB = np.maximum(np.maximum(f0 * 3e4, f1 * 2e4), f2 * 1e4).astype(np.float32) - fut.astype(np.float32)
    tm = lambda a: np.ascontiguousarray(a.reshape(16, 128, 32).transpose(1, 0, 2))
    cm = np.stack([tm(maskC), tm(mask01), tm(selA), tm(selB)], 1)
    E = np.zeros((128, 16, 128), np.float32)
    for kt in range(16):
        for key in range(128):
            nn = 2 * kt + key // 64
            E[nn, kt, key] = 1.0
            E[64 + nn, kt, key] = 1.0
    kl = np.arange(128)[:, None]
    tl = np.arange(128)[None, :]
    triC = np.where(kl > tl, NEG, 0.0).astype(np.float32)
    triW = np.where(kl <= tl, NEG, 0.0).astype(np.float32)
    return cm, E, np.stack([triC, triW], 1)


def _sample_consts():
    sel = np.zeros((16, 2, 129), np.float32)
    sel[:, 0, :] = 1.0
    for c, v in ((0, 3e4), (127, 1e4), (128, 2e4)):
        sel[:, 0, c] = 0.0
        sel[:, 1, c] = v
    r = np.arange(16)
    Smat = (r[:, None] // 4 == r[None, :] // 4).astype(np.float32)
    masknew = np.full((16, 4, 16), NEG, np.float32)
    delta = np.zeros((16, 4, 4), np.float32)
    for b in range(4):
        for rr in range(16):
            j = rr // 4
            for jp in range(j + 1):
                masknew[rr, b, 4 * b + jp] = 0.0
        for j in range(4):
            delta[4 * b + j, b, j] = 1.0
    i = np.arange(512)
    maskwin = np.where(i[None, :] <= (r[:, None] // 4), NEG, 0.0).astype(np.float32)
    return sel, Smat, masknew, delta, maskwin


def _cmp_layouts(inp):
    L = DEPTH
    w1 = inp["cmp_w1"].reshape(L, 2, 64, 64, 64)
    W1bd = np.zeros((L, 2, 128, 64, 128), np.float32)
    W2bd = np.zeros((L, 3, 128, 128), np.float32)
    peT = np.zeros((L, 2, 128, 64), np.float32)
    perm = np.arange(64)
    perm[0:8] = np.arange(8, 16)
    perm[8:16] = np.arange(0, 8)
    for g in range(2):
        sl = slice(g * 64, g * 64 + 64)
        W1bd[:, :, sl, :, sl] = np.transpose(w1, (0, 1, 3, 2, 4))
        W2bd[:, 0, sl, sl] = inp["cmp_w2"][:, 0]
        W2bd[:, 1, sl, sl] = inp["cmp_w2"][:, 1]
        w2p = inp["cmp_w2"][:, 0][:, :, perm].copy()
        w2p[:, :, 16:] = 0.0
        W2bd[:, 2, sl, sl] = w2p
        peT[:, :, sl, :] = np.transpose(inp["cmp_pe"], (0, 1, 3, 2))
    return W1bd, W2bd, peT


class _Stop(Exception):
    pass


class Buf:
    def __init__(self, t, name, ntok=1):
        self.t = t
        self.tok = Tok(name)
        self.toks = [Tok(f"{name}{i}") for i in range(ntok)] if ntok > 1 else [self.tok]


def build(cfg=None):
    cfg = cfg or {}
    nlayers = cfg.get("nlayers", DEPTH)
    taps = cfg.get("taps", {})
    stop_after = cfg.get("stop_after", None)
    nc = bass.Bass("TRN2", target_bir_lowering=False)
    P = Prog(nc)
    mem = Mem(nc)

    def din(name, shape, dt=F32):
        return nc.dram_tensor(name, list(shape), dt, kind="ExternalInput").ap()

    def dout(name, shape, dt=F32):
        return nc.dram_tensor(name, list(shape), dt, kind="ExternalOutput").ap()

    D = {}
    D["xT0"] = din("xT0", [128, 8, NT])
    D["cT"] = din("cT", [128, 8, 5])
    D["ident"] = din("ident", [128, 128])
    D["ada_w"] = din("ada_w", [DEPTH, 1024, 3072])
    D["ada_bT"] = din("ada_bT", [DEPTH, 128, 24])
    D["norm_gT"] = din("norm_gT", [DEPTH, 128, 8])
    D["final_gT"] = din("final_gT", [128, 8])
    D["w_in"] = din("w_in", [DEPTH, 1024, 4888])
    D["aNL"] = din("aNL", [DEPTH, 128, 3, 16])
    D["aPR"] = din("aPR", [DEPTH, 128, 4, 3, 128])
    D["BPR"] = din("BPR", [DEPTH, 128, 4, 2, 128])
    D["BNL"] = din("BNL", [DEPTH, 128, 4, 2, 128])
    D["CNL"] = din("CNL", [DEPTH, 128, 4, 2, 4, 32])
    D["dv"] = din("dv", [DEPTH, 128, 4])
    D["h0NL"] = din("h0NL", [DEPTH, 128, 2, 16, 4])
    D["glu_w"] = din("glu_w", [DEPTH, 512, 512])
    D["glu_bT"] = din("glu_bT", [DEPTH, 128, 4])
    D["w_s5_out"] = din("w_s5_out", [DEPTH, 512, 1024])
    D["w_nsa_out"] = din("w_nsa_out", [DEPTH, 512, 1024])
    D["w_o"] = din("w_o", [DEPTH, 1024, 1024])
    D["cosT"] = din("cosT", [128, 17, 8])
    D["sinT"] = din("sinT", [128, 17, 8])
    D["swin"] = din("swin", [4, DEPTH, 512, 256])
    D["cm"] = din("cm", [128, 4, 16, 32])
    D["Ecst"] = din("Ecst", [128, 16, 128])
    D["tri"] = din("tri", [128, 2, 128])
    D["ropeC"] = din("ropeC", [128, 2, 128])
    D["W1bd"] = din("W1bd", [DEPTH, 2, 128, 64, 128])
    D["W2bd"] = din("W2bd", [DEPTH, 3, 128, 128])
    D["peT"] = din("peT", [DEPTH, 2, 128, 64])
    D["cache2"] = din("cache2", [cfg.get("npool", 2560) * DEPTH * 128 * 2, 256])
    D["ptab"] = din("ptab", [4, 64], I32)
    D["iot2"] = din("iot2", [128, 1])
    D["s_sel"] = din("s_sel", [16, 2, 129])
    D["s_smat"] = din("s_smat", [16, 16])
    D["s_mnew"] = din("s_mnew", [16, 4, 16])
    D["s_delta"] = din("s_delta", [16, 4, 4])
    D["s_mwin"] = din("s_mwin", [16, 512])
    O = {}
    O["kvP"] = dout("kvP", [DEPTH, TP, 512])
    O["kvS"] = dout("kvS", [DEPTH, 16, 512])
    O["winP"] = dout("winP", [DEPTH, 512, 256])
    O["winS"] = dout("winS", [4, DEPTH, 512, 256])
    O["yT"] = dout("yT", [128, 8, NT])
    O["hlP"] = dout("hlP", [DEPTH, 128, 2, 16])
    O["hlS"] = dout("hlS", [DEPTH, 128, 2, 16, 4])
    TAP = {n: dout("tap_" + n, shp[0], BF16 if shp[1] == "bf16" else F32) for n, shp in taps.items()}
    out_tok = Tok("out")

    def chk(name):
        if stop_after == name:
            raise _Stop()

    def tt(eng, out, in0, in1, op, R, W):
        return P.op(eng, lambda e: e.tensor_tensor(out=out, in0=in0, in1=in1, op=op), R, W)

    def ts(eng, out, in0, s1, s2, op0, op1, R, W):
        if op1 is None:
            return P.op(eng, lambda e: e.tensor_scalar(out=out, in0=in0, scalar1=s1, scalar2=None, op0=op0), R, W)
        return P.op(eng, lambda e: e.tensor_scalar(out=out, in0=in0, scalar1=s1, scalar2=s2, op0=op0, op1=op1), R, W)

    def stt(out, in0, sc, in1, op0, op1, R, W):
        return P.op("dve", lambda e: e.scalar_tensor_tensor(out=out, in0=in0, scalar=sc, in1=in1, op0=op0, op1=op1), R, W)

    def cp(eng, out, in_, R, W):
        return P.op(eng, lambda e: e.tensor_copy(out=out, in_=in_), R, W)

    def act(out, in_, func, R, W, bias=None, scale=None):
        kw = {}
        if bias is not None:
            kw["bias"] = bias
        if scale is not None:
            kw["scale"] = scale
        return P.op("act", lambda e: e.activation(out=out, in_=in_, func=func, **kw), R, W)

    pe_mode = [None]

    def _rnd(v):
        return 32 if v <= 32 else (64 if v <= 64 else 128)

    def pe_drain_if(mode):
        if pe_mode[0] is not None and pe_mode[0] != mode and P.cnt["pe"]:
            P._wait("pe", ("e_pe", P.cnt["pe"]))
        pe_mode[0] = mode

    def mm(out, lhsT, rhs, start, stop, R, W):
        shp = list(lhsT.shape)
        kt_ = _rnd(shp[0])
        pe_drain_if((kt_, _rnd(int(np.prod(shp[1:]))), lhsT.start_partition() if kt_ < 128 else 0))
        return P.op("pe", lambda e: e.matmul(out, lhsT=lhsT, rhs=rhs, start=start, stop=stop), R, W, pe_accum=True)

    def tr(out, in_, ident, R, W):
        shp = list(in_.shape)
        pe_drain_if((_rnd(shp[0]), _rnd(int(np.prod(shp[1:])))))
        return P.op("pe", lambda e: e.transpose(out=out, in_=in_, identity=ident), R, W, pe_accum=True)

    def ms(eng, ap, val, W):
        return P.op(eng, lambda e: e.memset(ap, val), (), W)

    def ld(q, key, out, in_, W, R=()):
        return P.dma(q, key, lambda e: e.dma_start(out=out, in_=in_), R, W)

    def tap(name, ap, R):
        if name in TAP:
            P.dma("sp", "tap", lambda e: e.dma_start(out=TAP[name], in_=ap), R, [out_tok])

    def cmul(eng, ore, oim, xre, xim, yre, yim, t1, t2, R, W, Tt):
        tt(eng, t1, xre, yre, ALU.mult, R, [Tt])
        tt(eng, t2, xim, yim, ALU.mult, R, [Tt])
        tt(eng, ore, t1, t2, ALU.subtract, [Tt], W)
        tt(eng, t1, xre, yim, ALU.mult, R, [Tt])
        tt(eng, t2, xim, yre, ALU.mult, R, [Tt])
        tt(eng, oim, t1, t2, ALU.add, [Tt], W)

    xT = Buf(mem.sb("xT", [128, 8, NT], F32), "xT", 5)
    identf = Buf(mem.sb("identf", [128, 128], F32), "identf")
    identb = Buf(mem.sb("identb", [128, 128], BF16), "identb")
    onesb = Buf(mem.sb("onesb", [128, 128], BF16), "onesb")
    siluC = Buf(mem.sb("siluC", [128, 8, 5], F32), "siluC")
    PS = [Buf(mem.ps(f"ps{i}", [128, 512], F32), f"ps{i}") for i in range(8)]
    for b__ in PS:
        b__.tok.excl = True
    BANKS = [(i * 512, min(NT, (i + 1) * 512)) for i in range(5)]

    ropeT = Buf(mem.sb("ropeT", [128, 2, 17, 8], F32), "ropeT")
    ropeC = Buf(mem.sb("ropeC", [128, 2, 128], F32), "ropeC")
    zerob = Buf(mem.sb("zerob", [128, 512], BF16), "zerob")
    rmk = Buf(mem.sb("rmk", [128, 2], F32), "rmk")
    ld("sp", "cst", ropeC.t[:], D["ropeC"], [ropeC.tok])
    ms("pool", zerob.t[:], 0.0, [zerob.tok])
    ms("pool", rmk.t[64:128, 0:1], 1.0, [rmk.tok])
    ms("pool", rmk.t[64:128, 1:2], 1.0, [rmk.tok])
    ms("pool", rmk.t[64:96, 1:2], 0.0, [rmk.tok])
    ts("pool", rmk.t[64:128, 0:1], rmk.t[64:128, 1:2], -1.0, 1.0, ALU.mult, ALU.add, [rmk.tok], [rmk.tok])
    ld("sp", "cst", ropeT.t[:, 0], D["cosT"], [ropeT.tok])
    ld("sp", "cst", ropeT.t[:, 1], D["sinT"], [ropeT.tok])
    for b_, (c0, c1) in enumerate(BANKS):
        ld("sp", "xin", xT.t[:, :, c0:c1], D["xT0"][:, :, c0:c1], [xT.toks[b_]])
    ld("sp", "cst", identf.t[:], D["ident"], [identf.tok])
    ld("sp", "cst", siluC.t[:], D["cT"], [siluC.tok])
    cp("dve", identb.t[:], identf.t[:], [identf.tok], [identb.tok])
    ms("dve", onesb.t[:], 1.0, [onesb.tok])
    act(siluC.t[:], siluC.t[:], AF.Silu, [siluC.tok], [siluC.tok])

    modT = Buf(mem.sb("modT", [128, 24, 5], F32), "modT")
    s1g = Buf(mem.sb("s1g", [128, 8, 5], F32), "s1g")
    s1gS = Buf(mem.sb("s1gS", [128, 8, 16], F32), "s1gS")
    shS = Buf(mem.sb("shS", [128, 8, 16], F32), "shS")
    gtS = Buf(mem.sb("gtS", [128, 8, 16], F32), "gtS")
    lvec = Buf(mem.sb("lvec", [128, 24 + 8 + 4], F32), "lvec")

    def adaln_phase(l):
        m0 = mem.mark()
        adw = [Buf(mem.sb(f"adw{i}", [128, 8, 512], F32), f"adw{i}") for i in range(2)]
        ld("sp", "lvec", lvec.t[:, 0:24], D["ada_bT"][l], [lvec.tok])
        ld("sp", "lvec", lvec.t[:, 24:32], D["norm_gT"][l], [lvec.tok])
        ld("sp", "lvec", lvec.t[:, 32:36], D["glu_bT"][l], [lvec.tok])
        psA = PS[0]
        wv = D["ada_w"][l].rearrange("(kc p) n -> p kc n", p=128)
        for blk in range(6):
            buf = adw[blk % 2]
            ld("sp", f"adw{blk % 2}", buf.t[:], wv[:, :, blk * 512:(blk + 1) * 512], [buf.tok])
            for oc4 in range(4):
                oc = blk * 4 + oc4
                for kc in range(8):
                    mm(psA.t[:, oc * 5:(oc + 1) * 5], buf.t[:, kc, oc4 * 128:(oc4 + 1) * 128], siluC.t[:, kc, :],
                       kc == 0, kc == 7, [buf.tok, siluC.tok], [psA.tok])
        tt("dve", modT.t[:], psA.t[:, 0:120].rearrange("p (o b) -> p o b", b=5),
           lvec.t[:, 0:24].unsqueeze(2).to_broadcast([128, 24, 5]), ALU.add, [psA.tok, lvec.tok], [modT.tok])
        ts("dve", s1g.t[:], modT.t[:, 8:16, :], 1.0, None, ALU.add, None, [modT.tok], [s1g.tok])
        tt("dve", s1g.t[:], s1g.t[:], lvec.t[:, 24:32].unsqueeze(2).to_broadcast([128, 8, 5]), ALU.mult,
           [s1g.tok, lvec.tok], [s1g.tok])
        for dst, src in ((s1gS, s1g.t[:, :, 1:5]), (shS, modT.t[:, 0:8, 1:5]), (gtS, modT.t[:, 16:24, 1:5])):
            cp("dve", dst.t[:].rearrange("p k (b j) -> p k b j", j=4),
               src.unsqueeze(3).to_broadcast([128, 8, 4, 4]), [modT.tok, s1g.tok], [dst.tok])
        end_phase(m0)

    def end_phase(m0):
        P.barrier()
        mem.release(m0)

    def norm_phase(hT):
        m0 = mem.mark()
        sqs = [Buf(mem.sb(f"sq{i}", [128, 8, 256], BF16), f"sq{i}") for i in range(2)]
        xns = [Buf(mem.sb(f"xn{i}", [128, 8, 256], F32), f"xn{i}") for i in range(2)]
        rts = [Buf(mem.sb(f"rt{i}", [128, 256], F32), f"rt{i}") for i in range(2)]
        chunks = [(i * 256, i * 256 + 256) for i in range(8)] + [(TP, NT)]
        for ci, (c0, c1) in enumerate(chunks):
            w = c1 - c0
            b_ = min(c0 // 512, 4)
            ps = PS[1 + (ci % 2)]
            sq, xn, rt = sqs[ci % 2], xns[ci % 2], rts[ci % 2]
            act(sq.t[:, :, 0:w], xT.t[:, :, c0:c1], AF.Square, [xT.toks[b_]], [sq.tok])
            for kc in range(8):
                mm(ps.t[:, 0:w], onesb.t[:], sq.t[:, kc, 0:w], kc == 0, kc == 7, [onesb.tok, sq.tok], [ps.tok])
            act(rt.t[:, 0:w], ps.t[:, 0:w], AF.Sqrt, [ps.tok], [rt.tok], bias=1e-6, scale=1.0 / 1024.0)
            P.op("dve", lambda e, w=w, rt=rt: e.reciprocal(out=rt.t[:, 0:w], in_=rt.t[:, 0:w]), [rt.tok], [rt.tok])
            tt("dve", xn.t[:, :, 0:w], xT.t[:, :, c0:c1], rt.t[:, 0:w].unsqueeze(1).to_broadcast([128, 8, w]),
               ALU.mult, [xT.toks[b_], rt.tok], [xn.tok])
            if b_ < 4:
                for kc in range(8):
                    act(hT.t[:, kc, c0:c1], xn.t[:, kc, 0:w], AF.Identity, [xn.tok, s1g.tok, modT.tok], [hT.toks[b_]],
                        bias=modT.t[:, kc, 0:1], scale=s1g.t[:, kc, 0:1])
            else:
                tt("dve", xn.t[:, :, 0:w], xn.t[:, :, 0:w], s1gS.t[:], ALU.mult, [xn.tok, s1gS.tok], [xn.tok])
                tt("dve", hT.t[:, :, c0:c1], xn.t[:, :, 0:w], shS.t[:], ALU.add, [xn.tok, shS.tok], [hT.toks[b_]])
        end_phase(m0)

    hlP = Buf(mem.sb("hlP", [128, 2, 16], F32), "hlP")
    hlS = Buf(mem.sb("hlS", [128, 2, 16, 4], F32), "hlS")

    def s5_params(eng, are, aim, ldt, shape, nm):
        T = Tok(nm)
        mk = lambda n, dt=F32: mem.sb(f"{nm}_{n}", shape, dt)
        abr, abi, er, ei = mk("abr"), mk("abi"), mk("er"), mk("ei")
        m1 = mem.mark()
        dt_, ang, mag, r, kf, mk_, sn, cs, t1, t2 = [mk(n) for n in ("dt", "ang", "mag", "r", "kf", "m", "sn", "cs", "t1", "t2")]
        ki = mk("ki", I32)
        R = W = [T]
        A = lambda t: t[:]
        act(A(dt_), ldt, AF.Exp, R, W)
        tt(eng, A(ang), aim, A(dt_), ALU.mult, R, W)
        tt(eng, A(t1), are, A(dt_), ALU.mult, R, W)
        act(A(mag), A(t1), AF.Exp, R, W)

        def red_sin(out, add):
            ts(eng, A(r), A(ang), add, None, ALU.add, None, R, W)
            ts(eng, A(ki), A(r), 1.0 / TWO_PI, None, ALU.mult, None, R, W)
            cp(eng, A(kf), A(ki), R, W)
            ts(eng, A(kf), A(kf), -TWO_PI, None, ALU.mult, None, R, W)
            tt(eng, A(r), A(r), A(kf), ALU.add, R, W)
            ts(eng, A(mk_), A(r), float(np.pi), -TWO_PI, ALU.is_gt, ALU.mult, R, W)
            tt(eng, A(r), A(r), A(mk_), ALU.add, R, W)
            ts(eng, A(mk_), A(r), -float(np.pi), TWO_PI, ALU.is_lt, ALU.mult, R, W)
            tt(eng, A(r), A(r), A(mk_), ALU.add, R, W)
            act(A(out), A(r), AF.Sin, R, W)
        red_sin(sn, 0.0)
        red_sin(cs, float(np.pi / 2))
        tt(eng, A(abr), A(mag), A(cs), ALU.mult, R, W)
        tt(eng, A(abi), A(mag), A(sn), ALU.mult, R, W)
        tt(eng, A(t1), are, are, ALU.mult, R, W)
        tt(eng, A(t2), aim, aim, ALU.mult, R, W)
        tt(eng, A(t1), A(t1), A(t2), ALU.add, R, W)
        P.op("dve", lambda e: e.reciprocal(out=A(t1), in_=A(t1)), R, W)
        ts(eng, A(t2), A(abr), -1.0, None, ALU.add, None, R, W)
        tt(eng, A(er), A(t2), are, ALU.mult, R, W)
        tt(eng, A(mk_), A(abi), aim, ALU.mult, R, W)
        tt(eng, A(er), A(er), A(mk_), ALU.add, R, W)
        tt(eng, A(er), A(er), A(t1), ALU.mult, R, W)
        tt(eng, A(ei), A(abi), are, ALU.mult, R, W)
        tt(eng, A(mk_), A(t2), aim, ALU.mult, R, W)
        tt(eng, A(ei), A(ei), A(mk_), ALU.subtract, R, W)
        tt(eng, A(ei), A(ei), A(t1), ALU.mult, R, W)
        P.barrier()
        mem.release(m1)
        return abr, abi, er, ei, T

    def s5_phase(l, uT, zs5T):
        m0 = mem.mark()
        aNL = mem.sb("aNL", [128, 3, 16], F32)
        aPR = mem.sb("aPR", [128, 4, 3, 128], F32)
        Tin = Tok("s5in")
        ld("sp", "s5in", aNL[:], D["aNL"][l], [Tin])
        ld("sp", "s5in", aPR[:], D["aPR"][l], [Tin])
        P.barrier()
        nabr, nabi, ner, nei, TN = s5_params("dve", aNL[:, 0, :], aNL[:, 1, :], aNL[:, 2, :], [128, 16], "pn")
        pabr, pabi, per_, pei, TPp = s5_params("dve", aPR[:, :, 0, :], aPR[:, :, 1, :], aPR[:, :, 2, :], [128, 4, 128], "pp")
        chk("s5a")
        pw = [(nabr, nabi)]
        tA, tB = mem.sb("pw_t1", [128, 16], F32), mem.sb("pw_t2", [128, 16], F32)
        for k in range(4):
            r_, i_ = mem.sb(f"pw{k}r", [128, 16], F32), mem.sb(f"pw{k}i", [128, 16], F32)
            cmul("dve", r_[:], i_[:], pw[-1][0][:], pw[-1][1][:], pw[-1][0][:], pw[-1][1][:], tA[:], tB[:], [TN], [TN], TN)
            pw.append((r_, i_))
        a4r, a4i = pw[2]
        a16r, a16i = pw[4]
        Bp = mem.sb("Bp", [128, 2, 128], F32)
        Bn = mem.sb("Bn", [128, 2, 128], F32)
        Cn = mem.sb("Cn", [128, 2, 4, 32], F32)
        h0 = mem.sb("h0", [128, 2, 4, 4], F32)
        dvq = mem.sb("dvq", [128, 4], F32)
        TL = Tok("s5ld")
        ld("sp", "s5ld", dvq[:], D["dv"][l], [TL])
        curP = [mem.sb(f"curP{i}", [128, 2, 128], F32) for i in range(2)]
        curC = [mem.sb(f"curC{i}", [128, 2, 4, 32], F32) for i in range(2)]
        tP1, tP2 = mem.sb("tP1", [128, 128], F32), mem.sb("tP2", [128, 128], F32)
        tC1, tC2 = mem.sb("tC1", [128, 4, 32], F32), mem.sb("tC2", [128, 4, 32], F32)
        BAtab = mem.sb("BAtab", [128, 16, 2, 128], BF16)
        CAtab = mem.sb("CAtab", [128, 17, 2, 5, 32], BF16)
        BAtab3 = mem.sb("BAtab3", [128, 16, 2, 128], BF16)
        rmask = mem.sb("rmask", [128, 2], F32)
        BbN = mem.sb("BbN", [128, 2, 128], F32)
        BbNb = mem.sb("BbNb", [128, 2, 5, 32], BF16)
        Ktab = mem.sb("Ktab", [128, 16, 128], BF16)
        yacc = mem.sb("yacc", [128, NT], F32)
        SS = [mem.sb(f"SS{i}", [128, 2, 4, 128], F32) for i in range(2)]
        st = [mem.sb(f"st{i}", [128, 4, 128], F32) for i in range(4)]
        Ak = [mem.sb(f"Ak{i}", [128, 2, 4], F32) for i in range(2)]
        Akt = [mem.sb(f"Akt{i}", [128, 4], F32) for i in range(2)]
        Hb = mem.sb("Hb", [128, 2, 4, 128], BF16)
        Hsb = mem.sb("Hsb", [128, 2, 4, 4], BF16)
        Zs = mem.sb("Zs", [128, 2, 4, 4], F32)
        hs1, hs2 = mem.sb("hs1", [128, 4, 4], F32), mem.sb("hs2", [128, 4, 4], F32)
        g1, g2 = mem.sb("g1", [128, 512], F32), mem.sb("g2", [128, 512], F32)
        TBA, TCA, TK, TBb, TY, TS, TH, TG, TZ = [Tok(n) for n in ("BA", "CA", "K", "Bb", "Y", "S", "H", "G", "Z")]
        Tst = [Tok(f"st{i}") for i in range(2)]
        TSS = [[Tok(f"SS{i}{j}") for j in range(2)] for i in range(2)]
        uv_all = uT.t
        TM_ = Tok("rmask")
        ms("pool", CAtab[:], 0.0, [TCA])
        ms("pool", BbNb[:], 0.0, [TBb])
        ms("pool", BAtab3[:], 0.0, [TBA])
        ms("pool", rmask[64:128, 0:1], 1.0, [TM_])
        ms("pool", rmask[64:128, 1:2], 0.0, [TM_])
        ms("pool", rmask[64:96, 1:2], 1.0, [TM_])
        ts("pool", rmask[64:128, 0:1], rmask[64:128, 1:2], 1.0, None, ALU.mult, None, [TM_], [TM_])
        ts("pool", rmask[64:128, 1:2], rmask[64:128, 0:1], -1.0, 1.0, ALU.mult, ALU.add, [TM_], [TM_])
        for q in range(4):
            qs = slice(q * 4, q * 4 + 4)
            ld("sp", "s5ld", Bp[:], D["BPR"][l][:, q], [TL])
            ld("sp", "s5ld", Bn[:], D["BNL"][l][:, q], [TL])
            ld("sp", "s5ld", Cn[:], D["CNL"][l][:, q], [TL])
            ld("sp", "s5ld", h0[:], D["h0NL"][l][:, :, qs, :], [TL])
            cmul("pool", curP[0][:, 0, :], curP[0][:, 1, :], Bp[:, 0, :], Bp[:, 1, :], per_[:, q, :], pei[:, q, :],
                 tP1[:], tP2[:], [TL, TPp], [TBA], TBA)
            ci = 0
            for j in range(15, -1, -1):
                cp("pool", BAtab[:, j, :, :], curP[ci][:], [TBA], [TBA])
                ts("pool", BAtab3[64:128, j, :, :], curP[ci][64:128], rmask[64:128, 1:2], None, ALU.mult, None, [TBA, TM_], [TBA])
                if j > 0:
                    cmul("pool", curP[1 - ci][:, 0, :], curP[1 - ci][:, 1, :], curP[ci][:, 0, :], curP[ci][:, 1, :],
                         pabr[:, q, :], pabi[:, q, :], tP1[:], tP2[:], [TPp, TBA], [TBA], TBA)
                    ci = 1 - ci
            chk("s5b")
            abr_b = nabr[:, qs].unsqueeze(2).to_broadcast([128, 4, 32])
            abi_b = nabi[:, qs].unsqueeze(2).to_broadcast([128, 4, 32])
            cp("dve", curC[0][:], Cn[:], [TL], [TCA])
            ci = 0
            for d in range(17):
                cp("dve", CAtab[:, d, :, 0:3, :], curC[ci][:, :, 0:3, :], [TCA], [TCA])
                cp("dve", CAtab[:, d, :, 4, :], curC[ci][:, :, 3, :], [TCA], [TCA])
                if d < 16:
                    cmul("dve", curC[1 - ci][:, 0], curC[1 - ci][:, 1], curC[ci][:, 0], curC[ci][:, 1], abr_b, abi_b,
                         tC1[:], tC2[:], [TN, TCA], [TCA], TCA)
                    ci = 1 - ci
            er_b = ner[:, qs].unsqueeze(2).to_broadcast([128, 4, 32])
            ei_b = nei[:, qs].unsqueeze(2).to_broadcast([128, 4, 32])
            v4 = lambda ap: ap.rearrange("p (a b) -> p a b", b=32)
            cmul("dve", v4(BbN[:, 0, :]), v4(BbN[:, 1, :]), v4(Bn[:, 0, :]), v4(Bn[:, 1, :]), er_b, ei_b, tC1[:], tC2[:],
                 [TL, TN], [TBb], TBb)
            BbN4 = BbN[:].rearrange("p r (a b) -> p r a b", b=32)
            cp("dve", BbNb[:, 0, 0:3, :], BbN4[:, 0, 0:3, :], [TBb], [TBb])
            cp("dve", BbNb[:, 0, 4, :], BbN4[:, 0, 3, :], [TBb], [TBb])
            ts("dve", BbNb[:, 1, 0:3, :], BbN4[:, 1, 0:3, :], -1.0, None, ALU.mult, None, [TBb], [TBb])
            ts("dve", BbNb[:, 1, 4, :], BbN4[:, 1, 3, :], -1.0, None, ALU.mult, None, [TBb], [TBb])
            chk("s5c")
            psK = PS[2]
            BI = [0, 1, 2, 4]
            for pb in range(2):
                rows = slice(pb * 32, pb * 32 + 32)
                for ri in range(2):
                    mm(psK.t[rows, :].rearrange("p (d c) -> p d c", c=32), BbNb[:, ri, pb, :],
                       CAtab[:, 0:16, ri, pb, :], ri == 0, ri == 1, [TBb, TCA], [psK.tok])
            k_ = 0
            for pb in (2, 3):
                for ri in range(2):
                    mm(psK.t[64:128, :].rearrange("p (d c) -> p d c", c=32),
                       BbNb[:, ri, pb:pb + 2, :].rearrange("p a b -> p (a b)"),
                       CAtab[:, 0:16, ri, BI[pb], :], k_ == 0, k_ == 3, [TBb, TCA], [psK.tok])
                    k_ += 1
            ms("pool", Ktab[:], 0.0, [TK])
            for pb in range(2):
                rows = slice(pb * 32, pb * 32 + 32)
                cp("dve", Ktab[rows, :, pb * 32:(pb + 1) * 32], psK.t[rows, :].rearrange("p (d c) -> p d c", c=32),
                   [psK.tok], [TK])
            for pb in (2, 3):
                ts("dve", Ktab[64:128, :, pb * 32:(pb + 1) * 32], psK.t[64:128, :].rearrange("p (d c) -> p d c", c=32),
                   rmask[64:128, pb - 2:pb - 1], None, ALU.mult, None, [psK.tok, TM_], [TK])
            stt(Ktab[:, 0, :], identf.t[:], dvq[:, q:q + 1], Ktab[:, 0, :], ALU.mult, ALU.add, [identf.tok, TL, TK], [TK])
            chk("s5d")
            for b_, (c0, c1) in enumerate(BANKS):
                ps = PS[b_ % 2]
                if b_ < 4:
                    Yv = ps.t[:].rearrange("p (n i) -> p n i", i=16)
                    uv = uv_all[:, q, c0:c1].rearrange("p (n i) -> p n i", i=16)
                    nl = 16
                else:
                    Yv = ps.t[:, 0:16].rearrange("p (n i) -> p n i", i=4)
                    uv = uv_all[:, q, c0:c1].rearrange("p (n i) -> p n i", i=4)
                    nl = 4
                for d in range(nl):
                    mm(Yv[:, :, d:nl], Ktab[:, d, :], uv[:, :, 0:nl - d], d == 0, d == nl - 1, [TK, uT.tok], [ps.tok])
                act(yacc[:, c0:c1], ps.t[:, 0:c1 - c0], AF.Copy, [ps.tok], [TY])
            chk("s5e")
            up = uv_all[:, q, 0:TP].rearrange("p (n j) -> p n j", j=16)
            us = uv_all[:, q, TP:NT].rearrange("p (b j) -> p b j", j=4)
            for pb in range(4):
                rows = slice(pb * 32, pb * 32 + 32)
                pz = PS[cfg.get('zbank', 4) + pb]
                tab = BAtab
                if pb == 3:
                    rows = slice(64, 128)
                    tab = BAtab3
                zmode = cfg.get("zmode", 0)
                if zmode == 2 and pb == 3:
                    continue
                if zmode == 3 and pb > 0:
                    continue
                for ri in range(2):
                    if zmode == 5:
                        continue
                    for j in range(16):
                        mm(pz.t[:, ri * 128:(ri + 1) * 128], tab[rows, j, ri, :], up[rows, :, j], j == 0, j == 15,
                           [TBA, uT.tok], [pz.tok])
                    if zmode == 1:
                        continue
                    for j in range(4):
                        mm(pz.t[:, 256 + ri * 4:260 + ri * 4], tab[rows, 12 + j, ri, :], us[rows, :, j], j == 0, j == 3,
                           [TBA, uT.tok], [pz.tok])
                if zmode == 4:
                    continue
                cp("dve", SS[0][:, 0, pb, :], pz.t[:, 0:128], [pz.tok], [TSS[0][0]])
                act(SS[0][:, 1, pb, :], pz.t[:, 128:256], AF.Copy, [pz.tok], [TSS[0][1]])
                cp("dve", Zs[:, :, pb, :], pz.t[:, 256:264].rearrange("p (r b) -> p r b", b=4), [pz.tok], [TZ])
            chk("s5f")
            cp("dve", Ak[0][:, 0, :], a16r[:, qs], [TN], [TG])
            cp("dve", Ak[0][:, 1, :], a16i[:, qs], [TN], [TG])
            cur = 0
            for k in range(7):
                s = 1 << k
                src, dst = SS[cur], SS[1 - cur]
                Tsrc, Tdst = TSS[cur], TSS[1 - cur]
                w_ = 128 - s
                Ar = Ak[k % 2][:, 0, :].unsqueeze(2).to_broadcast([128, 4, w_])
                Ai = Ak[k % 2][:, 1, :].unsqueeze(2).to_broadcast([128, 4, w_])
                tt("dve", st[0][:, :, 0:w_], src[:, 0, :, 0:w_], Ar, ALU.mult, [Tsrc[0], TG], [Tst[0]])
                tt("dve", st[1][:, :, 0:w_], src[:, 1, :, 0:w_], Ai, ALU.mult, [Tsrc[1], TG], [Tst[0]])
                tt("dve", st[0][:, :, 0:w_], st[0][:, :, 0:w_], st[1][:, :, 0:w_], ALU.subtract, [Tst[0]], [Tst[0]])
                tt("dve", dst[:, 0, :, s:128], src[:, 0, :, s:128], st[0][:, :, 0:w_], ALU.add, [Tsrc[0], Tst[0]], [Tdst[0]])
                cp("dve", dst[:, 0, :, 0:s], src[:, 0, :, 0:s], [Tsrc[0]], [Tdst[0]])
                tt("pool", st[2][:, :, 0:w_], src[:, 1, :, 0:w_], Ar, ALU.mult, [Tsrc[1], TG], [Tst[1]])
                tt("pool", st[3][:, :, 0:w_], src[:, 0, :, 0:w_], Ai, ALU.mult, [Tsrc[0], TG], [Tst[1]])
                tt("pool", st[2][:, :, 0:w_], st[2][:, :, 0:w_], st[3][:, :, 0:w_], ALU.add, [Tst[1]], [Tst[1]])
                tt("pool", dst[:, 1, :, s:128], src[:, 1, :, s:128], st[2][:, :, 0:w_], ALU.add, [Tsrc[1], Tst[1]], [Tdst[1]])
                cp("pool", dst[:, 1, :, 0:s], src[:, 1, :, 0:s], [Tsrc[1]], [Tdst[1]])
                if k < 6:
                    a_, b_2 = Ak[k % 2], Ak[1 - k % 2]
                    cmul("dve", b_2[:, 0, :], b_2[:, 1, :], a_[:, 0, :], a_[:, 1, :], a_[:, 0, :], a_[:, 1, :],
                         Akt[0][:], Akt[1][:], [TG], [TG], TG)
                cur = 1 - cur
            Sf, TSf = SS[cur], TSS[cur]
            chk("s5g")
            cp("dve", hlP.t[:, :, qs], Sf[:, :, :, 127], TSf, [hlP.tok])
            ms("pool", Hb[:, :, :, 0:1], 0.0, [TH])
            cp("dve", Hb[:, 0, :, 1:128], Sf[:, 0, :, 0:127], [TSf[0]], [TH])
            ts("dve", Hb[:, 1, :, 1:128], Sf[:, 1, :, 0:127], -1.0, None, ALU.mult, None, [TSf[1]], [TH])
            cp("dve", Hsb[:, 0], h0[:, 0], [TL], [TH])
            ts("dve", Hsb[:, 1], h0[:, 1], -1.0, None, ALU.mult, None, [TL], [TH])
            a4r_b = a4r[:, qs].unsqueeze(2).to_broadcast([128, 4, 4])
            a4i_b = a4i[:, qs].unsqueeze(2).to_broadcast([128, 4, 4])
            cmul("dve", hlS.t[:, 0, qs, :], hlS.t[:, 1, qs, :], h0[:, 0], h0[:, 1], a4r_b, a4i_b, hs1[:], hs2[:],
                 [TL, TN], [hlS.tok], hlS.tok)
            tt("dve", hlS.t[:, :, qs, :], hlS.t[:, :, qs, :], Zs[:], ALU.add, [hlS.tok, TZ], [hlS.tok])
            chk("s5h")
            def inter(outp, c0_, c1_, i, Hop, part):
                if part == 0:
                    for pb in range(2):
                        rows = slice(pb * 32, pb * 32 + 32)
                        for ri in range(2):
                            mm(outp.t[rows, c0_:c1_], CAtab[:, i + 1, ri, pb, :], Hop[:, ri, pb, :],
                               ri == 0, ri == 1, [TCA, TH], [outp.tok])
                else:
                    k_ = 0
                    for pb in (2, 3):
                        for ri in range(2):
                            mm(outp.t[64:128, c0_:c1_], CAtab[:, i + 1, ri, pb:pb + 2, :].rearrange("p a b -> p (a b)"),
                               Hop[:, ri, pb, :], k_ == 0, k_ == 3, [TCA, TH], [outp.tok])
                            k_ += 1
            pss = PS[3]
            for part in range(2):
                for i in range(16):
                    inter(PS[4 + i // 4], (i % 4) * 128, (i % 4 + 1) * 128, i, Hb, part)
                for i in range(4):
                    inter(pss, i * 4, i * 4 + 4, i, Hsb, part)
            chk("s5i")
            yv = yacc[:, 0:TP].rearrange("p (n i) -> p n i", i=16)
            for ib in range(4):
                tt("dve", yv[:, :, ib * 4:ib * 4 + 4], yv[:, :, ib * 4:ib * 4 + 4],
                   PS[4 + ib].t[:].rearrange("p (i n) -> p n i", n=128), ALU.add, [TY, PS[4 + ib].tok], [TY])
            ysv = yacc[:, TP:NT].rearrange("p (b i) -> p b i", i=4)
            tt("dve", ysv, ysv, pss.t[:, 0:16].rearrange("p (i b) -> p b i", b=4), ALU.add, [TY, pss.tok], [TY])
            tap(f"y5scan{q}", yacc[:], [TY])
            for b_, (c0, c1) in enumerate(BANKS):
                w = c1 - c0
                yy = yacc[:, c0:c1]
                tt("dve", g1[:, 0:w], yy, yy, ALU.mult, [TY], [TG])
                ts("dve", g1[:, 0:w], g1[:, 0:w], 0.044715 * 1.5957691216, 1.5957691216, ALU.mult, ALU.add, [TG], [TG])
                tt("dve", g1[:, 0:w], g1[:, 0:w], yy, ALU.mult, [TG, TY], [TG])
                act(g2[:, 0:w], g1[:, 0:w], AF.Sigmoid, [TG], [TG])
                tt("dve", uv_all[:, q, c0:c1], yy, g2[:, 0:w], ALU.mult, [TG, TY], [uT.tok])
        P.dma("sp", "out", lambda e: e.dma_start(out=O["hlP"][l], in_=hlP.t[:]), [hlP.tok], [out_tok])
        P.dma("sp", "out", lambda e: e.dma_start(out=O["hlS"][l], in_=hlS.t[:]), [hlS.tok], [out_tok])
        end_phase(m0)

    def uz_proj_phase(l, hT, uT, zs5T):
        m0 = mem.mark()
        wuz = Buf(mem.sb("wuz", [128, 8, 1024], BF16), "wuz", 2)
        wv = D["w_in"][l].rearrange("(kc p) n -> p kc n", p=128)
        for h_ in range(2):
            for kc in range(8):
                ld("pool", f"wuz{h_}", wuz.t[:, kc, h_ * 512:(h_ + 1) * 512], wv[:, kc, h_ * 512:(h_ + 1) * 512], [wuz.toks[h_]])
        k = 0
        for oc in range(8):
            for b_, (c0, c1) in enumerate(BANKS):
                w = c1 - c0
                ps = PS[k % 4]
                k += 1
                for kc in range(8):
                    mm(ps.t[:, 0:w], wuz.t[:, kc, oc * 128:(oc + 1) * 128], hT.t[:, kc, c0:c1], kc == 0, kc == 7,
                       [wuz.toks[oc // 4], hT.toks[b_]], [ps.tok])
                if oc < 4:
                    act(uT.t[:, oc, c0:c1], ps.t[:, 0:w], AF.Copy, [ps.tok], [uT.tok])
                else:
                    act(zs5T.t[:, oc - 4, c0:c1], ps.t[:, 0:w], AF.Silu, [ps.tok], [zs5T.tok])
        end_phase(m0)

    def glu_phase(l, uT, zs5T):
        m0 = mem.mark()
        wg = Buf(mem.sb("wg", [128, 4, 512], BF16), "wg")
        sg = Buf(mem.sb("sg", [128, 512], F32), "sg")
        for kc in range(4):
            ld("pool", "wg", wg.t[:, kc, :], D["glu_w"][l].rearrange("(kc p) n -> p kc n", p=128)[:, kc, :], [wg.tok])
        k = 0
        for oc in range(4):
            for b_, (c0, c1) in enumerate(BANKS):
                w = c1 - c0
                ps = PS[k % 4]
                k += 1
                for kc in range(4):
                    mm(ps.t[:, 0:w], wg.t[:, kc, oc * 128:(oc + 1) * 128], uT.t[:, kc, c0:c1], kc == 0, kc == 3,
                       [wg.tok, uT.tok], [ps.tok])
                act(sg.t[:, 0:w], ps.t[:, 0:w], AF.Sigmoid, [ps.tok, lvec.tok], [sg.tok], bias=lvec.t[:, 32 + oc:33 + oc])
                tt("dve", zs5T.t[:, oc, c0:c1], zs5T.t[:, oc, c0:c1], sg.t[:, 0:w], ALU.mult, [sg.tok, zs5T.tok], [zs5T.tok])
        for b_, (c0, c1) in enumerate(BANKS):
            tt("dve", uT.t[:, :, c0:c1], uT.t[:, :, c0:c1], zs5T.t[:, :, c0:c1], ALU.mult, [uT.tok, zs5T.tok], [uT.tok])
        end_phase(m0)

    TILES = [(i, i * 128, 128) for i in range(16)] + [(16, TP, 16)]
    Q_OFF, KV_OFF, G_OFF, ZN_OFF, MG_OFF = 1024, 1536, 2304, 2328, 2840

    def rope_inplace(v, ti, rows, H, tmp, Tv, Tt):
        cos = ropeT.t[0:rows, 0, ti, :].unsqueeze(1).to_broadcast([rows, H, 8])
        sin = ropeT.t[0:rows, 1, ti, :].unsqueeze(1).to_broadcast([rows, H, 8])
        x1, x2 = v[:, :, 0:8], v[:, :, 8:16]
        t = [tmp[0:rows, i, 0:H, :] for i in range(4)]
        tt("dve", t[0], x1, cos, ALU.mult, [Tv, ropeT.tok], [Tt])
        tt("dve", t[1], x2, sin, ALU.mult, [Tv, ropeT.tok], [Tt])
        tt("dve", t[2], x1, sin, ALU.mult, [Tv, ropeT.tok], [Tt])
        tt("dve", t[3], x2, cos, ALU.mult, [Tv, ropeT.tok], [Tt])
        tt("dve", x1, t[0], t[1], ALU.subtract, [Tt], [Tv])
        tt("dve", x2, t[2], t[3], ALU.add, [Tt], [Tv])

    def tm_proj_phase(l, hT, A):
        m0 = mem.mark()
        wv = D["w_in"][l].rearrange("(kc p) n -> p kc n", p=128)
        wb = [Buf(mem.sb(f"wtm{i}", [128, 8, 512], BF16), f"wtm{i}") for i in range(2)]
        stg = [Buf(mem.sb(f"stg{i}", [128, 512], F32), f"stg{i}") for i in range(4)]
        rtmp = mem.sb("rtmp", [128, 4, 8, 8], F32)
        Trt = Tok("rtmp")
        nld = [0]

        def load_w(c0, ncol):
            buf = wb[nld[0] % 2]
            nld[0] += 1
            for kc in range(8):
                ld("pool", "wtm" + str(nld[0] % 2), buf.t[:, kc, 0:ncol], wv[:, kc, c0:c0 + ncol], [buf.tok])
            return buf
        k = [0]

        def proj_tile(buf, ncol, t0, rows):
            ps = PS[k[0] % 4]
            k[0] += 1
            for kc in range(8):
                mm(ps.t[0:rows, 0:ncol], hT.t[:, kc, t0:t0 + rows], buf.t[:, kc, 0:ncol], kc == 0, kc == 7,
                   [buf.tok, hT.toks[min(t0 // 512, 4)]], [ps.tok])
            return ps
        wq = load_w(Q_OFF, 512)
        wk = load_w(KV_OFF, 512)
        for (ti, t0, rows) in TILES:
            ps = proj_tile(wq, 512, t0, rows)
            sg_ = stg[ti % 4]
            act(sg_.t[0:rows, :].rearrange("p (hq g d) -> p hq g d", hq=4, g=2),
                ps.t[0:rows, :].rearrange("p (g hq d) -> p hq g d", g=2, hq=4), AF.Copy, [ps.tok], [sg_.tok], scale=0.125)
            rope_inplace(sg_.t[0:rows, :].rearrange("p (h d) -> p h d", d=64), ti, rows, 8, rtmp, sg_.tok, Trt)
            pt = PS[4 + ti % 4]
            for hq in range(4):
                tr(pt.t[:, hq * 128:hq * 128 + rows], sg_.t[0:rows, hq * 128:(hq + 1) * 128], identf.t[0:rows, 0:rows],
                   [sg_.tok, identf.tok], [pt.tok])
            if ti < 16:
                cp("dve", A["qT"].t[:, :, t0:t0 + 128], pt.t[:].rearrange("p (h t) -> p h t", t=128), [pt.tok], [A["qT"].tok])
            else:
                cp("dve", A["qTs"].t[:], pt.t[:].rearrange("p (h t) -> p h t", t=128)[:, :, 0:16], [pt.tok], [A["qTs"].tok])
        ww = load_w(KV_OFF + 512, 280)
        for (ti, t0, rows) in TILES:
            ps = proj_tile(wk, 512, t0, rows)
            sg_ = stg[ti % 4]
            act(sg_.t[0:rows, :], ps.t[0:rows, :], AF.Copy, [ps.tok], [sg_.tok])
            rope_inplace(sg_.t[0:rows, 256:384].rearrange("p (h d) -> p h d", d=64), ti, rows, 2, rtmp, sg_.tok, Trt)
            if ti < 16:
                P.dma("sp", f"stg{ti % 2}", lambda e, sg_=sg_, t0=t0: e.dma_start(out=O["kvP"][l][t0:t0 + 128, :], in_=sg_.t[:]),
                      [sg_.tok], [out_tok])
            else:
                P.dma("sp", f"stg{ti % 2}", lambda e, sg_=sg_: e.dma_start(out=O["kvS"][l], in_=sg_.t[0:16, :]),
                      [sg_.tok], [out_tok])
            pt = PS[4 + ti % 4]
            for s_ in range(3):
                tr(pt.t[:, s_ * 128:s_ * 128 + rows], sg_.t[0:rows, s_ * 128:(s_ + 1) * 128], identf.t[0:rows, 0:rows],
                   [sg_.tok, identf.tok], [pt.tok])
            cp("dve", A["KT3"].t[:, :, t0:t0 + rows], pt.t[:, 0:384].rearrange("p (s t) -> p s t", t=128)[:, :, 0:rows],
               [pt.tok], [A["KT3"].tok])
            cp("pool", A["vsel"].t[0:rows, ti, :, 0:64], sg_.t[0:rows, 384:512].rearrange("p (g d) -> p g d", d=64),
               [sg_.tok], [A["vsel"].tok])
            if ti == 16:
                cp("dve", A["S_kT"].t[:, 0, :], pt.t[:, 256:272], [pt.tok], [A["S_kT"].tok])
                cp("pool", A["S_v"].t[:, 0, :, :], sg_.t[0:16, 384:512].rearrange("p (g d) -> p g d", d=64), [sg_.tok], [A["S_v"].tok])
        wz = load_w(ZN_OFF, 512)
        for (ti, t0, rows) in TILES:
            ps = proj_tile(ww, 280, t0, rows)
            sg_ = stg[ti % 4]
            act(sg_.t[0:rows, 0:256], ps.t[0:rows, 0:256], AF.Copy, [ps.tok], [sg_.tok])
            act(A["gsig"].t[0:rows, ti, :], ps.t[0:rows, 256:280], AF.Sigmoid, [ps.tok], [A["gsig"].tok])
            rope_inplace(sg_.t[0:rows, 0:128].rearrange("p (h d) -> p h d", d=64), ti, rows, 2, rtmp, sg_.tok, Trt)
            if 12 <= ti < 16:
                P.dma("sp", f"stg{ti % 2}", lambda e, sg_=sg_, ti=ti: e.dma_start(
                    out=O["winP"][l][(ti - 12) * 128:(ti - 11) * 128, :], in_=sg_.t[:, 0:256]), [sg_.tok], [out_tok])
            elif ti == 16:
                for b in range(4):
                    P.dma("sp", f"stg{ti % 2}", lambda e, sg_=sg_, b=b: e.dma_start(
                        out=O["winS"][b, l, 508:512, :], in_=sg_.t[4 * b:4 * b + 4, 0:256]), [sg_.tok], [out_tok])
                    P.dma("sp", "wcopy", lambda e, b=b: e.dma_start(
                        out=O["winS"][b, l, 0:508, :], in_=D["swin"][b, l, 4:512, :]), [], [out_tok])
            pt = PS[4 + ti % 4]
            tr(pt.t[:, 0:rows], sg_.t[0:rows, 0:128], identf.t[0:rows, 0:rows], [sg_.tok, identf.tok], [pt.tok])
            cp("dve", A["kwinT"].t[:, t0:t0 + rows], pt.t[:, 0:rows], [pt.tok], [A["kwinT"].tok])
            cp("pool", A["vwin"].t[0:rows, ti, :, 0:64], sg_.t[0:rows, 128:256].rearrange("p (g d) -> p g d", d=64),
               [sg_.tok], [A["vwin"].tok])
            if ti == 16:
                cp("dve", A["S_kT"].t[:, 1, :], pt.t[:, 0:16], [pt.tok], [A["S_kT"].tok])
                cp("pool", A["S_v"].t[:, 1, :, :], sg_.t[0:16, 128:256].rearrange("p (g d) -> p g d", d=64), [sg_.tok], [A["S_v"].tok])
        kk = 0
        for oc in range(4):
            for b_, (c0, c1) in enumerate(BANKS):
                w = c1 - c0
                ps = PS[kk % 4]
                kk += 1
                for kc in range(8):
                    mm(ps.t[:, 0:w], wz.t[:, kc, oc * 128:(oc + 1) * 128], hT.t[:, kc, c0:c1], kc == 0, kc == 7,
                       [wz.tok, hT.toks[b_]], [ps.tok])
                act(A["zsT"].t[:, oc, c0:c1], ps.t[:, 0:w], AF.Silu, [ps.tok], [A["zsT"].tok])
        end_phase(m0)

    def compress_setup(l, A, W1, W2):
        for v in range(3):
            ld("pool", "w2", W2.t[:, v, :], D["W2bd"][l, v], [W2.tok])

    def load_w1(l, strm, W1):
        for c4 in range(16):
            ld("pool", "w1", W1.t[:, c4 * 4:(c4 + 1) * 4, :], D["W1bd"][l, strm][:, c4 * 4:(c4 + 1) * 4, :], [W1.tok])

    def compress(l, strm, rowsT, nblk, W1, psC, Trows):
        rv = rowsT.rearrange("p (n j) -> p n j", j=64)
        for j in range(64):
            mm(psC.t[:, 0:nblk + 1], W1.t[:, j, :], rv[:, :, j], j == 0, j == 63, [W1.tok, Trows], [psC.tok])

    def attn_prompt_phase(l, A):
        m0 = mem.mark()
        cmC = Buf(mem.sb("cmC", [128, 4, 16, 32], F32), "cmC")
        Ecst = Buf(mem.sb("Ecst", [128, 16, 128], BF16), "Ecst")
        triB = Buf(mem.sb("triB", [128, 2, 128], BF16), "triB")
        ld("sp", "cst", cmC.t[:], D["cm"], [cmC.tok])
        for kt4 in range(4):
            ld("pool", "cstp", Ecst.t[:, kt4 * 4:(kt4 + 1) * 4, :], D["Ecst"][:, kt4 * 4:(kt4 + 1) * 4, :], [Ecst.tok])
        ld("pool", "cstp", triB.t[:], D["tri"], [triB.tok])
        W1l = [Buf(mem.sb(f"W1_{i}", [128, 64, 128], BF16), f"W1_{i}") for i in range(2)]
        W2 = Buf(mem.sb("W2", [128, 3, 128], BF16), "W2")
        for i_ in range(2):
            load_w1(l, i_, W1l[i_])
        cb = mem.sb("cb", [128, 2], F32)
        silk = mem.sb("silk", [128, 32], BF16)
        silv4 = mem.sb("silv4", [128, 4, 32], BF16)
        kcT = mem.sb("kcT", [128, 32], BF16)
        Vbd2 = mem.sb("Vbd2", [128, 2, 128], BF16)
        kt1, kt2 = mem.sb("kt1", [128, 32], F32), mem.sb("kt2", [128, 32], F32)
        MT = mem.sb("MT", [128, TP], BF16)
        Tc = Tok("cmp")
        TMT = Tok("MT")
        compress_setup(l, A, None, W2)
        for strm in range(2):
            ld("pool", "pe", A["KT3"].t[:, strm, 2048:2112], D["peT"][l, strm], [A["KT3"].tok])
        for strm in range(2):
            psC = PS[2]
            compress(l, strm, A["KT3"].t[:, strm, 0:2112], 32, W1l[strm], psC, A["KT3"].tok)
            cp("dve", cb[:, strm:strm + 1], psC.t[:, 32:33], [psC.tok], [Tc])
            if strm == 0:
                act(silk[:], psC.t[:, 0:32], AF.Silu, [psC.tok, Tc], [Tc], bias=cb[:, 0:1])
            else:
                act(silv4[:], psC.t[:, 0:32].unsqueeze(1).to_broadcast([128, 4, 32]), AF.Silu, [psC.tok, Tc], [Tc], bias=cb[:, 1:2])
        ps = PS[3]
        mm(ps.t[:, 0:32], W2.t[:, 0, :], silk[:], True, True, [W2.tok, Tc], [ps.tok])
        mm(ps.t[:, 32:64], W2.t[:, 2, :], silk[:], True, True, [W2.tok, Tc], [ps.tok])
        tt("dve", kt1[:], ps.t[:, 0:32], ropeC.t[:, 0, 0:32], ALU.mult, [ps.tok, ropeC.tok], [Tc])
        tt("dve", kt2[:], ps.t[:, 32:64], ropeC.t[:, 1, 0:32], ALU.mult, [ps.tok, ropeC.tok], [Tc])
        tt("dve", kcT[:], kt1[:], kt2[:], ALU.add, [Tc], [Tc])
        ps2 = PS[1]
        mm(ps2.t[:, 0:128], silv4[:].rearrange("p a b -> p (a b)"), W2.t[:, 1, :], True, True, [W2.tok, Tc], [ps2.tok])
        ms("pool", Vbd2[:], 0.0, [Tc])
        for g in range(2):
            gs = slice(g * 64, g * 64 + 64)
            cp("dve", Vbd2[0:32, g, 0:64], ps2.t[0:32, gs], [ps2.tok], [Tc])
            cp("dve", Vbd2[32:64, g, 64:128], ps2.t[32:64, gs], [ps2.tok], [Tc])
            ts("dve", Vbd2[64:128, g, 0:64], ps2.t[64:128, gs], rmk.t[64:128, 0:1], None, ALU.mult, None, [ps2.tok, rmk.tok], [Tc])
            ts("dve", Vbd2[64:128, g, 64:128], ps2.t[64:128, gs], rmk.t[64:128, 1:2], None, ALU.mult, None, [ps2.tok, rmk.tok], [Tc])
        tap("kcT", kcT[:], [Tc])
        tap("Vbd2", Vbd2[:], [Tc])
        chk("ap0")
        sc = mem.sb("sc", [128, 8, 32], F32)
        mx = mem.sb("mx", [128, 8], F32)
        imp = mem.sb("imp", [128, 2, 32], F32)
        wk = mem.sb("wk", [128, 2, 32], F32)
        m8 = mem.sb("m8", [128, 2, 16], F32)
        t1 = mem.sb("t1", [128, 2, 64], F32)
        t2 = mem.sb("t2", [128, 2, 32], F32)
        pT = mem.sb("pT", [128, 2, 128], BF16)
        oacc = mem.sb("oacc", [128, 8, 64], F32)
        tmpo = mem.sb("tmpo", [128, 4, 64], F32)
        rc = mem.sb("rc", [128, 4], F32)
        pbuf = [Buf(mem.sb(f"pbuf{i}", [128, 512], BF16), f"pbuf{i}") for i in range(4)]
        Ts, To, Tr = Tok("sc"), Tok("oacc"), Tok("rc")
        ms("pool", t1[:], 0.0, [Ts])
        npb = [0]

        def dense_branch(i, g, t0, KTap, Vbuf, pso, kts, blockmask, gate_off):
            first = True
            for c0_ in range(0, len(kts), 4):
                grp = kts[c0_:c0_ + 4]
                for k_, kt in enumerate(grp):
                    pss = PS[k_]
                    tri = 0 if kt == i else (1 if kt == i - 4 and not blockmask else None)
                    mm(pss.t[:], KTap[g * 64:(g + 1) * 64, kt * 128:(kt + 1) * 128], A["qT"].t[g * 64:(g + 1) * 64, :, t0:t0 + 128],
                       True, (not blockmask) and tri is None, [A["KT3"].tok, A["kwinT"].tok, A["qT"].tok], [pss.tok])
                    if blockmask:
                        mm(pss.t[:], Ecst.t[64 * g:64 * g + 64, kt, :],
                           MT[64 * g:64 * g + 64, t0:t0 + 128].unsqueeze(1).to_broadcast([64, 4, 128]),
                           False, tri is None, [Ecst.tok, TMT], [pss.tok])
                for k_, kt in enumerate(grp):
                    pss = PS[k_]
                    tri = 0 if kt == i else (1 if kt == i - 4 and not blockmask else None)
                    if tri is not None:
                        mm(pss.t[:], identb.t[:], triB.t[:, tri, :].unsqueeze(1).to_broadcast([128, 4, 128]),
                           False, True, [identb.tok, triB.tok], [pss.tok])
                if first:
                    mm(pso.t[:, 0:260], zerob.t[:, 0:128], zerob.t[:, 0:260], True, False, [zerob.tok], [pso.tok])
                    first = False
                for k_, kt in enumerate(grp):
                    pss = PS[k_]
                    pb_ = pbuf[k_]
                    act(pb_.t[:], pss.t[:], AF.Exp, [pss.tok], [pb_.tok])
                    for hq in range(4):
                        mm(pso.t[:, hq * 65:(hq + 1) * 65], pb_.t[:, hq * 128:(hq + 1) * 128], Vbuf.t[:, kt, g, :],
                           False, kt == kts[-1], [pb_.tok, Vbuf.tok], [pso.tok])
            pv = pso.t[:, 0:260].rearrange("p (h c) -> p h c", c=65)
            P.op("dve", lambda e: e.reciprocal(out=rc[:], in_=pv[:, :, 64]), [pso.tok], [Tr])
            tt("dve", rc[:], rc[:], A["gsig"].t[:, i, gate_off + g * 4:gate_off + g * 4 + 4], ALU.mult, [Tr, A["gsig"].tok], [Tr])
            tt("dve", tmpo[:], pv[:, :, 0:64], rc[:].unsqueeze(2).to_broadcast([128, 4, 64]), ALU.mult, [pso.tok, Tr], [Tr])
            tt("dve", oacc[:, g * 4:(g + 1) * 4, :], oacc[:, g * 4:(g + 1) * 4, :], tmpo[:], ALU.add, [To, Tr], [To])

        for i in range(16):
            t0 = i * 128
            psS = PS[0]
            for g in range(2):
                for hq in range(4):
                    h = g * 4 + hq
                    mm(psS.t[:, h * 32:(h + 1) * 32], A["qT"].t[g * 64:(g + 1) * 64, hq, t0:t0 + 128], kcT[g * 64:(g + 1) * 64, :],
                       True, True, [A["qT"].tok, Tc], [psS.tok])
            bc8 = lambda ap: ap.unsqueeze(1).to_broadcast([128, 8, 32])
            tt("dve", sc[:], psS.t[:, 0:256].rearrange("p (h n) -> p h n", n=32), bc8(cmC.t[:, 0, i, :]), ALU.add,
               [psS.tok, cmC.tok], [Ts])
            P.op("dve", lambda e: e.tensor_reduce(out=mx[:], in_=sc[:], axis=AX.X, op=ALU.max), [Ts], [Ts])
            tt("dve", sc[:], sc[:], mx[:].unsqueeze(2).to_broadcast([128, 8, 32]), ALU.subtract, [Ts], [Ts])
            act(sc[:], sc[:], AF.Exp, [Ts], [Ts])
            P.op("dve", lambda e: e.tensor_reduce(out=mx[:], in_=sc[:], axis=AX.X, op=ALU.add), [Ts], [Ts])
            P.op("dve", lambda e: e.reciprocal(out=mx[:], in_=mx[:]), [Ts], [Ts])
            tt("dve", sc[:], sc[:], mx[:].unsqueeze(2).to_broadcast([128, 8, 32]), ALU.mult, [Ts], [Ts])
            tt("dve", sc[:], sc[:], bc8(cmC.t[:, 1, i, :]), ALU.mult, [Ts, cmC.tok], [Ts])
            P.op("dve", lambda e: e.tensor_reduce(out=imp[:], in_=sc[:].rearrange("p (g q) n -> p g n q", g=2), axis=AX.X, op=ALU.add),
                 [Ts], [Ts])
            bc2 = lambda ap: ap.unsqueeze(1).to_broadcast([128, 2, 32])
            tt("dve", imp[:], imp[:], bc2(cmC.t[:, 2, i, :]), ALU.mult, [Ts, cmC.tok], [Ts])
            tt("dve", imp[:], imp[:], bc2(cmC.t[:, 3, i, :]), ALU.add, [Ts, cmC.tok], [Ts])
            for g in range(2):
                P.op("dve", lambda e, g=g: e.max(out=m8[:, g, 0:8], in_=imp[:, g, :]), [Ts], [Ts])
                P.op("dve", lambda e, g=g: e.match_replace(out=wk[:, g, :], in_to_replace=m8[:, g, 0:8], in_values=imp[:, g, :],
                                                           imm_value=-1e9), [Ts], [Ts])
                P.op("dve", lambda e, g=g: e.max(out=m8[:, g, 8:16], in_=wk[:, g, :]), [Ts], [Ts])
                ts("dve", t1[:, g, 0:32], imp[:, g, :], m8[:, g, 15:16], None, ALU.is_ge, None, [Ts], [Ts])
            ts("dve", t2[:], imp[:], -0.5, None, ALU.is_gt, None, [Ts], [Ts])
            tt("dve", t1[:, :, 0:32], t1[:, :, 0:32], t2[:], ALU.mult, [Ts], [Ts])
            ts("dve", t1[:, :, 0:32], t1[:, :, 0:32], -NEG, NEG, ALU.mult, ALU.add, [Ts], [Ts])
            ptm = PS[0]
            tr(ptm.t[:, 0:128], t1[:].rearrange("p g n -> p (g n)"), identf.t[:], [Ts, identf.tok], [ptm.tok])
            cp("dve", MT[:, t0:t0 + 128], ptm.t[:, 0:128], [ptm.tok], [TMT])
            for g in range(2):
                tr(ptm.t[:, 128 + g * 128:256 + g * 128], sc[:, g * 4:(g + 1) * 4, :].rearrange("p h n -> p (h n)"), identf.t[:],
                   [Ts, identf.tok], [ptm.tok])
            cp("dve", pT[:], ptm.t[:, 128:384].rearrange("p (g t) -> p g t", g=2), [ptm.tok], [Ts])
            psO = PS[1]
            for g in range(2):
                for pr in range(2):
                    mm(psO.t[:, (g * 2 + pr) * 128:(g * 2 + pr + 1) * 128], pT[pr * 64:(pr + 1) * 64, g, :],
                       Vbd2[pr * 64:(pr + 1) * 64, g, :], True, True, [Ts, Tc], [psO.tok])
            tt("dve", oacc[:], psO.t[:].rearrange("p (h d) -> p h d", d=64),
               A["gsig"].t[:, i, 0:8].unsqueeze(2).to_broadcast([128, 8, 64]), ALU.mult, [psO.tok, A["gsig"].tok], [To])
            if "oc" in TAP:
                P.dma("sp", "tap", lambda e, i=i: e.dma_start(out=TAP["oc"][:, i], in_=oacc[:]), [To], [out_tok])
            for g in range(2):
                if not cfg.get("nosel"):
                    dense_branch(i, g, t0, A["KT3"].t[:, 2, :], A["vsel"], PS[4 + g], list(range(i + 1)), True, 8)
            for g in range(2):
                if not cfg.get("nowin"):
                    dense_branch(i, g, t0, A["kwinT"].t, A["vwin"], PS[6 + g], list(range(max(0, i - 4), i + 1)), False, 16)
            if "o" in TAP:
                P.dma("sp", "tap", lambda e, i=i: e.dma_start(out=TAP["o"][:, i], in_=oacc[:]), [To], [out_tok])
            pto = PS[0]
            for c in range(4):
                tr(pto.t[:, c * 128:(c + 1) * 128], oacc[:].rearrange("p h d -> p (h d)")[:, c * 128:(c + 1) * 128], identf.t[:],
                   [To, identf.tok], [pto.tok])
            zv = A["zsT"].t[:, :, t0:t0 + 128]
            tt("dve", zv, pto.t[:].rearrange("p (c t) -> p c t", t=128), zv, ALU.mult, [pto.tok, A["zsT"].tok], [A["zsT"].tok])
        end_phase(m0)

    def attn_sample_phase(l, A):
        m0 = mem.mark()
        W1l = [Buf(mem.sb(f"W1s_{i}", [128, 64, 128], BF16), f"W1s_{i}") for i in range(2)]
        W2 = Buf(mem.sb("W2s", [128, 3, 128], BF16), "W2s")
        for i_ in range(2):
            load_w1(l, i_, W1l[i_])
        compress_setup(l, A, None, W2)
        selc = mem.sb("selc", [16, 2, 129], F32)
        smat = mem.sb("smat", [16, 16], F32)
        mnew = mem.sb("mnew", [16, 4, 16], F32)
        delt = mem.sb("delt", [16, 4, 4], F32)
        mwin = mem.sb("mwin", [16, 512], F32)
        iot2 = mem.sb("iot2", [128, 1], F32)
        Tcs = Tok("scst")
        for dst, src in ((selc, "s_sel"), (smat, "s_smat"), (mnew, "s_mnew"), (delt, "s_delta"), (mwin, "s_mwin"), (iot2, "iot2")):
            ld("sp", "c", dst[:], D[src], [Tcs])
        rowsK = Buf(mem.sb("rowsK", [128, 8256], BF16), "rowsK")
        rowsV = Buf(mem.sb("rowsV", [128, 8256], BF16), "rowsV")
        vselp = rowsV.t[:, 0:8192].rearrange("p (g t) -> p g t", t=128)
        pgb = [Buf(mem.sb(f"pgb{i}", [128, 256], F32), f"pgb{i}") for i in range(4)]
        pti = mem.sb("pti", [128, 64], I32)
        ptf = mem.sb("ptf", [128, 64], F32)
        idx = [mem.sb(f"idx{h}", [128, 64], I32) for h in range(2)]
        Ti = Tok("idx")
        cb = mem.sb("cbs", [128, 2], F32)
        silk = mem.sb("silks", [128, 128], BF16)
        silv = mem.sb("silvs", [128, 128], BF16)
        kcT = mem.sb("kcTs", [128, 128], BF16)
        vcS = mem.sb("vcS", [128, 128], BF16)
        kt1, kt2 = mem.sb("kt1s", [128, 128], F32), mem.sb("kt2s", [128, 128], F32)
        Tc = Tok("cmps")
        sc = mem.sb("scs", [16, 2, 128], F32)
        mx = mem.sb("mxs", [16, 2], F32)
        impx = mem.sb("impx", [16, 2, 129], F32)
        wk = mem.sb("wks", [16, 2, 129], F32)
        m8 = mem.sb("m8s", [16, 2, 16], F32)
        madd = mem.sb("madd", [16, 2, 129], F32)
        Ts = Tok("scs")
        sch = mem.sb("sch", [16, 2048], F32)
        snew = mem.sb("snew", [16, 16], F32)
        mxc = mem.sb("mxc", [16, 8], F32)
        gmx = mem.sb("gmx", [16, 1], F32)
        sums = mem.sb("sums", [16, 8], F32)
        PTb = mem.sb("PTb", [128, 16, 16], BF16)
        PTn = mem.sb("PTn", [16, 16], BF16)
        Tp = Tok("sch")
        swt = mem.sb("swt", [128, 4, 256], F32)
        kwT = mem.sb("kwTs", [128, 512], BF16)
        vwS = mem.sb("vwS", [128, 4, 128], BF16)
        swn = mem.sb("swn", [16, 512], F32)
        Tw = Tok("win")
        grep = mem.sb("grep", [16, 16], F32)
        gsb = mem.sb("gsb", [128, 16], F32)
        rsum = mem.sb("rsum", [128, 16], F32)
        oTs = mem.sb("oTs", [128, 2, 2, 4], F32)
        otmp = mem.sb("otmp", [128, 8], F32)
        To = Tok("oTs")
        ones_f = mem.sb("ones_f", [16, 128], F32)
        ms("dve", ones_f[:], 1.0, [Tcs])

        qsb = mem.sb("qsb", [128, 4, 16], BF16)
        cp("dve", qsb[:].rearrange("p b (j h) -> p b j h", h=4), A["qTs"].t[:].rearrange("p h (b j) -> p b j h", j=4),
           [A["qTs"].tok], [A["qTs"].tok])

        def qop(b, g):
            return qsb[g * 64:(g + 1) * 64, b, :]

        def finish_branch(b, g, branch, psO, psSum, first):
            gcol = branch * 8 + g * 4
            tt("dve", grep[:].rearrange("p (j h) -> p j h", h=4), delt[:, b, :].unsqueeze(2).to_broadcast([16, 4, 4]),
               A["gsig"].t[0:16, 16, gcol:gcol + 4].unsqueeze(1).to_broadcast([16, 4, 4]), ALU.mult, [Tcs, A["gsig"].tok], [To])
            pg_ = PS[7]
            mm(pg_.t[:, 0:16], ones_f[:], grep[:], True, True, [To, Tcs], [pg_.tok])
            P.op("dve", lambda e: e.reciprocal(out=rsum[:], in_=psSum.t[:, 0:16]), [psSum.tok], [To])
            tt("dve", gsb[:], rsum[:], pg_.t[:, 0:16], ALU.mult, [To, pg_.tok], [To])
            gv = gsb[:].rearrange("p (j r q) -> p j r q", r=2, q=2)
            for par in range(2):
                rows = slice(par * 64, par * 64 + 64)
                ov = psO.t[rows, 0:8].rearrange("p (j r) -> p r j", r=2)
                gvv = gv[rows, :, :, par].rearrange("p j r -> p r j")
                if first:
                    tt("dve", oTs[rows, g, :, :], ov, gvv, ALU.mult, [psO.tok, To], [To])
                else:
                    tt("dve", otmp[rows, :].rearrange("p (r j) -> p r j", j=4), ov, gvv, ALU.mult, [psO.tok, To], [To])
                    tt("dve", oTs[rows, g, :, :], oTs[rows, g, :, :], otmp[rows, :].rearrange("p (r j) -> p r j", j=4),
                       ALU.add, [To], [To])

        def pv_T(tiles, psO, psSum):
            n = len(tiles)
            for par in range(2):
                for k, (Vap, PTap) in enumerate(tiles):
                    rhs = PTap.rearrange("p (j r q) -> p j r q", r=2, q=2)[:, :, :, par]
                    mm(psO.t[par * 64:(par + 1) * 64, 0:8], Vap, rhs, k == 0, k == n - 1, [Tp, Tc, Tw, rowsV.tok, A["S_v"].tok], [psO.tok])
            for k, (Vap, PTap) in enumerate(tiles):
                K_ = list(PTap.shape)[0]
                mm(psSum.t[:, 0:16], onesb.t[0:K_, :], PTap, k == 0, k == n - 1, [Tp, Tc, Tw, onesb.tok], [psSum.tok])

        def transposes(src_ap_fn, ntile, kw, dstPT, Tsrc):
            ptp = PS[4]
            for t in range(ntile):
                tr(ptp.t[0:kw, t * 16:(t + 1) * 16], src_ap_fn(t), identf.t[0:16, 0:16], [Tsrc, identf.tok], [ptp.tok])
            cp("dve", dstPT[0:kw, 0:ntile, :], ptp.t[0:kw, 0:ntile * 16].rearrange("p (t c) -> p t c", c=16), [ptp.tok], [Tp])

        cache = D["cache2"]
        for b in range(4):
            P.dma("sp", "c", lambda e, b=b: e.dma_start(out=pti[:], in_=D["ptab"][b:b + 1, :].to_broadcast([128, 64])), [Ti], [Ti])
            cp("dve", ptf[:], pti[:], [Ti], [Ti])
            ts("dve", ptf[:], ptf[:], 1024.0, float(l * 256), ALU.mult, ALU.add, [Ti], [Ti])
            ts("dve", ptf[:], ptf[:], iot2[:, 0:1], None, ALU.add, None, [Ti, Tcs], [Ti])
            cp("dve", idx[0][:], ptf[:], [Ti], [Ti])
            ts("dve", ptf[:], ptf[:], 1.0, None, ALU.add, None, [Ti], [Ti])
            cp("dve", idx[1][:], ptf[:], [Ti], [Ti])
            for strm in range(2):
                ld("pool", "pe", (rowsK if strm == 0 else rowsV).t[:, 8192:8256], D["peT"][l, strm], [(rowsK if strm == 0 else rowsV).tok])
            for pp in range(32):
                ptp = PS[pp % 2]
                for k in range(2):
                    pg = pp * 2 + k
                    buf = pgb[pg % 4]
                    P.dma("pool", "g", lambda e, buf=buf, pg=pg: e.indirect_dma_start(
                        out=buf.t[:], out_offset=None, in_=cache[:, :],
                        in_offset=bass.IndirectOffsetOnAxis(ap=idx[0][:, pg:pg + 1], axis=0)), [Ti], [buf.tok])
                    for s_ in range(2):
                        tr(ptp.t[:, (k * 2 + s_) * 128:(k * 2 + s_ + 1) * 128], buf.t[:, s_ * 128:(s_ + 1) * 128], identf.t[:],
                           [buf.tok, identf.tok], [ptp.tok])
                pv4 = ptp.t[:].rearrange("p (k s t) -> p s k t", k=2, s=2)
                cp("dve", rowsK.t[:, pp * 256:(pp + 1) * 256].rearrange("p (k t) -> p k t", k=2), pv4[:, 0], [ptp.tok], [rowsK.tok])
                act(rowsV.t[:, pp * 256:(pp + 1) * 256].rearrange("p (k t) -> p k t", k=2), pv4[:, 1], AF.Copy, [ptp.tok], [rowsV.tok])
            chk("as1")
            for strm in range(2):
                psC = PS[2]
                rb = rowsK if strm == 0 else rowsV
                compress(l, strm, rb.t[:, 0:8256], 128, W1l[strm], psC, rb.tok)
                cp("dve", cb[:, strm:strm + 1], psC.t[:, 128:129], [psC.tok], [Tc])
                act((silk if strm == 0 else silv)[:], psC.t[:, 0:128], AF.Silu, [psC.tok, Tc], [Tc], bias=cb[:, strm:strm + 1])
            ps = PS[3]
            mm(ps.t[:, 0:128], W2.t[:, 0, :], silk[:], True, True, [W2.tok, Tc], [ps.tok])
            mm(ps.t[:, 128:256], W2.t[:, 2, :], silk[:], True, True, [W2.tok, Tc], [ps.tok])
            tt("dve", kt1[:], ps.t[:, 0:128], ropeC.t[:, 0, :], ALU.mult, [ps.tok, ropeC.tok], [Tc])
            tt("dve", kt2[:], ps.t[:, 128:256], ropeC.t[:, 1, :], ALU.mult, [ps.tok, ropeC.tok], [Tc])
            tt("dve", kcT[:], kt1[:], kt2[:], ALU.add, [Tc], [Tc])
            mm(ps.t[:, 256:384], silv[:], W2.t[:, 1, :], True, True, [W2.tok, Tc], [ps.tok])
            cp("dve", vcS[:], ps.t[:, 256:384], [ps.tok], [Tc])
            psS = PS[3]
            for g in range(2):
                mm(psS.t[0:16, g * 128:(g + 1) * 128], qop(b, g), kcT[g * 64:(g + 1) * 64, :], True, True, [A["qTs"].tok, Tc], [psS.tok])
            cp("dve", sc[:], psS.t[0:16, 0:256].rearrange("p (g n) -> p g n", g=2), [psS.tok], [Ts])
            P.op("dve", lambda e: e.tensor_reduce(out=mx[:], in_=sc[:], axis=AX.X, op=ALU.max), [Ts], [Ts])
            tt("dve", sc[:], sc[:], mx[:].unsqueeze(2).to_broadcast([16, 2, 128]), ALU.subtract, [Ts], [Ts])
            act(sc[:], sc[:], AF.Exp, [Ts], [Ts])
            P.op("dve", lambda e: e.tensor_reduce(out=mx[:], in_=sc[:], axis=AX.X, op=ALU.add), [Ts], [Ts])
            P.op("dve", lambda e: e.reciprocal(out=mx[:], in_=mx[:]), [Ts], [Ts])
            tt("dve", sc[:], sc[:], mx[:].unsqueeze(2).to_broadcast([16, 2, 128]), ALU.mult, [Ts], [Ts])
            psI = PS[2]
            mm(psI.t[0:16, 0:256], smat[:], sc[:].rearrange("p g n -> p (g n)"), True, True, [Ts, Tcs], [psI.tok])
            tt("dve", impx[:, :, 0:128], psI.t[0:16, 0:256].rearrange("p (g n) -> p g n", g=2),
               selc[:, 0, 0:128].unsqueeze(1).to_broadcast([16, 2, 128]), ALU.mult, [psI.tok, Tcs], [Ts])
            tt("dve", impx[:, :, 0:128], impx[:, :, 0:128], selc[:, 1, 0:128].unsqueeze(1).to_broadcast([16, 2, 128]), ALU.add, [Ts, Tcs], [Ts])
            cp("dve", impx[:, :, 128:129], selc[:, 1, 128:129].unsqueeze(1).to_broadcast([16, 2, 1]), [Tcs], [Ts])
            for g in range(2):
                P.op("dve", lambda e, g=g: e.max(out=m8[:, g, 0:8], in_=impx[:, g, :]), [Ts], [Ts])
                P.op("dve", lambda e, g=g: e.match_replace(out=wk[:, g, :], in_to_replace=m8[:, g, 0:8], in_values=impx[:, g, :],
                                                           imm_value=-1e9), [Ts], [Ts])
                P.op("dve", lambda e, g=g: e.max(out=m8[:, g, 8:16], in_=wk[:, g, :]), [Ts], [Ts])
                ts("dve", madd[:, g, :], impx[:, g, :], m8[:, g, 15:16], None, ALU.is_ge, None, [Ts], [Ts])
            ts("dve", madd[:], madd[:], -NEG, NEG, ALU.mult, ALU.add, [Ts], [Ts])
            for g in range(2):
                transposes(lambda t, g=g: sc[:, g, :], 1, 128, PTb, Ts)
                pv_T([(vcS[:, g * 64:(g + 1) * 64], PTb[:, 0, :])], PS[5], PS[6])
                finish_branch(b, g, 0, PS[5], PS[6], True)
            chk("as2")
            for pq in range(16):
                ptp = PS[pq % 2]
                for k in range(4):
                    pg = pq * 4 + k
                    buf = pgb[pg % 4]
                    P.dma("pool", "g", lambda e, buf=buf, pg=pg: e.indirect_dma_start(
                        out=buf.t[:], out_offset=None, in_=cache[:, :],
                        in_offset=bass.IndirectOffsetOnAxis(ap=idx[1][:, pg:pg + 1], axis=0)), [Ti], [buf.tok])
                    tr(ptp.t[:, k * 128:(k + 1) * 128], buf.t[:, 0:128], identf.t[:], [buf.tok, identf.tok], [ptp.tok])
                    act(vselp[:, pg, :], buf.t[:, 128:256], AF.Copy, [buf.tok], [rowsV.tok])
                cp("dve", rowsK.t[:, pq * 512:(pq + 1) * 512], ptp.t[:], [ptp.tok], [rowsK.tok])
            chk("as3")
            P.dma("sp", "w", lambda e, b=b: e.dma_start(out=swt[:], in_=D["swin"][b, l].rearrange("(t p) c -> p t c", p=128)), [Tw], [Tw])
            ptw = PS[2]
            for t in range(4):
                tr(ptw.t[:, t * 128:(t + 1) * 128], swt[:, t, 0:128], identf.t[:], [Tw, identf.tok], [ptw.tok])
            cp("dve", kwT[:], ptw.t[:], [ptw.tok], [Tw])
            cp("dve", vwS[:], swt[:, :, 128:256], [Tw], [Tw])
            for g in range(2):
                def sel_chunk(c):
                    for k in range(4):
                        pk = PS[k]
                        mm(pk.t[0:16, :], qop(b, g), rowsK.t[g * 64:(g + 1) * 64, c * 2048 + k * 512:c * 2048 + (k + 1) * 512],
                           True, True, [A["qTs"].tok, rowsK.tok], [pk.tok])
                        tt("dve", sch[:, k * 512:(k + 1) * 512].rearrange("p (n e) -> p n e", e=64),
                           pk.t[0:16, :].rearrange("p (n e) -> p n e", e=64),
                           madd[:, g, c * 32 + k * 8:c * 32 + k * 8 + 8].unsqueeze(2).to_broadcast([16, 8, 64]), ALU.add,
                           [pk.tok, Ts], [Tp])
                pn = PS[7]
                mm(pn.t[0:16, 0:16], qop(b, g), A["S_kT"].t[g * 64:(g + 1) * 64, 0, :], True, True, [A["qTs"].tok, A["S_kT"].tok], [pn.tok])
                tt("dve", snew[:], pn.t[0:16, 0:16], mnew[:, b, :], ALU.add, [pn.tok, Tcs], [Tp])
                ts("dve", snew[:], snew[:], madd[:, g, 128:129], None, ALU.add, None, [Tp, Ts], [Tp])
                tiles = []
                psO, psSum = PS[5], PS[6]
                first = True
                for c in range(4):
                    sel_chunk(c)
                    act(sch[:], sch[:], AF.Exp, [Tp], [Tp])
                    transposes(lambda t: sch[:, t * 128:(t + 1) * 128], 16, 128, PTb, Tp)
                    for par in range(2):
                        for t in range(16):
                            rhs = PTb[:, t, :].rearrange("p (j r q) -> p j r q", r=2, q=2)[:, :, :, par]
                            mm(psO.t[par * 64:(par + 1) * 64, 8 * c:8 * c + 8], vselp[:, c * 16 + t, g * 64:(g + 1) * 64], rhs,
                               t == 0, t == 15, [Tp, rowsV.tok], [psO.tok])
                    for t in range(16):
                        mm(psSum.t[:, 16 * c:16 * c + 16], onesb.t[:], PTb[:, t, :], t == 0, t == 15, [Tp, onesb.tok], [psSum.tok])
                act(snew[:], snew[:], AF.Exp, [Tp], [Tp])
                ptn = PS[4]
                tr(ptn.t[0:16, 0:16], snew[:], identf.t[0:16, 0:16], [Tp, identf.tok], [ptn.tok])
                cp("dve", PTn[:], ptn.t[0:16, 0:16], [ptn.tok], [Tp])
                for par in range(2):
                    rhs = PTn[:].rearrange("p (j r q) -> p j r q", r=2, q=2)[:, :, :, par]
                    mm(psO.t[par * 64:(par + 1) * 64, 32:40], A["S_v"].t[:, 0, g, :], rhs, True, True, [Tp, A["S_v"].tok], [psO.tok])
                mm(psSum.t[:, 64:80], onesb.t[0:16, :], PTn[:], True, True, [Tp, onesb.tok], [psSum.tok])
                cp("dve", otmp[:, 0:8], psO.t[:, 0:8], [psO.tok], [To])
                cp("dve", rsum[:], psSum.t[:, 0:16], [psSum.tok], [To])
                for k in range(1, 5):
                    tt("dve", otmp[:, 0:8], psO.t[:, 8 * k:8 * k + 8], otmp[:, 0:8], ALU.add, [psO.tok, To], [To])
                    tt("dve", rsum[:], psSum.t[:, 16 * k:16 * k + 16], rsum[:], ALU.add, [psSum.tok, To], [To])
                finish_from_sbuf(b, g, 1, otmp, rsum, finish_branch, grep, delt, gsb, oTs, To, Tcs, A, ones_f)
                pw_ = PS[0]
                mm(pw_.t[0:16, :], qop(b, g), kwT[g * 64:(g + 1) * 64, :], True, True, [A["qTs"].tok, Tw], [pw_.tok])
                tt("dve", swn[:], pw_.t[0:16, :], mwin[:], ALU.add, [pw_.tok, Tcs], [Tp])
                pn = PS[7]
                mm(pn.t[0:16, 0:16], qop(b, g), A["S_kT"].t[g * 64:(g + 1) * 64, 1, :], True, True, [A["qTs"].tok, A["S_kT"].tok], [pn.tok])
                tt("dve", snew[:], pn.t[0:16, 0:16], mnew[:, b, :], ALU.add, [pn.tok, Tcs], [Tp])
                P.op("dve", lambda e: e.tensor_reduce(out=mxc[:, 0:1], in_=swn[:], axis=AX.X, op=ALU.max), [Tp], [Tp])
                P.op("dve", lambda e: e.tensor_reduce(out=mxc[:, 1:2], in_=snew[:], axis=AX.X, op=ALU.max), [Tp], [Tp])
                P.op("dve", lambda e: e.tensor_reduce(out=gmx[:], in_=mxc[:, 0:2], axis=AX.X, op=ALU.max), [Tp], [Tp])
                ts("dve", gmx[:], gmx[:], -1.0, None, ALU.mult, None, [Tp], [Tp])
                act(swn[:], swn[:], AF.Exp, [Tp], [Tp], bias=gmx[:, 0:1])
                act(snew[:], snew[:], AF.Exp, [Tp], [Tp], bias=gmx[:, 0:1])
                transposes(lambda t: swn[:, t * 128:(t + 1) * 128], 4, 128, PTb, Tp)
                ptn = PS[4]
                tr(ptn.t[0:16, 256:272], snew[:], identf.t[0:16, 0:16], [Tp, identf.tok], [ptn.tok])
                cp("dve", PTn[:], ptn.t[0:16, 256:272], [ptn.tok], [Tp])
                tiles = [(vwS[:, t, g * 64:(g + 1) * 64], PTb[:, t, :]) for t in range(4)] + [(A["S_v"].t[:, 1, g, :], PTn[:])]
                pv_T(tiles, PS[5], PS[6])
                finish_branch(b, g, 2, PS[5], PS[6], False)
            zv = A["zsT"].t[:, :, TP + 4 * b:TP + 4 * b + 4]
            tt("dve", zv, zv, oTs[:].rearrange("p g r j -> p (g r) j"), ALU.mult, [To, A["zsT"].tok], [A["zsT"].tok])
        end_phase(m0)

    def finish_from_sbuf(b, g, branch, osb, ssb, finish_branch, grep, delt, gsb, oTs, To, Tcs, A, ones_f):
        gcol = branch * 8 + g * 4
        tt("dve", grep[:].rearrange("p (j h) -> p j h", h=4), delt[:, b, :].unsqueeze(2).to_broadcast([16, 4, 4]),
           A["gsig"].t[0:16, 16, gcol:gcol + 4].unsqueeze(1).to_broadcast([16, 4, 4]), ALU.mult, [Tcs, A["gsig"].tok], [To])
        pg_ = PS[7]
        mm(pg_.t[:, 0:16], ones_f[:], grep[:], True, True, [To, Tcs], [pg_.tok])
        P.op("dve", lambda e: e.reciprocal(out=ssb[:], in_=ssb[:]), [To], [To])
        tt("dve", gsb[:], ssb[:], pg_.t[:, 0:16], ALU.mult, [To, pg_.tok], [To])
        gv = gsb[:].rearrange("p (j r q) -> p j r q", r=2, q=2)
        for par in range(2):
            rows = slice(par * 64, par * 64 + 64)
            ov = osb[rows, 0:8].rearrange("p (j r) -> p r j", r=2)
            gvv = gv[rows, :, :, par].rearrange("p j r -> p r j")
            tt("dve", osb[rows, 0:8].rearrange("p (j r) -> p r j", r=2), ov, gvv, ALU.mult, [To], [To])
            tt("dve", oTs[rows, g, :, :], oTs[rows, g, :, :], osb[rows, 0:8].rearrange("p (j r) -> p r j", r=2), ALU.add, [To], [To])

    def merge_phase(l, hT, y5gT, onsaT):
        m0 = mem.mark()
        mg = Buf(mem.sb("mg", [128, 8, NT], BF16), "mg", 5)
        wv = D["w_in"][l].rearrange("(kc p) n -> p kc n", p=128)
        w5v = D["w_s5_out"][l].rearrange("(kc p) n -> p kc n", p=128)
        wnv = D["w_nsa_out"][l].rearrange("(kc p) n -> p kc n", p=128)
        wov = D["w_o"][l].rearrange("(kc p) n -> p kc n", p=128)
        wms = Buf(mem.sb("wms", [128, 8, 256], BF16), "wms")
        wmn = Buf(mem.sb("wmn", [128, 8, 256], BF16), "wmn")
        w5 = Buf(mem.sb("w5", [128, 4, 256], BF16), "w5")
        wn = Buf(mem.sb("wn", [128, 4, 256], BF16), "wn")
        wo = Buf(mem.sb("wo", [128, 8, 256], BF16), "wo")
        sg1s = [Buf(mem.sb(f"sg1_{i}", [128, 512], F32), f"sg1_{i}") for i in range(2)]
        sg2s = [Buf(mem.sb(f"sg2_{i}", [128, 512], F32), f"sg2_{i}") for i in range(2)]
        sg1, sg2 = sg1s[0], sg2s[0]
        it_ = 0
        for grp in range(4):
            c0g = grp * 256
            for kc in range(8):
                ld("pool", "w", wms.t[:, kc, :], wv[:, kc, MG_OFF + c0g:MG_OFF + c0g + 256], [wms.tok])
                ld("pool", "w", wmn.t[:, kc, :], wv[:, kc, MG_OFF + 1024 + c0g:MG_OFF + 1024 + c0g + 256], [wmn.tok])
            for kc in range(4):
                ld("pool", "w", w5.t[:, kc, :], w5v[:, kc, c0g:c0g + 256], [w5.tok])
                ld("pool", "w", wn.t[:, kc, :], wnv[:, kc, c0g:c0g + 256], [wn.tok])
            for o2 in range(2):
                oc = grp * 2 + o2
                cs = slice(o2 * 128, (o2 + 1) * 128)
                for b_, (c0, c1) in enumerate(BANKS):
                    w = c1 - c0
                    o4 = 4 * (it_ % 2)
                    p1, p2, p3, p4 = PS[o4], PS[o4 + 1], PS[o4 + 2], PS[o4 + 3]
                    sg1, sg2 = sg1s[it_ % 2], sg2s[it_ % 2]
                    it_ += 1
                    for kc in range(8):
                        mm(p1.t[:, 0:w], wms.t[:, kc, cs], hT.t[:, kc, c0:c1], kc == 0, kc == 7, [wms.tok, hT.toks[b_]], [p1.tok])
                    for kc in range(8):
                        mm(p2.t[:, 0:w], wmn.t[:, kc, cs], hT.t[:, kc, c0:c1], kc == 0, kc == 7, [wmn.tok, hT.toks[b_]], [p2.tok])
                    for kc in range(4):
                        mm(p3.t[:, 0:w], w5.t[:, kc, cs], y5gT.t[:, kc, c0:c1], kc == 0, kc == 3, [w5.tok, y5gT.tok], [p3.tok])
                    for kc in range(4):
                        mm(p4.t[:, 0:w], wn.t[:, kc, cs], onsaT.t[:, kc, c0:c1], kc == 0, kc == 3, [wn.tok, onsaT.tok], [p4.tok])
                    act(sg1.t[:, 0:w], p1.t[:, 0:w], AF.Sigmoid, [p1.tok], [sg1.tok])
                    act(sg2.t[:, 0:w], p2.t[:, 0:w], AF.Sigmoid, [p2.tok], [sg2.tok])
                    tt("dve", sg1.t[:, 0:w], sg1.t[:, 0:w], p3.t[:, 0:w], ALU.mult, [sg1.tok, p3.tok], [sg1.tok])
                    tt("dve", sg2.t[:, 0:w], sg2.t[:, 0:w], p4.t[:, 0:w], ALU.mult, [sg2.tok, p4.tok], [sg2.tok])
                    tt("dve", mg.t[:, oc, c0:c1], sg1.t[:, 0:w], sg2.t[:, 0:w], ALU.add, [sg1.tok, sg2.tok], [mg.toks[b_]])
        for grp in range(4):
            c0g = grp * 256
            for kc in range(8):
                ld("pool", "w", wo.t[:, kc, :], wov[:, kc, c0g:c0g + 256], [wo.tok])
            for o2 in range(2):
                oc = grp * 2 + o2
                cs = slice(o2 * 128, (o2 + 1) * 128)
                for b_, (c0, c1) in enumerate(BANKS):
                    w = c1 - c0
                    ps = PS[4 + (b_ % 2)]
                    for kc in range(8):
                        mm(ps.t[:, 0:w], wo.t[:, kc, cs], mg.t[:, kc, c0:c1], kc == 0, kc == 7, [wo.tok, mg.toks[b_]], [ps.tok])
                    if b_ < 4:
                        stt(xT.t[:, oc, c0:c1], ps.t[:, 0:w], modT.t[:, 16 + oc, 0:1], xT.t[:, oc, c0:c1], ALU.mult, ALU.add,
                            [ps.tok, modT.tok, xT.toks[b_]], [xT.toks[b_]])
                    else:
                        tt("dve", sg1.t[:, 0:w], ps.t[:, 0:w], gtS.t[:, oc, :], ALU.mult, [ps.tok, gtS.tok], [sg1.tok])
                        tt("dve", xT.t[:, oc, c0:c1], xT.t[:, oc, c0:c1], sg1.t[:, 0:w], ALU.add, [sg1.tok, xT.toks[b_]], [xT.toks[b_]])
        end_phase(m0)

    def layer(l):
        if stop_after == "pro":
            return
        adaln_phase(l)
        tap("modT", modT.t[:], [modT.tok])
        if stop_after == "ada":
            return
        mL = mem.mark()
        uT = Buf(mem.sb("uT", [128, 4, NT], BF16), "uT")
        zs5T = Buf(mem.sb("zs5T", [128, 4, NT], BF16), "zs5T")
        m1 = mem.mark()
        hT = Buf(mem.sb("hT", [128, 8, NT], BF16), "hT", 5)
        norm_phase(hT)
        tap("hT", hT.t[:], hT.toks)
        if stop_after == "norm":
            return
        uz_proj_phase(l, hT, uT, zs5T)
        end_phase(m1)
        tap("uT", uT.t[:], [uT.tok])
        if stop_after == "uz":
            return
        s5_phase(l, uT, zs5T)
        if stop_after == "s5":
            return
        glu_phase(l, uT, zs5T)
        tap("y5g", uT.t[:], [uT.tok])
        if stop_after == "glu":
            raise _Stop()
        A = {}
        mA = mem.mark()
        A["qTs"] = Buf(mem.sb("qTs", [128, 4, 16], BF16), "qTs")
        A["S_kT"] = Buf(mem.sb("S_kT", [128, 2, 16], BF16), "S_kT")
        A["S_v"] = Buf(mem.sb("S_v", [16, 2, 2, 64], BF16), "S_v")
        A["gsig"] = Buf(mem.sb("gsig", [128, 17, 24], F32), "gsig")
        mB = mem.mark()
        A["qT"] = Buf(mem.sb("qT", [128, 4, TP], BF16), "qT")
        A["KT3"] = Buf(mem.sb("KT3", [128, 3, 2112], BF16), "KT3")
        A["kwinT"] = Buf(mem.sb("kwinT", [128, NT], BF16), "kwinT")
        A["vsel"] = Buf(mem.sb("vsel", [128, 17, 2, 65], BF16), "vsel")
        A["vwin"] = Buf(mem.sb("vwin", [128, 17, 2, 65], BF16), "vwin")
        A["zsT"] = zs5T
        ms("pool", A["vsel"].t[:], 1.0, [A["vsel"].tok])
        ms("pool", A["vwin"].t[:], 1.0, [A["vwin"].tok])
        m2 = mem.mark()
        hT = Buf(mem.sb("hT", [128, 8, NT], BF16), "hT", 5)
        norm_phase(hT)
        tm_proj_phase(l, hT, A)
        end_phase(m2)
        for n_ in ("qT", "KT3", "kwinT", "zsT", "gsig", "vsel", "vwin", "qTs"):
            tap(n_, A[n_].t[:], [A[n_].tok])
        if stop_after == "tm":
            raise _Stop()
        attn_prompt_phase(l, A)
        tap("onsaT", A["zsT"].t[:], [A["zsT"].tok])
        if stop_after == "ap":
            raise _Stop()
        end_phase(mB)
        attn_sample_phase(l, A)
        tap("onsaTs", A["zsT"].t[:, :, TP:NT], [A["zsT"].tok])
        if stop_after == "as":
            raise _Stop()
        end_phase(mA)
        m3 = mem.mark()
        hT = Buf(mem.sb("hT", [128, 8, NT], BF16), "hT", 5)
        norm_phase(hT)
        merge_phase(l, hT, uT, A["zsT"])
        end_phase(m3)
        tap(f"x{l}", xT.t[:], xT.toks)
        end_phase(mL)

    def final_phase():
        m0 = mem.mark()
        sq = Buf(mem.sb("sq", [128, 8, 256], BF16), "sq")
        xn = Buf(mem.sb("xn", [128, 8, 256], F32), "xn")
        rt = Buf(mem.sb("rt", [128, 256], F32), "rt")
        yo = [Buf(mem.sb(f"yo{i}", [128, 8, 256], F32), f"yo{i}") for i in range(2)]
        fg = Buf(mem.sb("fg", [128, 8], F32), "fg")
        ld("sp", "c", fg.t[:], D["final_gT"], [fg.tok])
        chunks = [(i * 256, i * 256 + 256) for i in range(8)] + [(TP, NT)]
        for ci, (c0, c1) in enumerate(chunks):
            w = c1 - c0
            b_ = min(c0 // 512, 4)
            ps = PS[1 + (ci % 2)]
            y_ = yo[ci % 2]
            act(sq.t[:, :, 0:w], xT.t[:, :, c0:c1], AF.Square, [xT.toks[b_]], [sq.tok])
            for kc in range(8):
                mm(ps.t[:, 0:w], onesb.t[:], sq.t[:, kc, 0:w], kc == 0, kc == 7, [onesb.tok, sq.tok], [ps.tok])
            act(rt.t[:, 0:w], ps.t[:, 0:w], AF.Sqrt, [ps.tok], [rt.tok], bias=1e-6, scale=1.0 / 1024.0)
            P.op("dve", lambda e, w=w: e.reciprocal(out=rt.t[:, 0:w], in_=rt.t[:, 0:w]), [rt.tok], [rt.tok])
            tt("dve", xn.t[:, :, 0:w], xT.t[:, :, c0:c1], rt.t[:, 0:w].unsqueeze(1).to_broadcast([128, 8, w]),
               ALU.mult, [xT.toks[b_], rt.tok], [xn.tok])
            tt("dve", y_.t[:, :, 0:w], xn.t[:, :, 0:w], fg.t[:].unsqueeze(2).to_broadcast([128, 8, w]), ALU.mult,
               [xn.tok, fg.tok], [y_.tok])
            P.dma("sp", "y", lambda e, y_=y_, c0=c0, c1=c1, w=w: e.dma_start(out=O["yT"][:, :, c0:c1], in_=y_.t[:, :, 0:w]),
                  [y_.tok], [out_tok])
        end_phase(m0)

    try:
        for l in range(nlayers):
            layer(l)
        final_phase()
    except _Stop:
        pass

    print('OPCOUNTS', P.cnt, P.dcnt)
    P.barrier()
    P.emit()
    P.close()
    mem.release(0)
    return nc


_NC_CACHE = {}


def _shared_inputs(inp):
    m = {}
    m["ident"] = np.eye(128, dtype=np.float32)
    m["ada_w"] = np.ascontiguousarray(inp["ada_w"], np.float32)
    m["ada_bT"] = np.ascontiguousarray(inp["ada_b"].reshape(DEPTH, 24, 128).transpose(0, 2, 1))
    m["norm_gT"] = np.ascontiguousarray(inp["norm_g"].reshape(DEPTH, 8, 128).transpose(0, 2, 1))
    m["final_gT"] = np.ascontiguousarray(inp["final_g"].reshape(8, 128).T)
    m["w_in"] = np.ascontiguousarray(inp["w_in"], np.float32)
    aNL, aPR, BPR, BNL, CNL, dv = _s5_layouts(inp)
    m.update(aNL=aNL, aPR=aPR, BPR=BPR, BNL=BNL, CNL=CNL, dv=dv)
    m["glu_w"] = np.ascontiguousarray(inp["s5_glu_w"], np.float32)
    m["glu_bT"] = np.ascontiguousarray(inp["s5_glu_b"].reshape(DEPTH, 4, 128).transpose(0, 2, 1))
    m["w_s5_out"] = np.ascontiguousarray(inp["w_s5_out"], np.float32)
    m["w_nsa_out"] = np.ascontiguousarray(inp["w_nsa_out"], np.float32)
    m["w_o"] = np.ascontiguousarray(inp["w_o"], np.float32)
    cos, sin, cosC, sinC = _rope_tables()
    m["cosT"] = cos
    m["sinT"] = sin
    m["ropeC"] = np.ascontiguousarray(np.stack([cosC, sinC], 1))
    cm, E, tri = _attn_consts()
    m["cm"] = cm
    m["Ecst"] = E
    m["tri"] = tri
    W1bd, W2bd, peT = _cmp_layouts(inp)
    m["W1bd"] = W1bd
    m["W2bd"] = W2bd
    m["peT"] = peT
    npool = inp["cache_kv"].shape[0]
    m["cache2"] = np.ascontiguousarray(inp["cache_kv"], np.float32).reshape(npool * DEPTH * 128 * 2, 256)
    m["iot2"] = (2 * np.arange(128, dtype=np.float32)).reshape(128, 1)
    sel, Smat, masknew, delta, maskwin = _sample_consts()
    m.update(s_sel=sel, s_smat=Smat, s_mnew=masknew, s_delta=delta, s_mwin=maskwin)
    return m


def _core_inputs(inp, c, shared):
    m = dict(shared)
    xp = inp["x_prompt"][c]
    xs = inp["x_sample"][4 * c:4 * c + 4].reshape(16, 1024)
    x = np.concatenate([xp, xs], 0)
    m["xT0"] = np.ascontiguousarray(x.T.reshape(8, 128, NT).transpose(1, 0, 2))
    cc = np.concatenate([inp["c_prompt"][c:c + 1], inp["c_sample"][4 * c:4 * c + 4]], 0)
    m["cT"] = np.ascontiguousarray(cc.T.reshape(8, 128, 5).transpose(1, 0, 2))
    m["h0NL"] = _h0_layout(inp["state_ssm"][4 * c:4 * c + 4])
    m["swin"] = np.ascontiguousarray(inp["state_win"][4 * c:4 * c + 4].reshape(4, DEPTH, 512, 256))
    m["ptab"] = np.ascontiguousarray(inp["page_table"][4 * c:4 * c + 4].astype(np.int32))
    return m


def _assemble(results, ncores):
    B, DB = ncores, 4 * ncores
    y_p = np.zeros((B, TP, 1024), np.float32)
    y_s = np.zeros((DB, 4, 1024), np.float32)
    kv_p = np.zeros((B, DEPTH, TP, 4, 2, 64), np.float32)
    kv_s = np.zeros((DB, DEPTH, 4, 4, 2, 64), np.float32)
    win_p = np.zeros((B, DEPTH, 512, 2, 2, 64), np.float32)
    win_s = np.zeros((DB, DEPTH, 512, 2, 2, 64), np.float32)
    ssm_p = np.zeros((B, DEPTH, 2, 32, 64), np.float32)
    ssm_s = np.zeros((DB, DEPTH, 2, 32, 64), np.float32)
    for c, r in enumerate(results):
        yT = np.asarray(r["yT"])
        y = yT.transpose(2, 1, 0).reshape(NT, 1024)
        y_p[c] = y[:TP]
        y_s[4 * c:4 * c + 4] = y[TP:].reshape(4, 4, 1024)
        kv_p[c] = np.asarray(r["kvP"]).reshape(DEPTH, TP, 4, 2, 64)
        kv_s[4 * c:4 * c + 4] = np.asarray(r["kvS"]).reshape(DEPTH, 4, 4, 4, 2, 64).transpose(1, 0, 2, 3, 4, 5)
        win_p[c] = np.asarray(r["winP"]).reshape(DEPTH, 512, 2, 2, 64)
        win_s[4 * c:4 * c + 4] = np.asarray(r["winS"]).reshape(4, DEPTH, 512, 2, 2, 64)
        hlP = np.asarray(r["hlP"])
        hlS = np.asarray(r["hlS"])
        for q in range(4):
            for pb in range(4):
                for gl in range(2):
                    g = 8 * q + 2 * pb + gl
                    ssm_p[c, :, :, g, :] = hlP[:, gl * 64:gl * 64 + 64, :, q * 4 + pb].transpose(0, 2, 1)
                    ssm_s[4 * c:4 * c + 4, :, :, g, :] = hlS[:, gl * 64:gl * 64 + 64, :, q * 4 + pb, :].transpose(3, 0, 2, 1)
    return (y_p, y_s, kv_p, kv_s, win_p, win_s, ssm_p, ssm_s)


def kernel(**inputs):
    inp = {k: np.asarray(v) for k, v in inputs.items()}
    ncores = inp["x_prompt"].shape[0]
    npool = int(inp["cache_kv"].shape[0])
    key = (npool,)
    if key not in _NC_CACHE:
        _NC_CACHE[key] = build({"npool": npool})
    nc = _NC_CACHE[key]
    shared = _shared_inputs(inp)
    in_maps = [_core_inputs(inp, c, shared) for c in range(ncores)]
    res = run_bass_kernel_spmd(nc, in_maps, core_ids=list(range(ncores)))
    return _assemble(res.results, ncores)
```
